# Optimizing a Trainium2 kernel written in Bass

```python
import math
import jax, jax.numpy as jnp
from jax import lax
import numpy as np

D_MODEL = 1024
BATCH = 8
SEQ = 4096
DEPTH = 4

N_MIXERS = 4
RMS_EPS = 1e-6
LN_EPS = 1e-5
NEG_INF = -1e30
BIG = 1e9
D_FF = -(-8 * D_MODEL // (3 * 256)) * 256

CONV_WIDTH = 31

NSA_HEAD_DIM = 64
NSA_HEADS = D_MODEL // NSA_HEAD_DIM
NSA_KV_GROUPS = 4
NSA_CMP_BLOCK = 32
NSA_CMP_STRIDE = 16
NSA_SLC_BLOCK = 64
NSA_TOP_N = 16
NSA_LOCAL_BLOCKS = 2
NSA_WINDOW = 512
NSA_Q_CHUNK = 32
NSA_PROJ = NSA_HEADS * NSA_HEAD_DIM + 6 * NSA_KV_GROUPS * NSA_HEAD_DIM + 3 * NSA_HEADS
ROPE_THETA = 500000.0
ROPE_DIMS = NSA_HEAD_DIM // 4

S5_GROUP = 16
S5_STATE = 64
S5_N_GROUPS = D_MODEL // S5_GROUP

POOL_WINDOWS = (2, 4, 8, 16)
POOL_GROUP = D_MODEL // len(POOL_WINDOWS)

kernel_name = 'hybrid_conv_nsa_s5_pool_trunk'


def _n_layers_of(m):
    return (DEPTH - m + N_MIXERS - 1) // N_MIXERS


def rmsnorm(x, g):
    xf = x.astype(jnp.float32)
    y = xf * lax.rsqrt(jnp.mean(xf * xf, axis=-1, keepdims=True) + RMS_EPS)
    return (y * g.astype(jnp.float32)).astype(x.dtype)


def partial_rope(x, positions):
    half = ROPE_DIMS // 2
    inv_freq = ROPE_THETA ** (-jnp.arange(half, dtype=jnp.float32) / half)
    ang = positions.astype(jnp.float32)[:, None] * inv_freq[None, :]
    cos, sin = jnp.cos(ang), jnp.sin(ang)
    xr = x[..., :ROPE_DIMS].astype(jnp.float32)
    x1, x2 = xr[..., :half], xr[..., half:]
    rot = jnp.concatenate([x1 * cos - x2 * sin, x2 * cos + x1 * sin], axis=-1)
    return jnp.concatenate([rot.astype(x.dtype), x[..., ROPE_DIMS:]], axis=-1)


def masked_softmax(s, mask):
    s = jnp.where(mask, s.astype(jnp.float32), NEG_INF)
    return jax.nn.softmax(s, axis=-1) * mask.astype(jnp.float32)


def swiglu_ffn(h, w_in, w_out):
    gate, up = jnp.split(h @ w_in, 2, axis=-1)
    return (jax.nn.silu(gate) * up) @ w_out


def conformer_conv_module(h, w_in, b_in, w_dw, b_dw, ln_g, ln_b, w_out):
    a, g = jnp.split(h @ w_in + b_in, 2, axis=-1)
    u = a * jax.nn.sigmoid(g)
    u = lax.conv_general_dilated(
        u, w_dw[:, None, :].astype(u.dtype), window_strides=(1,),
        padding=[(CONV_WIDTH - 1, 0)], dimension_numbers=('NWC', 'WIO', 'NWC'),
        feature_group_count=D_MODEL) + b_dw
    uf = u.astype(jnp.float32)
    mu = jnp.mean(uf, axis=-1, keepdims=True)
    var = jnp.mean(jnp.square(uf - mu), axis=-1, keepdims=True)
    uf = (uf - mu) * lax.rsqrt(var + LN_EPS) * ln_g.astype(jnp.float32) + ln_b.astype(jnp.float32)
    u = jax.nn.silu(uf).astype(h.dtype)
    return u @ w_out


def nsa_attention(h, positions, w_in, q_gain, k_gain, cmp_pos, cmp_w1, cmp_b1, cmp_w2, w_out):
    bsz, seq, _ = h.shape
    H, G, dh = NSA_HEADS, NSA_KV_GROUPS, NSA_HEAD_DIM
    R = H // G
    qd, kvd = H * dh, 6 * G * dh
    proj = h @ w_in
    q = proj[..., :qd].reshape(bsz, seq, H, dh).transpose(0, 2, 1, 3)
    kv = proj[..., qd:qd + kvd].reshape(bsz, seq, 3, 2, G, dh).transpose(2, 3, 0, 4, 1, 5)
    gates = jax.nn.sigmoid(proj[..., qd + kvd:].astype(jnp.float32)).reshape(bsz, seq, 3, H)
    q = partial_rope(rmsnorm(q, q_gain), positions)
    k = partial_rope(rmsnorm(kv[:, 0], k_gain[:, None, None, None, :]), positions)
    v = kv[:, 1]

    n_cmp = (seq - NSA_CMP_BLOCK) // NSA_CMP_STRIDE + 1
    blk_idx = np.arange(n_cmp)[:, None] * NSA_CMP_STRIDE + np.arange(NSA_CMP_BLOCK)[None, :]
    kv_c = jnp.stack([k[0], v[0]])
    blocks = kv_c[:, :, :, blk_idx, :] + cmp_pos[:, None, None, None]
    flat = blocks.reshape(2, bsz, G, n_cmp, NSA_CMP_BLOCK * dh)
    hid = jax.nn.gelu(jnp.einsum('cbgnf,cfe->cbgne', flat, cmp_w1) + cmp_b1[:, None, None, None, :])
    comp = jnp.einsum('cbgne,cef->cbgnf', hid, cmp_w2)
    k_cmp, v_cmp = comp[0], comp[1]
    cmp_end = jnp.asarray(blk_idx[:, -1])

    n_sel = seq // NSA_SLC_BLOCK
    top_n = min(NSA_TOP_N, n_sel)
    c_start = np.arange(n_cmp) * NSA_CMP_STRIDE
    c_end = c_start + NSA_CMP_BLOCK - 1
    s_start = np.arange(n_sel) * NSA_SLC_BLOCK
    s_end = s_start + NSA_SLC_BLOCK - 1
    overlap = jnp.asarray(((c_start[:, None] <= s_end[None, :]) &
                           (c_end[:, None] >= s_start[None, :])).astype(np.float32))
    k_slc, v_slc = k[1], v[1]
    pad = ((0, 0), (0, 0), (NSA_WINDOW, 0), (0, 0))
    k_win, v_win = jnp.pad(k[2], pad), jnp.pad(v[2], pad)
    scale = dh ** -0.5
    sel_offsets = jnp.arange(NSA_SLC_BLOCK)
    blk_ids = jnp.arange(n_sel)
    gather = jax.vmap(jax.vmap(lambda arr, idx: arr[idx]))

    def chunk(c):
        q0 = c * NSA_Q_CHUNK
        t = q0 + jnp.arange(NSA_Q_CHUNK)
        qc = lax.dynamic_slice_in_dim(q, q0, NSA_Q_CHUNK, axis=2).reshape(bsz, G, R, NSA_Q_CHUNK, dh)
        s_c = jnp.einsum('bgrqd,bgnd->bgrqn', qc, k_cmp) * scale
        p_c = masked_softmax(s_c, cmp_end[None, :] <= t[:, None])
        o_c = jnp.einsum('bgrqn,bgnd->bgrqd', p_c, v_cmp)
        imp = jnp.einsum('bgrqn,ns->bgqs', p_c, overlap)
        cur = t // NSA_SLC_BLOCK
        valid = blk_ids[None, :] <= cur[:, None]
        forced = (blk_ids[None, :] == 0) | (blk_ids[None, :] >= cur[:, None] - (NSA_LOCAL_BLOCKS - 1))
        score = jnp.where(valid, jnp.where(forced, BIG, imp), -BIG)
        _, sel = lax.top_k(score, top_n)
        tok = (sel[..., None] * NSA_SLC_BLOCK + sel_offsets).reshape(bsz, G, -1)
        k_s = gather(k_slc, tok).reshape(bsz, G, NSA_Q_CHUNK, top_n * NSA_SLC_BLOCK, dh)
        v_s = gather(v_slc, tok).reshape(bsz, G, NSA_Q_CHUNK, top_n * NSA_SLC_BLOCK, dh)
        tok = tok.reshape(bsz, G, 1, NSA_Q_CHUNK, top_n * NSA_SLC_BLOCK)
        s_s = jnp.einsum('bgrqd,bgqkd->bgrqk', qc, k_s) * scale
        p_s = masked_softmax(s_s, tok <= t[:, None])
        o_s = jnp.einsum('bgrqk,bgqkd->bgrqd', p_s, v_s)
        kw = lax.dynamic_slice_in_dim(k_win, q0, NSA_Q_CHUNK + NSA_WINDOW, axis=2)
        vw = lax.dynamic_slice_in_dim(v_win, q0, NSA_Q_CHUNK + NSA_WINDOW, axis=2)
        kpos = q0 - NSA_WINDOW + jnp.arange(NSA_Q_CHUNK + NSA_WINDOW)
        rel = t[:, None] - kpos[None, :]
        mask_w = (rel >= 0) & (rel < NSA_WINDOW) & (kpos[None, :] >= 0)
        s_w = jnp.einsum('bgrqd,bgkd->bgrqk', qc, kw) * scale
        p_w = masked_softmax(s_w, mask_w)
        o_w = jnp.einsum('bgrqk,bgkd->bgrqd', p_w, vw)
        g = lax.dynamic_slice_in_dim(gates, q0, NSA_Q_CHUNK, axis=1)
        g = g.transpose(2, 0, 3, 1).reshape(3, bsz, G, R, NSA_Q_CHUNK, 1)
        return g[0] * o_c + g[1] * o_s + g[2] * o_w

    o = lax.map(chunk, jnp.arange(seq // NSA_Q_CHUNK))
    o = o.transpose(1, 0, 4, 2, 3, 5).reshape(bsz, seq, H * dh)
    return o.astype(h.dtype) @ w_out


def _linear_recurrence(e1, e2):
    a1, b1 = e1
    a2, b2 = e2
    return a1 * a2, a2 * b1 + b2


def s5_ssm(h, lam_re, lam_im, log_step, b_re, b_im, c_re, c_im, d_skip, w_glu, b_glu):
    bsz, seq, _ = h.shape
    f32 = jnp.float32
    lam = lax.complex(lam_re.astype(f32), lam_im.astype(f32))
    step = jnp.exp(log_step.astype(f32))[:, None]
    lam_bar = jnp.exp(lam * step)
    b_bar = ((lam_bar - 1.0) / lam)[:, :, None] * lax.complex(b_re.astype(f32), b_im.astype(f32))
    c_mat = lax.complex(c_re.astype(f32), c_im.astype(f32))

    def run_sequence(u):
        ug = u.astype(f32).reshape(seq, S5_N_GROUPS, S5_GROUP)
        bu = jnp.einsum('gpc,sgc->sgp', b_bar, ug)
        a = jnp.broadcast_to(lam_bar, bu.shape)
        _, states = lax.associative_scan(_linear_recurrence, (a, bu), axis=0)
        return jnp.einsum('gcp,sgp->sgc', c_mat, states).real.reshape(seq, D_MODEL)

    y = lax.map(run_sequence, h) + d_skip.astype(f32) * h.astype(f32)
    y = jax.nn.gelu(y).astype(h.dtype)
    a, g = jnp.split(y @ w_glu + b_glu, 2, axis=-1)
    return a * jax.nn.sigmoid(g)


def multiscale_pool(h, w_grp, scale):
    bsz, seq, _ = h.shape
    f32 = jnp.float32
    hf = h.astype(f32)
    csum = jnp.concatenate([jnp.zeros((bsz, 1, D_MODEL), f32), lax.cumsum(hf, axis=1)], axis=1)
    t = jnp.arange(seq)
    outs = []
    for gi, win in enumerate(POOL_WINDOWS):
        lo, hi = gi * POOL_GROUP, (gi + 1) * POOL_GROUP
        start = jnp.maximum(t + 1 - win, 0)
        cnt = (t + 1 - start).astype(f32)[None, :, None]
        pooled = (csum[:, 1:, lo:hi] - csum[:, start, lo:hi]) / cnt - hf[:, :, lo:hi]
        outs.append(jnp.einsum('bsc,cd->bsd', pooled, w_grp[gi].astype(f32)))
    return (jnp.concatenate(outs, axis=-1) * scale.astype(f32)).astype(h.dtype)


def setup_inputs(seed: int = 0) -> dict:
    key = jax.random.key(seed)
    keys = iter(jax.random.split(key, 64))
    f32 = jnp.float32

    def nrm(shape, s):
        return jax.random.normal(next(keys), shape, f32) * s

    nA, nB, nC, nD = [_n_layers_of(m) for m in range(N_MIXERS)]
    D, F = D_MODEL, D_FF
    H, G, dh, L = NSA_HEADS, NSA_KV_GROUPS, NSA_HEAD_DIM, NSA_CMP_BLOCK
    Gs, P, C = S5_N_GROUPS, S5_STATE, S5_GROUP
    inv2 = math.sqrt(0.5)
    return {
        'x': nrm((BATCH, SEQ, D), 1.0),
        'positions': jnp.arange(SEQ, dtype=jnp.int32),
        'norm_mix': 1.0 + nrm((DEPTH, D), 0.02),
        'norm_ffn': 1.0 + nrm((DEPTH, D), 0.02),
        'ffn_w_in': nrm((DEPTH, D, 2 * F), D ** -0.5),
        'ffn_w_out': nrm((DEPTH, F, D), F ** -0.5),
        'conv_w_in': nrm((nA, D, 2 * D), D ** -0.5),
        'conv_b_in': nrm((nA, 2 * D), 0.02),
        'conv_w_dw': nrm((nA, CONV_WIDTH, D), CONV_WIDTH ** -0.5),
        'conv_b_dw': nrm((nA, D), 0.02),
        'conv_ln_g': 1.0 + nrm((nA, D), 0.02),
        'conv_ln_b': nrm((nA, D), 0.02),
        'conv_w_out': nrm((nA, D, D), D ** -0.5),
        'nsa_w_in': nrm((nB, D, NSA_PROJ), D ** -0.5),
        'nsa_q_gain': 1.0 + nrm((nB, dh), 0.02),
        'nsa_k_gain': 1.0 + nrm((nB, 3, dh), 0.02),
        'nsa_cmp_pos': nrm((nB, 2, L, dh), 0.02),
        'nsa_cmp_w1': nrm((nB, 2, L * dh, dh), (L * dh) ** -0.5),
        'nsa_cmp_b1': nrm((nB, 2, dh), 0.02),
        'nsa_cmp_w2': nrm((nB, 2, dh, dh), dh ** -0.5),
        'nsa_w_out': nrm((nB, H * dh, D), (H * dh) ** -0.5),
        's5_lam_re': jnp.full((nC, Gs, P), -0.5, f32),
        's5_lam_im': jnp.broadcast_to(math.pi * jnp.arange(P, dtype=f32), (nC, Gs, P)),
        's5_log_step': jax.random.uniform(next(keys), (nC, Gs), f32, math.log(1e-3), math.log(1e-1)),
        's5_b_re': nrm((nC, Gs, P, C), inv2 * C ** -0.5),
        's5_b_im': nrm((nC, Gs, P, C), inv2 * C ** -0.5),
        's5_c_re': nrm((nC, Gs, C, P), inv2 * P ** -0.5),
        's5_c_im': nrm((nC, Gs, C, P), inv2 * P ** -0.5),
        's5_d': nrm((nC, D), 0.5),
        's5_w_glu': nrm((nC, D, 2 * D), D ** -0.5),
        's5_b_glu': nrm((nC, 2 * D), 0.02),
        'pool_w': nrm((nD, len(POOL_WINDOWS), POOL_GROUP, POOL_GROUP), POOL_GROUP ** -0.5),
        'pool_scale': 1.0 + nrm((nD, D), 0.05),
    }


def reference(x, positions, norm_mix, norm_ffn, ffn_w_in, ffn_w_out,
              conv_w_in, conv_b_in, conv_w_dw, conv_b_dw, conv_ln_g, conv_ln_b, conv_w_out,
              nsa_w_in, nsa_q_gain, nsa_k_gain, nsa_cmp_pos, nsa_cmp_w1, nsa_cmp_b1, nsa_cmp_w2, nsa_w_out,
              s5_lam_re, s5_lam_im, s5_log_step, s5_b_re, s5_b_im, s5_c_re, s5_c_im, s5_d, s5_w_glu, s5_b_glu,
              pool_w, pool_scale):
    for i in range(DEPTH):
        m, j = i % N_MIXERS, i // N_MIXERS
        h = rmsnorm(x, norm_mix[i])
        if m == 0:
            y = conformer_conv_module(h, conv_w_in[j], conv_b_in[j], conv_w_dw[j], conv_b_dw[j],
                                      conv_ln_g[j], conv_ln_b[j], conv_w_out[j])
        elif m == 1:
            y = nsa_attention(h, positions, nsa_w_in[j], nsa_q_gain[j], nsa_k_gain[j], nsa_cmp_pos[j],
                              nsa_cmp_w1[j], nsa_cmp_b1[j], nsa_cmp_w2[j], nsa_w_out[j])
        elif m == 2:
            y = s5_ssm(h, s5_lam_re[j], s5_lam_im[j], s5_log_step[j], s5_b_re[j], s5_b_im[j],
                       s5_c_re[j], s5_c_im[j], s5_d[j], s5_w_glu[j], s5_b_glu[j])
        else:
            y = multiscale_pool(h, pool_w[j], pool_scale[j])
        x = x + y.astype(x.dtype)
        h = rmsnorm(x, norm_ffn[i])
        x = x + swiglu_ffn(h, ffn_w_in[i], ffn_w_out[i]).astype(x.dtype)
    return x
```

```python
import numpy as np
from contextlib import ExitStack
import concourse.bass as bass
import concourse.mybir as mybir
from concourse.bass_utils import run_bass_kernel_spmd

F32 = mybir.dt.float32
BF16 = mybir.dt.bfloat16
I32 = mybir.dt.int32
ALU = mybir.AluOpType
AF = mybir.ActivationFunctionType
AX = mybir.AxisListType

SEM_ROLL = 30000
SAME_ENGINE_SYNC = True


class Buf:
    __slots__ = ("name", "w", "r", "dsem", "dcnt")

    def __init__(self, name):
        self.name = name
        self.w = None
        self.r = {}
        self.dsem = None
        self.dcnt = 0


class Ctx:
    def __init__(self, nc):
        self.nc = nc
        self.engs = {"pe": nc.tensor, "dve": nc.vector, "act": nc.scalar,
                     "pool": nc.gpsimd, "sp": nc.sync}
        self.stack = ExitStack()
        self.sem = {}
        self.cnt = {}
        self.seen = {e: {} for e in self.engs}
        self.nsem = 0
        self.allsems = {}
        self.sem_pool = []
        self.live_owners = []
        for e in self.engs:
            self._newsem(e)

    def _alloc_sem(self, name):
        s = self.stack.enter_context(self.nc.semaphore(name))
        self.nsem += 1
        return s

    def _newsem(self, e):
        self.sem[e] = self._alloc_sem(f"s_{e}_{self.nsem}")
        self.cnt[e] = 0

    def buf(self, name):
        return Buf(name)

    def bufs(self, name, n):
        return [Buf(f"{name}{i}") for i in range(n)]

    def _collect(self, eng, reads, writes, skip_sem=None):
        need = {}

        def req(st):
            if st is None:
                return
            s, v, e = st
            if e == eng and (eng == "pe" or not SAME_ENGINE_SYNC) and e in ("pe", "dve", "act", "pool"):
                return
            if skip_sem is not None and s.name == skip_sem.name:
                return
            k = s.name
            if k not in need or need[k][1] < v:
                need[k] = (s, v)

        for b in reads:
            req(b.w)
        for b in writes:
            req(b.w)
            for st in b.r.values():
                req(st)
        E = self.engs[eng]
        for k, (s, v) in need.items():
            if self.seen[eng].get(k, 0) < v:
                E.wait_ge(s, v)
                self.seen[eng][k] = v

    def op(self, eng, fn, reads=(), writes=()):
        self._collect(eng, reads, writes)
        ins = fn(self.engs[eng])
        if self.cnt[eng] >= SEM_ROLL:
            self._newsem(eng)
        s = self.sem[eng]
        self.cnt[eng] += 1
        ins.then_inc(s, 1)
        st = (s, self.cnt[eng], eng)
        self.allsems[s.name] = (s, self.cnt[eng])
        for b in reads:
            b.r[eng] = st
        for b in writes:
            b.w = st
            b.r = {}
        return ins

    def dma(self, q, out, in_, owner, reads=(), writes=(), par=False, **kw):
        if owner.dsem is None:
            if self.sem_pool:
                owner.dsem, owner.dcnt = self.sem_pool.pop()
            else:
                owner.dsem = self._alloc_sem(f"d_{owner.name}_{self.nsem}")
                owner.dcnt = 0
            self.live_owners.append(owner)
        self._collect(q, reads, writes, skip_sem=owner.dsem if par else None)
        ins = self.engs[q].dma_start(out=out, in_=in_, **kw)
        owner.dcnt += 16
        ins.then_inc(owner.dsem, 16)
        st = (owner.dsem, owner.dcnt, "dma")
        self.allsems[owner.dsem.name] = (owner.dsem, owner.dcnt)
        for b in reads:
            b.r["dma_" + owner.dsem.name] = st
        for b in writes:
            b.w = st
            b.r = {}
        return ins

    def barrier(self, engines=None):
        for e in (engines or self.engs):
            E = self.engs[e]
            for k, (s, v) in self.allsems.items():
                if s.name == self.sem[e].name:
                    continue
                if self.seen[e].get(k, 0) < v:
                    E.wait_ge(s, v)
                    self.seen[e][k] = v
        if engines is None:
            for o in self.live_owners:
                self.sem_pool.append((o.dsem, o.dcnt))
                o.dsem = None
            self.live_owners = []

    def finish(self):
        self.barrier()
        self.stack.close()

_TN = [0]
def SB(nc, name, shape, dt):
    _TN[0] += 1
    return nc.sbuf_tensor(f"{name}_{_TN[0]}", shape, dt)
def PSB(nc, name, shape, dt):
    _TN[0] += 1
    return nc.psum_tensor(f"{name}_{_TN[0]}", shape, dt)
D = 1024; S = 4096; FF = 2816; NT = S // 128
RMS_EPS = 1e-6

class PS:
    def __init__(self, C, nc, es):
        self.t = []; self.b = []
        for i in range(8):
            t = es.enter_context(PSB(nc, f"psb{i}", [128, 512], F32))
            self.t.append(t); self.b.append(C.buf(f"psb{i}"))
        self.i = 0; self.lo = 0
    def reserve(self, n):
        self.lo = n; self.i = n
    def next(self):
        i = self.i; self.i = i + 1 if i + 1 < 8 else self.lo
        return self.t[i], self.b[i]

def make_ident(C, nc, es):
    idf = es.enter_context(SB(nc, "identf", [128, 128], F32))
    idb = es.enter_context(SB(nc, "identb", [128, 128], BF16))
    tmp = es.enter_context(SB(nc, "identi", [128, 128], I32))
    B = C.buf("ident")
    C.op("pool", lambda E: E.iota(tmp[:], pattern=[[1, 128]], base=0, channel_multiplier=-1), writes=[B])
    C.op("dve", lambda E: E.tensor_scalar(idf[:], tmp[:], 0.0, None, op0=ALU.is_equal), reads=[B], writes=[B])
    C.op("dve", lambda E: E.tensor_copy(idb[:], idf[:]), reads=[B], writes=[B])
    return idf, idb, B

def load_bcast_row(C, nc, es, name, row_ap, n):
    t = es.enter_context(SB(nc, name, [128, n], F32))
    B = C.buf(name)
    C.dma("sp", t[:], row_ap.partition_broadcast(128), B, writes=[B])
    return t, B

def norm_tile(C, nc, xs_ap, XB, gbc, GB, h, HB, junk, JB, ss, SSB):
    C.op("dve", lambda E: E.tensor_tensor(junk[:], xs_ap, xs_ap, op=ALU.mult), reads=[XB], writes=[JB])
    C.op("dve", lambda E: E.reduce_sum(ss[:, 0:1], junk[:], axis=AX.X), reads=[JB], writes=[SSB])
    C.op("dve", lambda E: E.tensor_scalar(ss[:, 1:2], ss[:, 0:1], 1.0 / D, RMS_EPS, op0=ALU.mult, op1=ALU.add),
         reads=[SSB], writes=[SSB])
    C.op("act", lambda E: E.activation(out=ss[:, 2:3], in_=ss[:, 1:2], func=AF.Sqrt), reads=[SSB], writes=[SSB])
    C.op("dve", lambda E: E.reciprocal(ss[:, 3:4], ss[:, 2:3]), reads=[SSB], writes=[SSB])
    C.op("dve", lambda E: E.scalar_tensor_tensor(out=h[:], in0=xs_ap, scalar=ss[:, 3:4], in1=gbc[:],
                                                 op0=ALU.mult, op1=ALU.mult),
         reads=[XB, SSB, GB], writes=[HB])

def transpose_tile(C, nc, ps, h, HB, idb, IB, dst_ap, DB, eng="act"):
    pt, PB = ps.next()
    ptb = pt[:].bitcast(BF16).rearrange("p (a b) -> p a b", a=8)
    for dc in range(8):
        C.op("pe", lambda E: E.transpose(ptb[:, dc, :], h[:, dc * 128:(dc + 1) * 128], idb[:]),
             reads=[HB, IB], writes=[PB])
    if eng == "act":
        C.op("act", lambda E: E.copy(dst_ap, ptb), reads=[PB], writes=[DB])
    else:
        C.op("dve", lambda E: E.tensor_copy(dst_ap, ptb), reads=[PB], writes=[DB])
def ffn_phase(C, nc, xin, xin_bufs, xout, xout_bufs, gain_ap, win_ap, wout_ap):
    TC = 256; NCH = S // TC; KT = TC // 128; NF = FF // 128
    with ExitStack() as es:
        ps = PS(C, nc, es)
        idf, idb, IB = make_ident(C, nc, es)
        gbc, GB = load_bcast_row(C, nc, es, "gbc", gain_ap, D)
        win = es.enter_context(SB(nc, "win", [128, 8, 2 * FF], BF16))
        wout = es.enter_context(SB(nc, "wout", [128, NF, D], BF16))
        WIB = C.bufs("winb", 8); WOB = C.bufs("woutb", 2)
        wsrc = win_ap.rearrange("(kc p) f -> p kc f", p=128)
        for kc in range(8):
            C.dma("pool", win[:, kc, :], wsrc[:, kc, :], WIB[kc], writes=[WIB[kc]])
        wosrc = wout_ap.rearrange("(fc p) d -> p fc d", p=128)
        for hh in range(2):
            C.dma("pool", wout[:, hh * 11:(hh + 1) * 11, :], wosrc[:, hh * 11:(hh + 1) * 11, :], WOB[hh], writes=[WOB[hh]])
        xs = [es.enter_context(SB(nc, f"xs{i}", [128, KT, D], F32)) for i in range(2)]
        XS = C.bufs("xs", 2)
        hT = [es.enter_context(SB(nc, f"hT{i}", [128, 8, TC], BF16)) for i in range(2)]
        HT = C.bufs("hT", 2)
        hb = [es.enter_context(SB(nc, f"hb{i}", [128, D], BF16)) for i in range(2)]
        HB = C.bufs("hb", 2)
        junk = es.enter_context(SB(nc, "junk", [128, D], F32)); JB = C.buf("junk")
        ss = [es.enter_context(SB(nc, f"ss{i}", [128, 4], F32)) for i in range(2)]
        SSB = C.bufs("ss", 2)
        actT = es.enter_context(SB(nc, "actT", [128, NF, TC], BF16)); AT = C.bufs("actT", NF)
        sg = [es.enter_context(SB(nc, f"sg{i}", [128, TC], F32)) for i in range(2)]
        SG = C.bufs("sg", 2)
        xin_t = xin.rearrange("(c k p) d -> c p k d", p=128, k=KT)
        xout_t = xout.rearrange("(c k p) d -> c p k d", p=128, k=KT)

        def load(c):
            sl = c % 2
            C.dma("sp", xs[sl][:], xin_t[c], XS[sl], reads=[xin_bufs[c * KT + k] for k in range(KT)], writes=[XS[sl]])

        def prep(c):
            sl = c % 2
            for k in range(KT):
                i = (c * KT + k) % 2
                norm_tile(C, nc, xs[sl][:, k, :], XS[sl], gbc, GB, hb[i], HB[i], junk, JB, ss[i], SSB[i])
                transpose_tile(C, nc, ps, hb[i], HB[i], idb, IB, hT[sl][:, :, k * 128:(k + 1) * 128], HT[sl])

        load(0)
        for c in range(NCH):
            sl = c % 2
            if c + 1 < NCH:
                load(c + 1)
            prep(c)
            for fc in range(NF):
                pt, PB = ps.next()
                for half in range(2):
                    col = half * FF + fc * 128
                    for kc in range(8):
                        C.op("pe", lambda E: E.matmul(pt[:, half * TC:(half + 1) * TC], win[:, kc, col:col + 128], hT[sl][:, kc, :],
                                                      start=(kc == 0), stop=(kc == 7)),
                             reads=[WIB[kc], HT[sl]], writes=[PB])
                j = fc % 2
                C.op("act", lambda E: E.activation(out=sg[j][:], in_=pt[:, 0:TC], func=AF.Silu), reads=[PB], writes=[SG[j]])
                C.op("dve", lambda E: E.tensor_tensor(actT[:, fc, :], sg[j][:], pt[:, TC:2 * TC], op=ALU.mult),
                     reads=[SG[j], PB], writes=[AT[fc]])
            for k in range(KT):
                for dh in range(2):
                    pt, PB = ps.next()
                    for fc in range(NF):
                        C.op("pe", lambda E: E.matmul(pt[:, :], actT[:, fc, k * 128:(k + 1) * 128], wout[:, fc, dh * 512:(dh + 1) * 512],
                                                      start=(fc == 0), stop=(fc == NF - 1)),
                             reads=[AT[fc], WOB[fc // 11]], writes=[PB])
                    C.op("dve", lambda E: E.tensor_tensor(xs[sl][:, k, dh * 512:(dh + 1) * 512], xs[sl][:, k, dh * 512:(dh + 1) * 512], pt[:, :], op=ALU.add),
                         reads=[PB, XS[sl]], writes=[XS[sl]])
            C.dma("sp", xout_t[c], xs[sl][:], XS[sl], reads=[XS[sl]], writes=[xout_bufs[c * KT + k] for k in range(KT)])
        C.barrier()
def load_cols(C, nc, es, ps, idf, IB, rows, name):
    R = sum(n for _, n in rows)
    rt = es.enter_context(SB(nc, name + "_rows", [R, D], F32)); RB = C.buf(name + "_rows")
    r0 = 0
    for ap, n in rows:
        C.dma("sp", rt[r0:r0 + n, :], ap, RB, writes=[RB], par=True)
        r0 += n
    cols = es.enter_context(SB(nc, name + "_cols", [128, 8, R], F32)); CB = C.buf(name + "_cols")
    for dc in range(8):
        pt, PB = ps.next()
        C.op("pe", lambda E: E.transpose(pt[:, 0:R], rt[:, dc * 128:(dc + 1) * 128], idf[0:R, 0:R]), reads=[RB, IB], writes=[PB])
        C.op("dve", lambda E: E.tensor_copy(cols[:, dc, :], pt[:, 0:R]), reads=[PB], writes=[CB])
    return cols, CB

def conv_phase(C, nc, T, li, xin, xin_bufs, xout, xout_bufs):
    j = li // 4
    TC = 256; NCH = S // TC; KT = TC // 128; KW = 31; HALO = KW - 1
    with ExitStack() as es:
        ps = PS(C, nc, es)
        idf, idb, IB = make_ident(C, nc, es)
        gbc, GB = load_bcast_row(C, nc, es, "gbc", T["norm_mix"][li], D)
        win = es.enter_context(SB(nc, "cwin", [128, 8, 2 * D], BF16)); WIB = C.bufs("cwinb", 8)
        wsrc = T["conv_w_in"][j].rearrange("(kc p) f -> p kc f", p=128)
        for kc in range(8):
            C.dma("pool", win[:, kc, :], wsrc[:, kc, :], WIB[kc], writes=[WIB[kc]])
        wout = es.enter_context(SB(nc, "cwout", [128, 8, D], BF16)); WOB = C.buf("cwoutb")
        C.dma("pool", wout[:], T["conv_w_out"][j].rearrange("(kc p) d -> p kc d", p=128), WOB, writes=[WOB])
        cols, CB = load_cols(C, nc, es, ps, idf, IB, [
            (T["conv_b_in"][j].rearrange("(two d) -> two d", two=2), 2),
            (T["conv_b_dw"][j:j + 1, :], 1), (T["conv_ln_g"][j:j + 1, :], 1), (T["conv_ln_b"][j:j + 1, :], 1),
            (T["conv_w_dw"][j], KW)], "cv")
        diag = es.enter_context(SB(nc, "diag", [128, KW, 8, 128], BF16)); DG = C.buf("diag")
        for k in range(KW):
            for dc in range(8):
                eng = "dve" if (k + dc) % 2 == 0 else "pool"
                C.op(eng, lambda E: E.tensor_scalar(diag[:, k, dc, :], idf[:], cols[:, dc, 5 + k:6 + k], None, op0=ALU.mult),
                     reads=[IB, CB], writes=[])
        C.op("dve", lambda E: E.tensor_copy(diag[:, 0, 0, 0:1], diag[:, 0, 0, 0:1]), reads=[], writes=[DG])
        DG2 = C.buf("diag2")
        C.op("pool", lambda E: E.tensor_copy(diag[:, 1, 0, 0:1], diag[:, 1, 0, 0:1]), reads=[], writes=[DG2])
        ones = es.enter_context(SB(nc, "ones", [128, 128], F32)); ONB = C.buf("ones")
        C.op("dve", lambda E: E.memset(ones[:], 1.0), writes=[ONB])
        xs = [es.enter_context(SB(nc, f"xs{i}", [128, KT, D], F32)) for i in range(2)]; XS = C.bufs("xs", 2)
        hT = [es.enter_context(SB(nc, f"hT{i}", [128, 8, TC], BF16)) for i in range(2)]; HT = C.bufs("hT", 2)
        hb = [es.enter_context(SB(nc, f"hb{i}", [128, D], BF16)) for i in range(2)]; HB = C.bufs("hb", 2)
        junk = es.enter_context(SB(nc, "junk", [128, D], F32)); JB = C.buf("junk")
        ss = [es.enter_context(SB(nc, f"ss{i}", [128, 4], F32)) for i in range(2)]; SSB = C.bufs("ss", 2)
        ub = es.enter_context(SB(nc, "ubuf", [128, 8, HALO + TC], BF16)); UB = C.bufs("ub", 8)
        C.op("dve", lambda E: E.memset(ub[:], 0.0), writes=UB)
        sg = [es.enter_context(SB(nc, f"sg{i}", [128, TC], F32)) for i in range(2)]; SG = C.bufs("sg", 2)
        v = es.enter_context(SB(nc, "v", [128, 8, TC], F32)); VB = C.bufs("v", 8)
        vsq = es.enter_context(SB(nc, "vsq", [128, 8, TC], F32)); VQ = C.bufs("vsq", 8)
        st = es.enter_context(SB(nc, "st", [128, 4, TC], F32)); STB = C.buf("st")
        z = [es.enter_context(SB(nc, f"z{i}", [128, TC], F32)) for i in range(2)]; ZB = C.bufs("z", 2)
        yT = es.enter_context(SB(nc, "yT", [128, 8, TC], BF16)); YT = C.bufs("yT", 8)
        xin_t = xin.rearrange("(c k p) d -> c p k d", p=128, k=KT)
        xout_t = xout.rearrange("(c k p) d -> c p k d", p=128, k=KT)

        def load(c):
            sl = c % 2
            C.dma("sp", xs[sl][:], xin_t[c], XS[sl], reads=[xin_bufs[c * KT + k] for k in range(KT)], writes=[XS[sl]])

        load(0)
        for c in range(NCH):
            sl = c % 2
            if c + 1 < NCH:
                load(c + 1)
            for k in range(KT):
                i = (c * KT + k) % 2
                norm_tile(C, nc, xs[sl][:, k, :], XS[sl], gbc, GB, hb[i], HB[i], junk, JB, ss[i], SSB[i])
                transpose_tile(C, nc, ps, hb[i], HB[i], idb, IB, hT[sl][:, :, k * 128:(k + 1) * 128], HT[sl])
            for dc in range(8):
                pt, PB = ps.next()
                for half in range(2):
                    col = half * D + dc * 128
                    for kc in range(8):
                        C.op("pe", lambda E: E.matmul(pt[:, half * TC:(half + 1) * TC], win[:, kc, col:col + 128], hT[sl][:, kc, :],
                                                      start=(kc == 0), stop=(kc == 7)), reads=[WIB[kc], HT[sl]], writes=[PB])
                jj = dc % 2
                C.op("act", lambda E: E.activation(out=sg[jj][:], in_=pt[:, TC:2 * TC], func=AF.Sigmoid, bias=cols[:, dc, 1:2]),
                     reads=[PB, CB], writes=[SG[jj]])
                C.op("dve", lambda E: E.scalar_tensor_tensor(out=ub[:, dc, HALO:HALO + TC], in0=pt[:, 0:TC], scalar=cols[:, dc, 0:1],
                                                             in1=sg[jj][:], op0=ALU.add, op1=ALU.mult),
                     reads=[PB, CB, SG[jj]], writes=[UB[dc]])
            for dc in range(8):
                pt, PB = ps.next()
                for k in range(KW):
                    C.op("pe", lambda E: E.matmul(pt[:, 0:TC], diag[:, k, dc, :], ub[:, dc, k:k + TC], start=(k == 0), stop=(k == KW - 1)),
                         reads=[DG, DG2, UB[dc]], writes=[PB])
                C.op("act", lambda E: E.activation(out=v[:, dc, :], in_=pt[:, 0:TC], func=AF.Identity, bias=cols[:, dc, 2:3]),
                     reads=[PB, CB], writes=[VB[dc]])
                C.op("pool", lambda E: E.tensor_tensor(vsq[:, dc, :], v[:, dc, :], v[:, dc, :], op=ALU.mult), reads=[VB[dc]], writes=[VQ[dc]])
                C.op("pool", lambda E: E.tensor_copy(ub[:, dc, 0:HALO], ub[:, dc, TC:TC + HALO]), reads=[], writes=[UB[dc]])
            pt, PB = ps.next()
            for dc in range(8):
                C.op("pe", lambda E: E.matmul(pt[:, 0:TC], ones[:], v[:, dc, :], start=(dc == 0), stop=(dc == 7)), reads=[ONB, VB[dc]], writes=[PB])
            for dc in range(8):
                C.op("pe", lambda E: E.matmul(pt[:, TC:2 * TC], ones[:], vsq[:, dc, :], start=(dc == 0), stop=(dc == 7)), reads=[ONB, VQ[dc]], writes=[PB])
            C.op("dve", lambda E: E.tensor_scalar(st[:, 0, :], pt[:, 0:TC], 1.0 / D, None, op0=ALU.mult), reads=[PB], writes=[STB])
            C.op("dve", lambda E: E.tensor_tensor(st[:, 1, :], st[:, 0, :], st[:, 0, :], op=ALU.mult), reads=[STB], writes=[STB])
            C.op("dve", lambda E: E.scalar_tensor_tensor(out=st[:, 2, :], in0=pt[:, TC:2 * TC], scalar=1.0 / D, in1=st[:, 1, :], op0=ALU.mult, op1=ALU.subtract),
                 reads=[PB, STB], writes=[STB])
            C.op("dve", lambda E: E.tensor_scalar(st[:, 2, :], st[:, 2, :], 1e-5, None, op0=ALU.add), reads=[STB], writes=[STB])
            C.op("act", lambda E: E.activation(out=st[:, 3, :], in_=st[:, 2, :], func=AF.Sqrt), reads=[STB], writes=[STB])
            C.op("dve", lambda E: E.reciprocal(st[:, 3, :], st[:, 3, :]), reads=[STB], writes=[STB])
            for dc in range(8):
                jj = dc % 2
                C.op("dve", lambda E: E.tensor_tensor(z[jj][:], v[:, dc, :], st[:, 0, :], op=ALU.subtract), reads=[VB[dc], STB], writes=[ZB[jj]])
                C.op("dve", lambda E: E.tensor_tensor(z[jj][:], z[jj][:], st[:, 3, :], op=ALU.mult), reads=[ZB[jj], STB], writes=[ZB[jj]])
                C.op("act", lambda E: E.activation(out=yT[:, dc, :], in_=z[jj][:], func=AF.Silu, scale=cols[:, dc, 3:4], bias=cols[:, dc, 4:5]),
                     reads=[ZB[jj], CB], writes=[YT[dc]])
            for k in range(KT):
                for dh in range(2):
                    pt, PB = ps.next()
                    for dc in range(8):
                        C.op("pe", lambda E: E.matmul(pt[:, :], yT[:, dc, k * 128:(k + 1) * 128], wout[:, dc, dh * 512:(dh + 1) * 512],
                                                      start=(dc == 0), stop=(dc == 7)), reads=[YT[dc], WOB], writes=[PB])
                    C.op("dve", lambda E: E.tensor_tensor(xs[sl][:, k, dh * 512:(dh + 1) * 512], xs[sl][:, k, dh * 512:(dh + 1) * 512], pt[:, :], op=ALU.add),
                         reads=[PB, XS[sl]], writes=[XS[sl]])
            C.dma("sp", xout_t[c], xs[sl][:], XS[sl], reads=[XS[sl]], writes=[xout_bufs[c * KT + k] for k in range(KT)])
        C.barrier()
def pool_phase(C, nc, T, li, xin, xin_bufs, xout, xout_bufs):
    j = li // 4
    TC = 256; NCH = S // TC; KT = TC // 128; HALO = 16
    WINS = (2, 4, 8, 16)
    with ExitStack() as es:
        ps = PS(C, nc, es)
        idf, idb, IB = make_ident(C, nc, es)
        gbc, GB = load_bcast_row(C, nc, es, "gbc", T["norm_mix"][li], D)
        scb, SCB = load_bcast_row(C, nc, es, "pscale", T["pool_scale"][j], D)
        pw = es.enter_context(SB(nc, "pw", [128, 4, 2, 256], BF16)); PWB = C.buf("pw")
        C.dma("pool", pw[:], T["pool_w"][j].rearrange("g (k p) d -> p g k d", p=128), PWB, writes=[PWB])
        invc = es.enter_context(SB(nc, "invc", [128, 4, 2, TC], F32)); ICB = C.buf("invc")
        iot = es.enter_context(SB(nc, "iot", [128, TC], F32)); IOB = C.buf("iot")
        C.op("pool", lambda E: E.iota(iot[:], pattern=[[1, TC]], base=1, channel_multiplier=0, allow_small_or_imprecise_dtypes=True), writes=[IOB])
        for gi, w in enumerate(WINS):
            C.op("dve", lambda E: E.tensor_scalar(invc[:, gi, 0, :], iot[:], float(w), None, op0=ALU.min), reads=[IOB], writes=[ICB])
            C.op("dve", lambda E: E.reciprocal(invc[:, gi, 0, :], invc[:, gi, 0, :]), reads=[ICB], writes=[ICB])
            C.op("dve", lambda E: E.memset(invc[:, gi, 1, :], 1.0 / w), writes=[ICB])
        xs = [es.enter_context(SB(nc, f"xs{i}", [128, KT, D], F32)) for i in range(2)]; XS = C.bufs("xs", 2)
        hb = [es.enter_context(SB(nc, f"hb{i}", [128, D], BF16)) for i in range(2)]; HB = C.bufs("hb", 2)
        junk = es.enter_context(SB(nc, "junk", [128, D], F32)); JB = C.buf("junk")
        ss = [es.enter_context(SB(nc, f"ss{i}", [128, 4], F32)) for i in range(2)]; SSB = C.bufs("ss", 2)
        hbuf = es.enter_context(SB(nc, "hbuf", [128, 8, HALO + TC], BF16)); HBF = C.buf("hbuf")
        C.op("dve", lambda E: E.memset(hbuf[:], 0.0), writes=[HBF])
        ta = es.enter_context(SB(nc, "ta", [128, 2, HALO + TC], F32)); TA = C.buf("ta")
        tb = es.enter_context(SB(nc, "tb", [128, 2, HALO + TC], F32)); TB = C.buf("tb")
        pT = es.enter_context(SB(nc, "pT", [128, 8, TC], BF16)); PT = C.bufs("pT", 4)
        tmp = [es.enter_context(SB(nc, f"tmp{i}", [128, 512], F32)) for i in range(2)]; TM = C.bufs("tmp", 2)
        xin_t = xin.rearrange("(c k p) d -> c p k d", p=128, k=KT)
        xout_t = xout.rearrange("(c k p) d -> c p k d", p=128, k=KT)
        W = HALO + TC

        def load(c):
            sl = c % 2
            C.dma("sp", xs[sl][:], xin_t[c], XS[sl], reads=[xin_bufs[c * KT + k] for k in range(KT)], writes=[XS[sl]])

        load(0)
        for c in range(NCH):
            sl = c % 2
            if c + 1 < NCH:
                load(c + 1)
            if c > 0:
                C.op("dve", lambda E: E.tensor_copy(hbuf[:, :, 0:HALO], hbuf[:, :, TC:TC + HALO]), reads=[HBF], writes=[HBF])
            for k in range(KT):
                i = (c * KT + k) % 2
                norm_tile(C, nc, xs[sl][:, k, :], XS[sl], gbc, GB, hb[i], HB[i], junk, JB, ss[i], SSB[i])
                transpose_tile(C, nc, ps, hb[i], HB[i], idb, IB, hbuf[:, :, HALO + k * 128:HALO + (k + 1) * 128], HBF)
            for gi, w in enumerate(WINS):
                src = hbuf[:, 2 * gi:2 * gi + 2, :]
                cur, CURB = src, HBF
                sh = 1
                dst_list = [(ta, TA), (tb, TB)]
                di = 0
                while sh < w:
                    dst, DSTB = dst_list[di]; di ^= 1
                    C.op("dve", lambda E: E.tensor_tensor(dst[:, :, sh:W], cur[:, :, sh:W], cur[:, :, 0:W - sh], op=ALU.add),
                         reads=[CURB], writes=[DSTB])
                    cur, CURB = dst, DSTB
                    sh *= 2
                ic = invc[:, gi, 0 if c == 0 else 1, :]
                dst, DSTB = dst_list[di]
                for q in range(2):
                    C.op("dve", lambda E: E.tensor_tensor(dst[:, q, HALO:W], cur[:, q, HALO:W], ic, op=ALU.mult), reads=[CURB, ICB], writes=[DSTB])
                C.op("dve", lambda E: E.tensor_tensor(pT[:, 2 * gi:2 * gi + 2, :], dst[:, :, HALO:W], src[:, :, HALO:W], op=ALU.subtract),
                     reads=[DSTB, HBF], writes=[PT[gi]])
            for k in range(KT):
                for hh in range(2):
                    pt, PB = ps.next()
                    for g2 in range(2):
                        gi = hh * 2 + g2
                        for kc in range(2):
                            C.op("pe", lambda E: E.matmul(pt[:, g2 * 256:(g2 + 1) * 256], pT[:, 2 * gi + kc, k * 128:(k + 1) * 128], pw[:, gi, kc, :],
                                                          start=(kc == 0), stop=(kc == 1)), reads=[PT[gi], PWB], writes=[PB])
                    C.op("dve", lambda E: E.tensor_tensor(tmp[hh][:], pt[:, :], scb[:, hh * 512:(hh + 1) * 512], op=ALU.mult), reads=[PB, SCB], writes=[TM[hh]])
                    C.op("pool", lambda E: E.tensor_tensor(xs[sl][:, k, hh * 512:(hh + 1) * 512], xs[sl][:, k, hh * 512:(hh + 1) * 512], tmp[hh][:], op=ALU.add),
                         reads=[TM[hh], XS[sl]], writes=[XS[sl]])
            C.dma("sp", xout_t[c], xs[sl][:], XS[sl], reads=[XS[sl]], writes=[xout_bufs[c * KT + k] for k in range(KT)])
        C.barrier()
MAGIC = 12582912.0
TWO_PI_S = 6.283185

def _sincos(C, turns, TB, shape, wk, WK, out_sin, out_cos, OB, eng="dve"):
    n, f = wk
    C.op(eng, lambda E: E.tensor_scalar(n, turns, MAGIC, MAGIC, op0=ALU.add, op1=ALU.subtract), reads=[TB], writes=[WK])
    C.op(eng, lambda E: E.tensor_tensor(f, turns, n, op=ALU.subtract), reads=[TB, WK], writes=[WK])
    C.op("act", lambda E: E.activation(out=out_sin, in_=f, func=AF.Sin, scale=TWO_PI_S), reads=[WK], writes=[OB])
    C.op(eng, lambda E: E.tensor_scalar(n, f, 0.25, None, op0=ALU.is_gt), reads=[WK], writes=[WK])
    C.op(eng, lambda E: E.scalar_tensor_tensor(out=f, in0=f, scalar=0.25, in1=n, op0=ALU.add, op1=ALU.subtract), reads=[WK], writes=[WK])
    C.op("act", lambda E: E.activation(out=out_cos, in_=f, func=AF.Sin, scale=TWO_PI_S), reads=[WK], writes=[OB])

def s5_phase(C, nc, T, li, xin, xin_bufs, xout, xout_bufs):
    j = li // 4
    TC = 256; NCH = S // TC; KT = TC // 128; NJ = S // 8
    INV2PI = float(1.0 / (2 * np.pi))
    with ExitStack() as es0:
        ps = PS(C, nc, es0)
        idf, idb, IB = make_ident(C, nc, es0)
        hTf = es0.enter_context(SB(nc, "hTf", [128, 8, S], BF16)); HTA = C.buf("hTfA")
        dcol, DCB = load_cols(C, nc, es0, ps, idf, IB, [(T["s5_d"][j:j + 1, :], 1)], "s5d")
        xin_t = xin.rearrange("(c k p) d -> c p k d", p=128, k=KT)
        xout_t = xout.rearrange("(c k p) d -> c p k d", p=128, k=KT)
        with ExitStack() as es:
            gbc, GB = load_bcast_row(C, nc, es, "gbc", T["norm_mix"][li], D)
            xs = [es.enter_context(SB(nc, f"xs{i}", [128, KT, D], F32)) for i in range(2)]; XS = C.bufs("xs", 2)
            hb = [es.enter_context(SB(nc, f"hb{i}", [128, D], BF16)) for i in range(2)]; HB = C.bufs("hb", 2)
            junk = es.enter_context(SB(nc, "junk", [128, D], F32)); JB = C.buf("junk")
            ss = [es.enter_context(SB(nc, f"ss{i}", [128, 4], F32)) for i in range(2)]; SSB = C.bufs("ss", 2)
            for c in range(NCH):
                sl = c % 2
                C.dma("sp", xs[sl][:], xin_t[c], XS[sl], reads=[xin_bufs[c * KT + k] for k in range(KT)], writes=[XS[sl]])
                for k in range(KT):
                    i = (c * KT + k) % 2
                    norm_tile(C, nc, xs[sl][:, k, :], XS[sl], gbc, GB, hb[i], HB[i], junk, JB, ss[i], SSB[i])
                    t0 = c * TC + k * 128
                    transpose_tile(C, nc, ps, hb[i], HB[i], idb, IB, hTf[:, :, t0:t0 + 128], HTA)
            C.barrier()
        with ExitStack() as es:
            def sb(name, shape, dt=F32):
                return es.enter_context(SB(nc, name, shape, dt))
            es.enter_context(nc.allow_non_contiguous_dma(reason="small parameter loads"))
            mask = sb("mask", [128, 8, 16]); MKB = C.buf("mask")
            mi = sb("maski", [128, 8, 16], I32)
            C.op("pool", lambda E: E.iota(mi[:], pattern=[[-16, 8], [0, 16]], base=0, channel_multiplier=1), writes=[MKB])
            m2 = sb("mask2", [128, 8, 16])
            C.op("dve", lambda E: E.tensor_scalar(mask[:], mi[:], 0.0, None, op0=ALU.is_ge), reads=[MKB], writes=[MKB])
            C.op("dve", lambda E: E.tensor_scalar(m2[:], mi[:], 15.0, None, op0=ALU.is_le), reads=[MKB], writes=[MKB])
            C.op("dve", lambda E: E.tensor_tensor(mask[:], mask[:], m2[:], op=ALU.mult), reads=[MKB], writes=[MKB])
            iot = sb("iot1", [128, NJ]); IOB = C.buf("iot1")
            C.op("pool", lambda E: E.iota(iot[:], pattern=[[1, NJ]], base=1, channel_multiplier=0, allow_small_or_imprecise_dtypes=True), writes=[IOB])
            lst = sb("lst", [128, 8]); LSB = C.buf("lst")
            lrr = sb("lrr", [128, 2, 8, 64]); LRB = C.buf("lrr")
            braw = sb("braw", [64, 2, 64, 16]); BRB = C.buf("braw")
            craw = sb("craw", [128, 2, 8, 64]); CRB = C.buf("craw")
            ls_v = T["s5_log_step"][j].rearrange("(dc g) -> g dc", g=8)
            lam = [T["s5_lam_re"][j], T["s5_lam_im"][j]]
            bb = [T["s5_b_re"][j], T["s5_b_im"][j]]
            cm = [T["s5_c_re"][j], T["s5_c_im"][j]]
            for ri in range(2):
                C.dma("sp", braw[:, ri, :, :], bb[ri].rearrange("g p c -> p g c"), BRB, writes=[BRB], par=True)
                C.dma("sp", craw[:, ri, :, :], cm[ri].rearrange("(dc g) c p -> (g c) dc p", g=8), CRB, writes=[CRB], par=True)
            for g8 in range(8):
                pr = slice(g8 * 16, (g8 + 1) * 16)
                C.dma("sp", lst[pr, :], ls_v[g8].partition_broadcast(16), LSB, writes=[LSB], par=True)
                for ri in range(2):
                    lv = lam[ri].rearrange("(dc g) p -> g dc p", g=8)
                    C.dma("sp", lrr[pr, ri, :, :], lv[g8].partition_broadcast(16), LRB, writes=[LRB], par=True)
            step = sb("step", [128, 8]); STB = C.buf("step")
            C.op("act", lambda E: E.activation(out=step[:], in_=lst[:], func=AF.Exp), reads=[LSB], writes=[STB])
            kst = sb("kst", [128, 8, 9]); KSB = C.buf("kst")
            for k in range(9):
                C.op("dve", lambda E: E.tensor_scalar(kst[:, :, k], step[:], float(k), None, op0=ALU.mult), reads=[STB], writes=[KSB])
            c16 = sb("c16", [128, 1]); C16B = C.buf("c16")
            C.op("dve", lambda E: E.memset(c16[:], 1.0 / 16), writes=[C16B])
            PVm = sb("PVm", [128, 2, 8, 4, 128], BF16); PVB = C.buf("PVm")
            QFm = sb("QFm", [128, 2, 9, 4, 128], BF16); QFB = C.buf("QFm")
            PVT = sb("PVT", [128, 2, 4, 128], BF16); PTB = C.buf("PVT")
            BDm = sb("BDm", [128, 8, 128], BF16); BDB = C.buf("BDm")
            cosT = sb("cosT", [128, 4, NJ]); sinT = sb("sinT", [128, 4, NJ]); TBB = C.buf("tabs")
            Xb = sb("Xb", [128, 2, 4, NJ + 1], BF16); XBB = C.bufs("Xb", 4)
            C.op("dve", lambda E: E.memset(Xb[:], 0.0), writes=XBB)
            w8 = [sb(f"w8_{i}", [128, 9, 64]) for i in range(8)]; W8 = C.bufs("w8", 8)
            w1 = [sb(f"w1_{i}", [128, 64]) for i in range(8)]; W1 = C.bufs("w1", 8)
            bcd = sb("bcd", [128, 2, 64]); BCB = C.buf("bcd")
            yex = [sb(f"yex{i}", [128, 2, 4, 128], BF16) for i in range(2)]; YEX = C.bufs("yex", 2)
            yr = sb("yr", [128, 2, 4, 128]); YRB = C.buf("yr")
            wt = [sb(f"wt_{i}", [128, NJ]) for i in range(8)]; WT = C.bufs("wt", 8)
            r8 = sb("r8", [128, 2, 4]); R8B = C.buf("r8")

            def tt(eng, out, a, b, op, reads, writes):
                C.op(eng, lambda E: E.tensor_tensor(out, a, b, op=op), reads=reads, writes=writes)

            for dc in range(8):
                for ri in range(2):
                    pt, PB = ps.next()
                    C.op("pe", lambda E: E.transpose(pt[:, 0:64], braw[:, ri, dc * 8:(dc + 1) * 8, :].rearrange("p g c -> p (g c)"), idf[0:64, 0:64]),
                         reads=[BRB, IB], writes=[PB])
                    C.op("act", lambda E: E.copy(bcd[:, ri, :], pt[:, 0:64]), reads=[PB], writes=[BCB])
                kb = kst[:, dc, :].unsqueeze(2).to_broadcast([128, 9, 64])
                tt("dve", w8[0][:], kb, lrr[:, 0, dc, :].unsqueeze(1).to_broadcast([128, 9, 64]), ALU.mult, [KSB, LRB], [W8[0]])
                tt("dve", w8[1][:], kb, lrr[:, 1, dc, :].unsqueeze(1).to_broadcast([128, 9, 64]), ALU.mult, [KSB, LRB], [W8[1]])
                C.op("act", lambda E: E.activation(out=w8[0][:], in_=w8[0][:], func=AF.Exp), reads=[W8[0]], writes=[W8[0]])
                C.op("dve", lambda E: E.tensor_scalar(w8[1][:], w8[1][:], INV2PI, None, op0=ALU.mult), reads=[W8[1]], writes=[W8[1]])
                _sincos(C, w8[1][:], W8[1], None, (w8[2][:], w8[3][:]), W8[2], w8[4][:], w8[5][:], W8[4])
                C.op("dve", lambda E: E.tensor_copy(w1[6][:], w8[0][:, 8, :]), reads=[W8[0]], writes=[W1[6]])
                C.op("dve", lambda E: E.tensor_scalar(w1[7][:], w8[1][:, 8, :], MAGIC, MAGIC, op0=ALU.add, op1=ALU.subtract), reads=[W8[1]], writes=[W1[7]])
                tt("dve", w1[7][:], w8[1][:, 8, :], w1[7][:], ALU.subtract, [W8[1], W1[7]], [W1[7]])
                tt("dve", w8[5][:], w8[5][:], w8[0][:], ALU.mult, [W8[4], W8[0]], [W8[4]])
                tt("dve", w8[4][:], w8[4][:], w8[0][:], ALU.mult, [W8[4], W8[0]], [W8[4]])
                Ere, Eim = w8[5], w8[4]
                mk4 = mask[:].unsqueeze(1).to_broadcast([128, 4, 8, 16])
                for q in range(2):
                    src = w1[6 + q][:].rearrange("p (pb r) -> p pb r", pb=4).unsqueeze(2).to_broadcast([128, 4, 8, 16])
                    tt("pool", yr[:, q, :, :].rearrange("p pb (g r) -> p pb g r", g=8), src, mk4, ALU.mult, [W1[6 + q], MKB], [YRB])
                pt, PB = ps.next()
                for q in range(2):
                    for pb in range(4):
                        C.op("pe", lambda E: E.matmul(pt[:, q * 4 + pb:q * 4 + pb + 1], yr[:, q, pb, :], c16[:], start=True, stop=True), reads=[YRB, C16B], writes=[PB])
                C.op("act", lambda E: E.copy(r8[:].rearrange("p a b -> p (a b)"), pt[:, 0:8]), reads=[PB], writes=[R8B])
                lre, lim = lrr[:, 0, dc, :], lrr[:, 1, dc, :]
                C.op("dve", lambda E: E.tensor_scalar(w1[0][:], Ere[:, 1, :], -1.0, None, op0=ALU.add), reads=[W8[4]], writes=[W1[0]])
                ni = Eim[:, 1, :]
                tt("dve", w1[1][:], lre, lre, ALU.mult, [LRB], [W1[1]])
                tt("dve", w1[2][:], lim, lim, ALU.mult, [LRB], [W1[2]])
                tt("dve", w1[1][:], w1[1][:], w1[2][:], ALU.add, [W1[1], W1[2]], [W1[1]])
                C.op("dve", lambda E: E.reciprocal(w1[1][:], w1[1][:]), reads=[W1[1]], writes=[W1[1]])
                tt("dve", w1[2][:], w1[0][:], lre, ALU.mult, [W1[0], LRB], [W1[2]])
                tt("dve", w1[3][:], ni, lim, ALU.mult, [W8[4], LRB], [W1[3]])
                tt("dve", w1[2][:], w1[2][:], w1[3][:], ALU.add, [W1[2], W1[3]], [W1[2]])
                tt("dve", w1[2][:], w1[2][:], w1[1][:], ALU.mult, [W1[2], W1[1]], [W1[2]])
                tt("dve", w1[3][:], ni, lre, ALU.mult, [W8[4], LRB], [W1[3]])
                tt("dve", w1[4][:], w1[0][:], lim, ALU.mult, [W1[0], LRB], [W1[4]])
                tt("dve", w1[3][:], w1[3][:], w1[4][:], ALU.subtract, [W1[3], W1[4]], [W1[3]])
                tt("dve", w1[3][:], w1[3][:], w1[1][:], ALU.mult, [W1[3], W1[1]], [W1[3]])
                bre, bim = bcd[:, 0, :], bcd[:, 1, :]
                tt("dve", w1[4][:], w1[2][:], bre, ALU.mult, [W1[2], BCB], [W1[4]])
                tt("dve", w1[5][:], w1[3][:], bim, ALU.mult, [W1[3], BCB], [W1[5]])
                tt("dve", w1[4][:], w1[4][:], w1[5][:], ALU.subtract, [W1[4], W1[5]], [W1[4]])
                tt("dve", w1[5][:], w1[2][:], bim, ALU.mult, [W1[2], BCB], [W1[5]])
                tt("dve", w1[0][:], w1[3][:], bre, ALU.mult, [W1[3], BCB], [W1[0]])
                tt("dve", w1[5][:], w1[5][:], w1[0][:], ALU.add, [W1[5], W1[0]], [W1[5]])
                Bre_b = w1[4][:].unsqueeze(1).to_broadcast([128, 9, 64]); Bim_b = w1[5][:].unsqueeze(1).to_broadcast([128, 9, 64])
                tt("dve", w8[0][:], Ere[:], Bre_b, ALU.mult, [W8[4], W1[4]], [W8[0]])
                tt("dve", w8[1][:], Eim[:], Bim_b, ALU.mult, [W8[4], W1[5]], [W8[1]])
                tt("dve", w8[0][:], w8[0][:], w8[1][:], ALU.subtract, [W8[0], W8[1]], [W8[0]])
                tt("dve", w8[1][:], Ere[:], Bim_b, ALU.mult, [W8[4], W1[5]], [W8[1]])
                tt("dve", w8[2][:], Eim[:], Bre_b, ALU.mult, [W8[4], W1[4]], [W8[2]])
                tt("dve", w8[1][:], w8[1][:], w8[2][:], ALU.add, [W8[1], W8[2]], [W8[1]])
                for s in range(8):
                    k = 7 - s
                    for ri in range(2):
                        src = w8[ri][:, k, :].rearrange("p (pb r) -> p pb r", pb=4).unsqueeze(2).to_broadcast([128, 4, 8, 16])
                        dst = PVm[:, ri, s, :, :].rearrange("p pb (g r) -> p pb g r", g=8)
                        tt("pool", dst, src, mk4, ALU.mult, [W8[ri], MKB], [PVB])
                cre = craw[:, 0, dc, :].unsqueeze(1).to_broadcast([128, 9, 64]); cim = craw[:, 1, dc, :].unsqueeze(1).to_broadcast([128, 9, 64])
                tt("dve", w8[2][:], cre, Ere[:], ALU.mult, [CRB, W8[4]], [W8[2]])
                tt("dve", w8[3][:], cim, Eim[:], ALU.mult, [CRB, W8[4]], [W8[3]])
                tt("dve", w8[2][:], w8[2][:], w8[3][:], ALU.subtract, [W8[2], W8[3]], [W8[2]])
                tt("dve", w8[3][:], cre, Eim[:], ALU.mult, [CRB, W8[4]], [W8[3]])
                tt("dve", w8[6][:], cim, Ere[:], ALU.mult, [CRB, W8[4]], [W8[6]])
                C.op("dve", lambda E: E.scalar_tensor_tensor(out=w8[3][:], in0=w8[3][:], scalar=-1.0, in1=w8[6][:], op0=ALU.mult, op1=ALU.subtract),
                     reads=[W8[3], W8[6]], writes=[W8[3]])
                for b in range(9):
                    yi = b % 2
                    for ri in range(2):
                        src = w8[2 + ri][:, b, :].rearrange("p (pb r) -> p pb r", pb=4).unsqueeze(2).to_broadcast([128, 4, 8, 16])
                        dst = yex[yi][:, ri, :, :].rearrange("p pb (g r) -> p pb g r", g=8)
                        tt("pool", dst, src, mk4, ALU.mult, [W8[2 + ri], MKB], [YEX[yi]])
                    pt, PB = ps.next()
                    ptb = pt[:].bitcast(BF16).rearrange("p (a b c) -> p a b c", a=2, b=4)
                    for ri in range(2):
                        for pb in range(4):
                            C.op("pe", lambda E: E.transpose(ptb[:, ri, pb, :], yex[yi][:, ri, pb, :], idb[:]), reads=[YEX[yi], IB], writes=[PB])
                    C.op("act", lambda E: E.copy(QFm[:, :, b, :, :], ptb), reads=[PB], writes=[QFB])
                for ri in range(2):
                    pt, PB = ps.next()
                    ptb = pt[:].bitcast(BF16).rearrange("p (a b) -> p a b", a=8)
                    for pb in range(4):
                        C.op("pe", lambda E: E.transpose(ptb[:, pb, :], PVm[:, ri, 7, pb, :], idb[:]), reads=[PVB, IB], writes=[PB])
                    C.op("act", lambda E: E.copy(PVT[:, ri, :, :], ptb[:, 0:4, :]), reads=[PB], writes=[PTB])
                for th in range(2):
                    pt, PB = ps.next()
                    for t4 in range(4):
                        tau = th * 4 + t4
                        n = 0
                        for ri in range(2):
                            for pb in range(4):
                                C.op("pe", lambda E: E.matmul(pt[:, t4 * 128:(t4 + 1) * 128], PVT[:, ri, pb, :], QFm[:, ri, tau, pb, :],
                                                              start=(n == 0), stop=(n == 7)), reads=[PTB, QFB], writes=[PB])
                                n += 1
                    C.op("act", lambda E: E.copy(BDm[:, th * 4:(th + 1) * 4, :], pt[:].rearrange("p (a b) -> p a b", a=4)), reads=[PB], writes=[BDB])
                for pb in range(4):
                    C.op("dve", lambda E: E.tensor_scalar(wt[0][:], iot[:], r8[:, 1, pb:pb + 1], None, op0=ALU.mult), reads=[IOB, R8B], writes=[WT[0]])
                    _sincos(C, wt[0][:], WT[0], None, (wt[1][:], wt[2][:]), WT[1], sinT[:, pb, :], cosT[:, pb, :], TBB)
                for pb in range(4):
                    pr_, PR = ps.next(); pi_, PI = ps.next()
                    for ri, (pt, PB) in enumerate(((pr_, PR), (pi_, PI))):
                        for s in range(8):
                            C.op("pe", lambda E: E.matmul(pt[:, :], PVm[:, ri, s, pb, :], hTf[:, dc, s::8], start=(s == 0), stop=(s == 7)),
                                 reads=[PVB, HTA], writes=[PB])
                    cs, sn = cosT[:, pb, :], sinT[:, pb, :]
                    tt("dve", wt[0][:], pr_[:, :], cs, ALU.mult, [PR, TBB], [WT[0]])
                    tt("dve", wt[1][:], pi_[:, :], sn, ALU.mult, [PI, TBB], [WT[1]])
                    tt("dve", wt[0][:], wt[0][:], wt[1][:], ALU.add, [WT[0], WT[1]], [WT[0]])
                    tt("dve", wt[1][:], pi_[:, :], cs, ALU.mult, [PI, TBB], [WT[1]])
                    tt("dve", wt[2][:], pr_[:, :], sn, ALU.mult, [PR, TBB], [WT[2]])
                    tt("dve", wt[1][:], wt[1][:], wt[2][:], ALU.subtract, [WT[1], WT[2]], [WT[1]])
                    rho = r8[:, 0, pb:pb + 1].to_broadcast([128, NJ])
                    C.op("dve", lambda E: E.tensor_tensor_scan(wt[3][:], rho, wt[0][:], 0.0, op0=ALU.mult, op1=ALU.add), reads=[R8B, WT[0]], writes=[WT[3]])
                    C.op("dve", lambda E: E.tensor_tensor_scan(wt[4][:], rho, wt[1][:], 0.0, op0=ALU.mult, op1=ALU.add), reads=[R8B, WT[1]], writes=[WT[4]])
                    tt("dve", wt[0][:], wt[3][:], cs, ALU.mult, [WT[3], TBB], [WT[0]])
                    tt("dve", wt[1][:], wt[4][:], sn, ALU.mult, [WT[4], TBB], [WT[1]])
                    tt("dve", Xb[:, 0, pb, 1:NJ + 1], wt[0][:], wt[1][:], ALU.subtract, [WT[0], WT[1]], [XBB[pb]])
                    tt("dve", wt[0][:], wt[4][:], cs, ALU.mult, [WT[4], TBB], [WT[0]])
                    tt("dve", wt[1][:], wt[3][:], sn, ALU.mult, [WT[3], TBB], [WT[1]])
                    tt("dve", Xb[:, 1, pb, 1:NJ + 1], wt[0][:], wt[1][:], ALU.add, [WT[0], WT[1]], [XBB[pb]])
                for tp in range(7, -1, -1):
                    pt, PB = ps.next()
                    nmm = (tp + 1) + 8; n = 0
                    for s in range(tp + 1):
                        C.op("pe", lambda E: E.matmul(pt[:, :], BDm[:, tp - s, :], hTf[:, dc, s::8], start=(n == 0), stop=(n == nmm - 1)),
                             reads=[BDB, HTA], writes=[PB]); n += 1
                    for ri in range(2):
                        for pb in range(4):
                            C.op("pe", lambda E: E.matmul(pt[:, :], QFm[:, ri, tp + 1, pb, :], Xb[:, ri, pb, 0:NJ], start=(n == 0), stop=(n == nmm - 1)),
                                 reads=[QFB, XBB[pb]], writes=[PB]); n += 1
                    hv = hTf[:, dc, tp::8]
                    C.op("dve", lambda E: E.scalar_tensor_tensor(out=wt[5][:], in0=hv, scalar=dcol[:, dc, 0:1], in1=pt[:, :], op0=ALU.mult, op1=ALU.add),
                         reads=[HTA, DCB, PB], writes=[WT[5]])
                    tt("pool", wt[6][:], wt[5][:], wt[5][:], ALU.mult, [WT[5]], [WT[6]])
                    C.op("pool", lambda E: E.tensor_scalar(wt[6][:], wt[6][:], 0.044715, 1.0, op0=ALU.mult, op1=ALU.add), reads=[WT[6]], writes=[WT[6]])
                    tt("pool", wt[6][:], wt[6][:], wt[5][:], ALU.mult, [WT[6], WT[5]], [WT[6]])
                    C.op("act", lambda E: E.activation(out=wt[7][:], in_=wt[6][:], func=AF.Sigmoid, scale=1.5957691216), reads=[WT[6]], writes=[WT[7]])
                    tt("dve", hv, wt[5][:], wt[7][:], ALU.mult, [WT[5], WT[7]], [HTA])
            C.barrier()
        with ExitStack() as es:
            wg = es.enter_context(SB(nc, "wglu", [128, 8, 2 * D], BF16)); WGB = C.bufs("wglu", 8)
            wsrc = T["s5_w_glu"][j].rearrange("(kc p) f -> p kc f", p=128)
            for kc in range(8):
                C.dma("pool", wg[:, kc, :], wsrc[:, kc, :], WGB[kc], writes=[WGB[kc]])
            bg, BGB = load_bcast_row(C, nc, es, "bglu", T["s5_b_glu"][j], 2 * D)
            xs = [es.enter_context(SB(nc, f"xs{i}", [128, KT, D], F32)) for i in range(2)]; XS = C.bufs("xs", 2)
            ta = [es.enter_context(SB(nc, f"ta{i}", [128, 512], F32)) for i in range(2)]; TA = C.bufs("ta", 2)
            tg = [es.enter_context(SB(nc, f"tg{i}", [128, 512], F32)) for i in range(2)]; TG = C.bufs("tg", 2)
            HTC = C.buf("hTfC")
            for c in range(NCH):
                sl = c % 2
                C.dma("sp", xs[sl][:], xin_t[c], XS[sl], reads=[xin_bufs[c * KT + k] for k in range(KT)], writes=[XS[sl]])
                for k in range(KT):
                    t0 = c * TC + k * 128
                    for dh in range(2):
                        pa, PA = ps.next(); pg, PG = ps.next()
                        for half, (pt, PB) in enumerate(((pa, PA), (pg, PG))):
                            col = half * D + dh * 512
                            for kc in range(8):
                                C.op("pe", lambda E: E.matmul(pt[:, :], hTf[:, kc, t0:t0 + 128], wg[:, kc, col:col + 512], start=(kc == 0), stop=(kc == 7)),
                                     reads=[HTC, WGB[kc]], writes=[PB])
                        i = dh
                        tt2 = lambda eng, out, a, b, op, r, w: C.op(eng, lambda E: E.tensor_tensor(out, a, b, op=op), reads=r, writes=w)
                        tt2("dve", tg[i][:], pg[:, :], bg[:, D + dh * 512:D + (dh + 1) * 512], ALU.add, [PG, BGB], [TG[i]])
                        C.op("act", lambda E: E.activation(out=tg[i][:], in_=tg[i][:], func=AF.Sigmoid), reads=[TG[i]], writes=[TG[i]])
                        tt2("dve", ta[i][:], pa[:, :], bg[:, dh * 512:(dh + 1) * 512], ALU.add, [PA, BGB], [TA[i]])
                        tt2("pool", ta[i][:], ta[i][:], tg[i][:], ALU.mult, [TA[i], TG[i]], [TA[i]])
                        tt2("pool", xs[sl][:, k, dh * 512:(dh + 1) * 512], xs[sl][:, k, dh * 512:(dh + 1) * 512], ta[i][:], ALU.add, [TA[i], XS[sl]], [XS[sl]])
                C.dma("sp", xout_t[c], xs[sl][:], XS[sl], reads=[XS[sl]], writes=[xout_bufs[c * KT + k] for k in range(KT)])
            C.barrier()
NEG = -30000.0

def nsa_phase(C, nc, T, li, xin, xin_bufs, xout, xout_bufs):
    j = li // 4
    TC = 256; NCH = S // TC; KT = TC // 128
    H, G, R, DH = 16, 4, 4, 64
    NPROJ = 2608; QD = 1024; KVD = 1536
    NCMP = 255; NSEL = 64
    qT_d = nc.dram_tensor(f"nsa_qT_{li}", [H * DH, S], BF16).ap()
    kT_d = nc.dram_tensor(f"nsa_kT_{li}", [12 * DH, S], BF16).ap()
    vT0_d = nc.dram_tensor(f"nsa_vT0_{li}", [4 * DH, S], BF16).ap()
    v_d = nc.dram_tensor(f"nsa_v_{li}", [S, 12 * DH], BF16).ap()
    QTD = C.bufs("qTd", NT); KTD = C.bufs("kTd", NT); VTD = C.bufs("vTd", NT); VD = C.bufs("vd", NT)
    xin_t = xin.rearrange("(c k p) d -> c p k d", p=128, k=KT)
    xout_t = xout.rearrange("(c k p) d -> c p k d", p=128, k=KT)

    def tt(eng, out, a, b, op, reads, writes):
        C.op(eng, lambda E: E.tensor_tensor(out, a, b, op=op), reads=reads, writes=writes)

    with ExitStack() as es0:
        es0.enter_context(nc.allow_non_contiguous_dma(reason="small parameter / strided loads"))
        ps = PS(C, nc, es0)
        idf, idb, IB = make_ident(C, nc, es0)
        gates = es0.enter_context(SB(nc, "gates", [128, NT, 48], F32)); GTB = C.buf("gates")
        o_all = es0.enter_context(SB(nc, "o_all", [128, NT, H * DH], BF16)); OAB = C.bufs("o_all", NT)
        with ExitStack() as es:
            def sb(name, shape, dt=F32):
                return es.enter_context(SB(nc, name, shape, dt))
            gbc, GB = load_bcast_row(C, nc, es, "gbc", T["norm_mix"][li], D)
            qg, QGB = load_bcast_row(C, nc, es, "qgain", T["nsa_q_gain"][j], DH)
            kg, KGB = load_bcast_row(C, nc, es, "kgain", T["nsa_k_gain"][j].rearrange("a b -> (a b)"), 3 * DH)
            win = sb("nwin", [128, 8, NPROJ], BF16); WIB = C.bufs("nwin", 8)
            wsrc = T["nsa_w_in"][j].rearrange("(kc p) f -> p kc f", p=128)
            for kc in range(8):
                C.dma("pool", win[:, kc, :], wsrc[:, kc, :], WIB[kc], writes=[WIB[kc]])
            posi = sb("posi", [128, NT], I32); POB = C.buf("posi")
            C.dma("sp", posi[:], T["positions"].rearrange("(k p) -> p k", p=128), POB, writes=[POB])
            posf = sb("posf", [128, NT]);
            C.op("dve", lambda E: E.tensor_copy(posf[:], posi[:]), reads=[POB], writes=[POB])
            trn = sb("trn", [128, NT, 8]); TRB = C.buf("trn")
            for i in range(8):
                invf = float(500000.0 ** (-i / 8.0) / (2 * np.pi))
                C.op("dve", lambda E: E.tensor_scalar(trn[:, :, i], posf[:], invf, None, op0=ALU.mult), reads=[POB], writes=[TRB])
            rc = sb("ropec", [128, NT, 8]); rs = sb("ropes", [128, NT, 8]); RPB = C.buf("rope")
            wk1 = sb("rwk1", [128, NT, 8]); wk2 = sb("rwk2", [128, NT, 8]); RWB = C.buf("rwk")
            _sincos(C, trn[:], TRB, None, (wk1[:], wk2[:]), RWB, rs[:], rc[:], RPB)
            xs = [sb(f"xs{i}", [128, KT, D]) for i in range(2)]; XS = C.bufs("xs", 2)
            hT = [sb(f"hT{i}", [128, 8, TC], BF16) for i in range(2)]; HT = C.bufs("hT", 2)
            hb = [sb(f"hb{i}", [128, D], BF16) for i in range(2)]; HB = C.bufs("hb", 2)
            junk = sb("junk", [128, D]); JB = C.buf("junk")
            ss = [sb(f"ss{i}", [128, 4]) for i in range(2)]; SSB = C.bufs("ss", 2)
            pj = sb("pj", [128, NPROJ]); PJB = C.buf("pj")
            sq = sb("sq", [128, 28 * DH]); SQB = C.buf("sq")
            st = sb("nst", [128, 4, 28]); STB = C.buf("nst")
            qk = sb("qk", [128, 28, DH]); QKB = C.buf("qk")
            rt = [sb(f"rt{i}", [128, 28, 8]) for i in range(4)]; RTB = C.bufs("rt", 4)
            qkb = sb("qkb", [128, 28 * DH + 4 * DH], BF16); QKBB = C.buf("qkb")
            vb = sb("vb", [128, 12 * DH], BF16); VBB = C.buf("vb")
            stg = [sb(f"stg{i}", [128, 16, 128], BF16) for i in range(2)]; STG = C.bufs("stg", 2)
            for c in range(NCH):
                sl = c % 2
                C.dma("sp", xs[sl][:], xin_t[c], XS[sl], reads=[xin_bufs[c * KT + k] for k in range(KT)], writes=[XS[sl]])
                for k in range(KT):
                    ti = c * KT + k; i = ti % 2
                    norm_tile(C, nc, xs[sl][:, k, :], XS[sl], gbc, GB, hb[i], HB[i], junk, JB, ss[i], SSB[i])
                    transpose_tile(C, nc, ps, hb[i], HB[i], idb, IB, hT[sl][:, :, k * 128:(k + 1) * 128], HT[sl])
                for k in range(KT):
                    ti = c * KT + k
                    for cb in range(6):
                        c0 = cb * 512; cw = min(512, NPROJ - c0)
                        pt, PB = ps.next()
                        for kc in range(8):
                            C.op("pe", lambda E: E.matmul(pt[:, 0:cw], hT[sl][:, kc, k * 128:(k + 1) * 128], win[:, kc, c0:c0 + cw], start=(kc == 0), stop=(kc == 7)),
                                 reads=[HT[sl], WIB[kc]], writes=[PB])
                        C.op("act", lambda E: E.copy(pj[:, c0:c0 + cw], pt[:, 0:cw]), reads=[PB], writes=[PJB])
                    C.op("act", lambda E: E.activation(out=gates[:, ti, :], in_=pj[:, QD + KVD:NPROJ], func=AF.Sigmoid), reads=[PJB], writes=[GTB])
                    pv = pj[:, QD:QD + KVD].rearrange("p (b kv g d) -> p b kv g d", b=3, kv=2, g=4)
                    C.op("pool", lambda E: E.tensor_copy(vb[:].rearrange("p (b g d) -> p b g d", b=3, g=4), pv[:, :, 1, :, :]), reads=[PJB], writes=[VBB])
                    C.dma("sp", v_d[ti * 128:(ti + 1) * 128, :], vb[:], VBB, reads=[VBB], writes=[VD[ti]])
                    C.op("pool", lambda E: E.tensor_copy(qk[:, 0:16, :], pj[:, 0:QD].rearrange("p (h d) -> p h d", h=16)), reads=[PJB], writes=[QKB])
                    C.op("pool", lambda E: E.tensor_copy(qk[:, 16:28, :].rearrange("p (b g) d -> p b g d", b=3), pv[:, :, 0, :, :]), reads=[PJB], writes=[QKB])
                    tt("dve", sq[:].rearrange("p (h d) -> p h d", h=28), qk[:], qk[:], ALU.mult, [QKB], [SQB])
                    C.op("dve", lambda E: E.reduce_sum(st[:, 0, :], sq[:].rearrange("p (h d) -> p h d", h=28), axis=AX.X), reads=[SQB], writes=[STB])
                    C.op("dve", lambda E: E.tensor_scalar(st[:, 1, :], st[:, 0, :], 1.0 / DH, RMS_EPS, op0=ALU.mult, op1=ALU.add), reads=[STB], writes=[STB])
                    C.op("act", lambda E: E.activation(out=st[:, 2, :], in_=st[:, 1, :], func=AF.Sqrt), reads=[STB], writes=[STB])
                    C.op("dve", lambda E: E.reciprocal(st[:, 3, :], st[:, 2, :]), reads=[STB], writes=[STB])
                    C.op("dve", lambda E: E.tensor_scalar(st[:, 3, 0:16], st[:, 3, 0:16], 0.125, None, op0=ALU.mult), reads=[STB], writes=[STB])
                    tt("dve", qk[:], qk[:], st[:, 3, :].unsqueeze(2).to_broadcast([128, 28, DH]), ALU.mult, [QKB, STB], [QKB])
                    tt("dve", qk[:, 0:16, :], qk[:, 0:16, :], qg[:].unsqueeze(1).to_broadcast([128, 16, DH]), ALU.mult, [QKB, QGB], [QKB])
                    tt("dve", qk[:, 16:28, :].rearrange("p (b g) d -> p b g d", b=3), qk[:, 16:28, :].rearrange("p (b g) d -> p b g d", b=3),
                       kg[:].rearrange("p (b d) -> p b d", b=3).unsqueeze(2).to_broadcast([128, 3, 4, DH]), ALU.mult, [QKB, KGB], [QKB])
                    cosb = rc[:, ti, :].unsqueeze(1).to_broadcast([128, 28, 8]); sinb = rs[:, ti, :].unsqueeze(1).to_broadcast([128, 28, 8])
                    x1 = qk[:, :, 0:8]; x2 = qk[:, :, 8:16]
                    tt("dve", rt[0][:], x1, cosb, ALU.mult, [QKB, RPB], [RTB[0]])
                    tt("dve", rt[1][:], x2, sinb, ALU.mult, [QKB, RPB], [RTB[1]])
                    tt("pool", rt[2][:], x2, cosb, ALU.mult, [QKB, RPB], [RTB[2]])
                    tt("pool", rt[3][:], x1, sinb, ALU.mult, [QKB, RPB], [RTB[3]])
                    tt("dve", x1, rt[0][:], rt[1][:], ALU.subtract, [RTB[0], RTB[1]], [QKB])
                    tt("dve", x2, rt[2][:], rt[3][:], ALU.add, [RTB[2], RTB[3]], [QKB])
                    C.op("act", lambda E: E.copy(qkb[:, 0:28 * DH], qk[:].rearrange("p h d -> p (h d)")), reads=[QKB], writes=[QKBB])
                    C.op("act", lambda E: E.copy(qkb[:, 28 * DH:32 * DH].rearrange("p (g d) -> p g d", g=4), pv[:, 0, 1, :, :]), reads=[PJB], writes=[QKBB])
                    sg_ = ti % 2
                    for half in range(2):
                        pt, PB = ps.next()
                        ptb = pt[:].bitcast(BF16).rearrange("p (a b) -> p a b", a=8)
                        for a in range(8):
                            blk = half * 8 + a
                            C.op("pe", lambda E: E.transpose(ptb[:, a, :], qkb[:, blk * 128:(blk + 1) * 128], idb[:]), reads=[QKBB, IB], writes=[PB])
                        C.op("act" if half == 0 else "dve",
                             (lambda E: E.copy(stg[sg_][:, half * 8:(half + 1) * 8, :], ptb)) if half == 0 else (lambda E: E.tensor_copy(stg[sg_][:, half * 8:(half + 1) * 8, :], ptb)),
                             reads=[PB], writes=[STG[sg_]])
                    tsl = slice(ti * 128, (ti + 1) * 128)
                    C.dma("sp", qT_d[:, tsl].rearrange("(a p) t -> p a t", p=128), stg[sg_][:, 0:8, :], STG[sg_], reads=[STG[sg_]], writes=[QTD[ti]])
                    C.dma("sp", kT_d[:, tsl].rearrange("(a p) t -> p a t", p=128), stg[sg_][:, 8:14, :], STG[sg_], reads=[STG[sg_]], writes=[KTD[ti]])
                    C.dma("sp", vT0_d[:, tsl].rearrange("(a p) t -> p a t", p=128), stg[sg_][:, 14:16, :], STG[sg_], reads=[STG[sg_]], writes=[VTD[ti]])
            C.barrier()
        import os
        if os.environ.get('NSA_STOP') == 'A':
            return
        with ExitStack() as es:
            def sb(name, shape, dt=F32):
                return es.enter_context(SB(nc, name, shape, dt))
            ps.reserve(4)
            ACC = [(ps.t[i], ps.b[i]) for i in range(4)]
            caus = sb("caus", [128, 4, 512], BF16); CAB = C.buf("caus")
            winm = sb("winm", [128, 4, 512], BF16)
            cz = sb("cz", [128, 512]); CZB = C.buf("cz")
            C.op("dve", lambda E: E.memset(cz[:], 0.0), writes=[CZB])
            ctmp = sb("ctmp", [128, 512])
            for d in range(4):
                C.op("pool", lambda E: E.affine_select(ctmp[:], cz[:], pattern=[[1, 512]], compare_op=ALU.is_ge, fill=NEG, base=-128 * d, channel_multiplier=-1),
                     reads=[CZB], writes=[CAB])
                C.op("dve", lambda E: E.tensor_copy(caus[:, d, :], ctmp[:]), reads=[CAB], writes=[CAB])
                C.op("dve", lambda E: E.tensor_scalar(winm[:, d, :], ctmp[:], -1.0, NEG, op0=ALU.mult, op1=ALU.add), reads=[CAB], writes=[CAB])
            mc = sb("mcmp", [128, 2, S], BF16); MCB = C.buf("mcmp")
            expm = sb("expm", [128, S], BF16); EXB = C.buf("expm")
            with ExitStack() as est:
                z16 = est.enter_context(SB(nc, "z16", [128, S], BF16))
                o16 = est.enter_context(SB(nc, "o16", [128, S], BF16))
                e16 = est.enter_context(SB(nc, "e16", [128, S], BF16))
                C.op("dve", lambda E: E.memset(z16[:], 0.0), writes=[MCB])
                for nt_ in range(2):
                    C.op("pool", lambda E: E.affine_select(mc[:, nt_, :], z16[:], pattern=[[1, S]], compare_op=ALU.is_ge, fill=NEG, base=-31 - 16 * 128 * nt_, channel_multiplier=-16),
                         reads=[MCB], writes=[MCB])
                C.op("dve", lambda E: E.memset(o16[:], 1.0), writes=[EXB])
                for hf in range(2):
                    pr_ = slice(hf * 64, (hf + 1) * 64)
                    C.op("pool", lambda E: E.affine_select(e16[pr_, :], o16[pr_, :], pattern=[[1, S]], compare_op=ALU.is_ge, fill=0.0, base=0, channel_multiplier=-64), reads=[EXB], writes=[EXB])
                    C.op("pool", lambda E: E.affine_select(expm[pr_, :], e16[pr_, :], pattern=[[-1, S]], compare_op=ALU.is_ge, fill=0.0, base=63, channel_multiplier=64), reads=[EXB], writes=[EXB])
                C.barrier()
            ov1 = sb("ov1", [128, 2, NSEL]); ov2 = sb("ov2", [128, 2, NSEL]); OVB = C.buf("ov")
            C.op("dve", lambda E: E.memset(ov1[:], 1.0), writes=[OVB])
            for nt_ in range(2):
                C.op("pool", lambda E: E.affine_select(ov2[:, nt_, :], ov1[:, nt_, :], pattern=[[-4, NSEL]], compare_op=ALU.is_ge, fill=0.0, base=1 + 128 * nt_, channel_multiplier=1), reads=[OVB], writes=[OVB])
                C.op("pool", lambda E: E.affine_select(ov1[:, nt_, :], ov2[:, nt_, :], pattern=[[4, NSEL]], compare_op=ALU.is_ge, fill=0.0, base=3 - 128 * nt_, channel_multiplier=-1), reads=[OVB], writes=[OVB])
            curt = sb("curt", [128, NT]); CUB = C.buf("curt")
            C.op("pool", lambda E: E.iota(curt[0:64, :], pattern=[[2, NT]], base=0, channel_multiplier=0, allow_small_or_imprecise_dtypes=True), writes=[CUB])
            C.op("pool", lambda E: E.iota(curt[64:128, :], pattern=[[2, NT]], base=1, channel_multiplier=0, allow_small_or_imprecise_dtypes=True), writes=[CUB])
            sidx = sb("sidx", [128, NSEL]);
            C.op("pool", lambda E: E.iota(sidx[:], pattern=[[1, NSEL]], base=0, channel_multiplier=0, allow_small_or_imprecise_dtypes=True), writes=[CUB])
            s0m = sb("s0m", [128, NSEL])
            C.op("dve", lambda E: E.tensor_scalar(s0m[:], sidx[:], 0.0, None, op0=ALU.is_equal), reads=[CUB], writes=[CUB])
            w1 = sb("cw1", [64, 2, 32, DH], BF16); W1B = C.buf("cw1")
            w2 = sb("cw2", [64, 2, DH], BF16); W2B = C.buf("cw2")
            for cc_ in range(2):
                C.dma("pool", w1[:, cc_, :, :], T["nsa_cmp_w1"][j][cc_].rearrange("(l d) e -> d l e", d=DH), W1B, writes=[W1B], par=True)
                C.dma("pool", w2[:, cc_, :], T["nsa_cmp_w2"][j][cc_], W2B, writes=[W2B], par=True)
            posT = sb("cposT", [64, 2, 32], BF16); PSTB = C.buf("cposT")
            C.dma("pool", posT[:], T["nsa_cmp_pos"][j].rearrange("c l d -> d c l"), PSTB, writes=[PSTB])
            b1c = sb("cb1", [64, 2]); B1B = C.buf("cb1")
            C.dma("sp", b1c[:], T["nsa_cmp_b1"][j].rearrange("c e -> e c"), B1B, writes=[B1B])
            cbias = sb("cbias", [64, 2]); CBB = C.buf("cbias")
            for cc_ in range(2):
                pt, PB = ps.next()
                for l in range(32):
                    C.op("pe", lambda E: E.matmul(pt[0:64, 0:1], w1[:, cc_, l, :], posT[:, cc_, l:l + 1], start=(l == 0), stop=(l == 31)), reads=[W1B, PSTB], writes=[PB])
                tt("dve", cbias[:, cc_:cc_ + 1], pt[0:64, 0:1], b1c[:, cc_:cc_ + 1], ALU.add, [PB, B1B], [CBB])
            qTg = sb("qTg", [128, 2, S], BF16); QGB_ = C.buf("qTg")
            kTg = sb("kTg", [128, 3, S], BF16); KGB_ = C.buf("kTg")
            vT0g = sb("vT0g", [64, S], BF16); V0B = C.buf("vT0g")
            vaug = sb("vaug", [128, 2, NT, DH + 1], BF16); VAB = C.buf("vaug")
            C.op("dve", lambda E: E.memset(vaug[:, :, :, DH:DH + 1], 1.0), writes=[VAB])
            kcT = sb("kcT", [128, 256], BF16); KCB = C.buf("kcT")
            vcx = sb("vcx", [128, 2, DH + 1 + NSEL], BF16); VCB = C.buf("vcx")
            C.op("dve", lambda E: E.memset(vcx[:], 0.0), writes=[VCB])
            for nt_ in range(2):
                C.op("dve", lambda E: E.tensor_copy(vcx[:, nt_, DH + 1:], ov1[:, nt_, :]), reads=[OVB], writes=[VCB])
                C.op("dve", lambda E: E.memset(vcx[:, nt_, DH:DH + 1], 1.0), writes=[VCB])
            hidT = sb("hidT", [64, 2, 256], BF16); HDB = C.buf("hidT")
            C.op("dve", lambda E: E.memset(hidT[:], 0.0), writes=[HDB])
            gw = [sb(f"gw{i}", [64, 256]) for i in range(3)]; GWB = C.bufs("gw", 3)
            pT = [sb(f"pT{i}", [128, 512], BF16) for i in range(3)]; PTB_ = C.bufs("pT", 3)
            imp = sb("imp", [128, 4, NSEL]); IMB = C.buf("imp")
            sc = [sb(f"sc{i}", [128, NSEL]) for i in range(4)]; SCB_ = C.bufs("sc", 4)
            m8 = sb("m8", [128, 16]); M8B = C.buf("m8")
            selb = sb("selb", [128, 4, 2 * NSEL], BF16); SLB = C.buf("selb")
            selT = sb("selT", [128, 512], BF16); STB2 = C.buf("selT")
            rsum = sb("rsum", [128, 4, 4, 3]); RSB = C.buf("rsum")
            ocm = sb("ocm", [128, 4, 4, DH]); OCB = C.buf("ocm")
            for g in range(G):
                for hp in range(2):
                    r0 = (g * 4 + hp * 2) * DH
                    C.dma("sp", qTg[:, hp, :], qT_d[r0:r0 + 128, :], QGB_, reads=QTD, writes=[QGB_], par=True)
                for br in range(3):
                    r0 = (br * 4 + g) * DH
                    for cp in range(2):
                        C.dma("sp", kTg[cp * 64:(cp + 1) * 64, br, :], kT_d[r0:r0 + 64, :], KGB_, reads=KTD, writes=[KGB_], par=True)
                C.dma("sp", vT0g[:], vT0_d[g * DH:(g + 1) * DH, :], V0B, reads=VTD, writes=[V0B])
                for bi, br in enumerate((1, 2)):
                    c0 = (br * 4 + g) * DH
                    C.dma("sp", vaug[:, bi, :, 0:DH], v_d[:, c0:c0 + DH].rearrange("(jt p) d -> p jt d", p=128), VAB, reads=VD, writes=[VAB], par=True)
                for cc_ in range(2):
                    src = kTg[0:64, 0, :] if cc_ == 0 else vT0g[:]
                    SRCB = KGB_ if cc_ == 0 else V0B
                    pt, PB = ps.next()
                    for l in range(32):
                        C.op("pe", lambda E: E.matmul(pt[0:64, 0:NCMP], w1[:, cc_, l, :], src[:, l:l + 16 * (NCMP - 1) + 1:16], start=(l == 0), stop=(l == 31)),
                             reads=[W1B, SRCB], writes=[PB])
                    C.op("act", lambda E: E.activation(out=gw[0][:, 0:NCMP], in_=pt[0:64, 0:NCMP], func=AF.Identity, bias=cbias[:, cc_:cc_ + 1]), reads=[PB, CBB], writes=[GWB[0]])
                    tt("dve", gw[1][:, 0:NCMP], gw[0][:, 0:NCMP], gw[0][:, 0:NCMP], ALU.mult, [GWB[0]], [GWB[1]])
                    C.op("dve", lambda E: E.tensor_scalar(gw[1][:, 0:NCMP], gw[1][:, 0:NCMP], 0.044715, 1.0, op0=ALU.mult, op1=ALU.add), reads=[GWB[1]], writes=[GWB[1]])
                    tt("dve", gw[1][:, 0:NCMP], gw[1][:, 0:NCMP], gw[0][:, 0:NCMP], ALU.mult, [GWB[1], GWB[0]], [GWB[1]])
                    C.op("act", lambda E: E.activation(out=gw[2][:, 0:NCMP], in_=gw[1][:, 0:NCMP], func=AF.Sigmoid, scale=1.5957691216), reads=[GWB[1]], writes=[GWB[2]])
                    tt("dve", hidT[:, cc_, 0:NCMP], gw[0][:, 0:NCMP], gw[2][:, 0:NCMP], ALU.mult, [GWB[0], GWB[2]], [HDB])
                pt, PB = ps.next()
                C.op("pe", lambda E: E.matmul(pt[0:64, 0:256], w2[:, 0, :], hidT[:, 0, :], start=True, stop=True), reads=[W2B, HDB], writes=[PB])
                C.op("act", lambda E: E.copy(kcT[0:64, :], pt[0:64, 0:256]), reads=[PB], writes=[KCB])
                pt, PB = ps.next()
                C.op("pe", lambda E: E.matmul(pt[64:128, 0:256], w2[:, 0, :], hidT[:, 0, :], start=True, stop=True, tile_position=(0, 64)), reads=[W2B, HDB], writes=[PB]) if False else None
                C.dma("sp", kcT[64:128, :], kcT[0:64, :], KCB, reads=[KCB], writes=[KCB])
                for nt_ in range(2):
                    pt, PB = ps.next()
                    C.op("pe", lambda E: E.matmul(pt[:, 0:DH], hidT[:, 1, nt_ * 128:(nt_ + 1) * 128], w2[:, 1, :], start=True, stop=True), reads=[HDB, W2B], writes=[PB])
                    C.op("act", lambda E: E.copy(vcx[:, nt_, 0:DH], pt[:, 0:DH]), reads=[PB], writes=[VCB])
                if os.environ.get('NSA_STOP') == 'B':
                    continue
                for tc in range(8):
                    tq = slice(tc * 512, (tc + 1) * 512)
                    nmax = (tc * 512 + 511 - 31) // 16
                    nts = [0] if nmax < 128 else [0, 1]
                    for r in range(R):
                        hp, h2 = r // 2, r % 2
                        prt = slice(h2 * 64, (h2 + 1) * 64)
                        pis = []
                        for nt_ in nts:
                            pt, PB = ps.next()
                            C.op("pe", lambda E: E.matmul(pt[:, :], kcT[prt, nt_ * 128:(nt_ + 1) * 128], qTg[prt, hp, tq], start=True, stop=False), reads=[KCB, QGB_], writes=[PB])
                            C.op("pe", lambda E: E.matmul(pt[:, :], idb[:], mc[:, nt_, tq], start=False, stop=True), reads=[IB, MCB], writes=[PB])
                            pi = nt_
                            C.op("act", lambda E: E.activation(out=pT[pi][:], in_=pt[:, :], func=AF.Exp), reads=[PB], writes=[PTB_[pi]])
                            pis.append(pi)
                        for ts in range(4):
                            pt, PB = ps.next()
                            W_ = DH + 1 + NSEL
                            for ii, nt_ in enumerate(nts):
                                C.op("pe", lambda E: E.matmul(pt[:, 0:W_], pT[nt_][:, ts * 128:(ts + 1) * 128], vcx[:, nt_, :], start=(ii == 0), stop=(ii == len(nts) - 1)),
                                     reads=[PTB_[nt_], VCB], writes=[PB])
                            ti = tc * 4 + ts
                            C.op("dve", lambda E: E.tensor_scalar(rsum[:, ts, r, 0:1], pt[:, DH:DH + 1], 1e-30, None, op0=ALU.max), reads=[PB], writes=[RSB])
                            C.op("dve", lambda E: E.reciprocal(rsum[:, ts, r, 0:1], rsum[:, ts, r, 0:1]), reads=[RSB], writes=[RSB])
                            if r == 0:
                                C.op("dve", lambda E: E.tensor_scalar(imp[:, ts, :], pt[:, DH + 1:W_], rsum[:, ts, r, 0:1], None, op0=ALU.mult), reads=[PB, RSB], writes=[IMB])
                            else:
                                C.op("dve", lambda E: E.scalar_tensor_tensor(out=imp[:, ts, :], in0=pt[:, DH + 1:W_], scalar=rsum[:, ts, r, 0:1], in1=imp[:, ts, :], op0=ALU.mult, op1=ALU.add),
                                     reads=[PB, RSB, IMB], writes=[IMB])
                            hcol = g * 4 + r
                            tt("dve", rsum[:, ts, r, 0:1], rsum[:, ts, r, 0:1], gates[:, ti, hcol:hcol + 1], ALU.mult, [RSB, GTB], [RSB])
                            C.op("dve", lambda E: E.tensor_scalar(ocm[:, ts, r, :], pt[:, 0:DH], rsum[:, ts, r, 0:1], None, op0=ALU.mult), reads=[PB, RSB], writes=[OCB])
                    for ts in ([] if 'sel' in os.environ.get('NSA_SKIP', '') else range(4)):
                        ti = tc * 4 + ts
                        C.op("dve", lambda E: E.tensor_scalar(sc[0][:], sidx[:], curt[:, ti:ti + 1], None, op0=ALU.subtract), reads=[CUB], writes=[SCB_[0]])
                        C.op("dve", lambda E: E.tensor_scalar(sc[1][:], sc[0][:], -1.0, None, op0=ALU.is_ge), reads=[SCB_[0]], writes=[SCB_[1]])
                        C.op("dve", lambda E: E.tensor_scalar(sc[2][:], sc[0][:], 0.0, None, op0=ALU.is_le), reads=[SCB_[0]], writes=[SCB_[2]])
                        tt("dve", sc[1][:], sc[1][:], sc[2][:], ALU.mult, [SCB_[1], SCB_[2]], [SCB_[1]])
                        tt("dve", sc[1][:], sc[1][:], s0m[:], ALU.max, [SCB_[1], CUB], [SCB_[1]])
                        C.op("dve", lambda E: E.tensor_scalar(sc[2][:], sc[0][:], 0.0, 2e9, op0=ALU.is_gt, op1=ALU.mult), reads=[SCB_[0]], writes=[SCB_[2]])
                        C.op("dve", lambda E: E.scalar_tensor_tensor(out=sc[3][:], in0=sc[1][:], scalar=1e9, in1=imp[:, ts, :], op0=ALU.mult, op1=ALU.add), reads=[SCB_[1], IMB], writes=[SCB_[3]])
                        tt("dve", sc[3][:], sc[3][:], sc[2][:], ALU.subtract, [SCB_[3], SCB_[2]], [SCB_[3]])
                        C.op("dve", lambda E: E.max(out=m8[:, 0:8], in_=sc[3][:]), reads=[SCB_[3]], writes=[M8B])
                        C.op("dve", lambda E: E.match_replace(out=sc[0][:], in_to_replace=m8[:, 0:8], in_values=sc[3][:], imm_value=-4e9), reads=[M8B, SCB_[3]], writes=[SCB_[0]])
                        C.op("dve", lambda E: E.max(out=m8[:, 8:16], in_=sc[0][:]), reads=[SCB_[0]], writes=[M8B])
                        C.op("dve", lambda E: E.tensor_scalar(selb[:, ts, 0:NSEL], sc[3][:], m8[:, 15:16], NEG, op0=ALU.is_lt, op1=ALU.mult), reads=[SCB_[3], M8B], writes=[SLB])
                        C.op("dve", lambda E: E.tensor_copy(selb[:, ts, NSEL:2 * NSEL], selb[:, ts, 0:NSEL]), reads=[SLB], writes=[SLB])
                    pt, PB = ps.next()
                    ptb = pt[:].bitcast(BF16)
                    for ts in range(4):
                        C.op("pe", lambda E: E.transpose(ptb[:, ts * 128:(ts + 1) * 128], selb[:, ts, :], idb[:]), reads=[SLB, IB], writes=[PB])
                    C.op("act", lambda E: E.copy(selT[:], ptb[:, 0:512]), reads=[PB], writes=[STB2])
                    for r in ([] if ('att' in os.environ.get('NSA_SKIP', '') or tc >= int(os.environ.get('NSA_TC', '8'))) else range(R)):
                        hp, h2 = r // 2, r % 2
                        prt = slice(h2 * 64, (h2 + 1) * 64)
                        hcol = g * 4 + r
                        for bi, br in enumerate((1, 2)):
                            jts = list(range(0, 4 * tc + 4)) if br == 1 else list(range(max(0, 4 * tc - 4), 4 * tc + 4))
                            for ji, jt in enumerate(jts):
                                pt, PB = ps.next()
                                d = jt - 4 * tc
                                extra = []
                                if br == 1 and 'exp' not in os.environ.get('NSA_SKIP', ''):
                                    extra.append((expm[prt, jt * 128:(jt + 1) * 128], selT[prt, :], [EXB, STB2]))
                                if d >= 0 and 'caus' not in os.environ.get('NSA_SKIP', ''):
                                    extra.append((idb[:], caus[:, d, :], [IB, CAB]))
                                elif br == 2 and d < 0:
                                    extra.append((idb[:], winm[:, d + 4, :], [IB, CAB]))
                                C.op("pe", lambda E: E.matmul(pt[:, :], kTg[prt, br, jt * 128:(jt + 1) * 128], qTg[prt, hp, tq], start=True, stop=(len(extra) == 0)),
                                     reads=[KGB_, QGB_], writes=[PB])
                                for ei, (l_, r_, rb_) in enumerate(extra):
                                    C.op("pe", lambda E: E.matmul(pt[:, :], l_, r_, start=False, stop=(ei == len(extra) - 1)), reads=rb_, writes=[PB])
                                pi = ji % 3
                                C.op("act", lambda E: E.activation(out=pT[pi][:], in_=pt[:, :], func=AF.Exp), reads=[PB], writes=[PTB_[pi]])
                                for ts in ([] if 'acc' in os.environ.get('NSA_SKIP', '') else range(4)):
                                    at, AB = ACC[ts]
                                    C.op("pe", lambda E: E.matmul(at[:, 0:DH + 1], pT[pi][:, ts * 128:(ts + 1) * 128], vaug[:, bi, jt, :], start=(ji == 0), stop=(ji == len(jts) - 1)),
                                         reads=[PTB_[pi], VAB], writes=[AB])
                            for ts in ([] if 'acc' in os.environ.get('NSA_SKIP', '') else range(4)):
                                at, AB = ACC[ts]
                                ti = tc * 4 + ts
                                cf = rsum[:, ts, r, bi + 1:bi + 2]
                                C.op("dve", lambda E: E.reciprocal(cf, at[:, DH:DH + 1]), reads=[AB], writes=[RSB])
                                gcol = br * 16 + hcol
                                tt("dve", cf, cf, gates[:, ti, gcol:gcol + 1], ALU.mult, [RSB, GTB], [RSB])
                                C.op("dve", lambda E: E.scalar_tensor_tensor(out=ocm[:, ts, r, :], in0=at[:, 0:DH], scalar=cf, in1=ocm[:, ts, r, :], op0=ALU.mult, op1=ALU.add),
                                     reads=[AB, RSB, OCB], writes=[OCB])
                    for ts in range(4):
                        ti = tc * 4 + ts
                        C.op("act", lambda E: E.copy(o_all[:, ti, g * 256:(g + 1) * 256], ocm[:, ts, :, :].rearrange("p r d -> p (r d)")), reads=[OCB], writes=[OAB[ti]])
            ps.reserve(0)
            C.barrier()
        with ExitStack() as es:
            wo = es.enter_context(SB(nc, "nwo", [128, 8, D], BF16)); WOB = C.buf("nwo")
            C.dma("pool", wo[:], T["nsa_w_out"][j].rearrange("(kc p) d -> p kc d", p=128), WOB, writes=[WOB])
            xs = [es.enter_context(SB(nc, f"xs{i}", [128, KT, D], F32)) for i in range(2)]; XS = C.bufs("xs", 2)
            oT = [es.enter_context(SB(nc, f"oT{i}", [128, 8, 128], BF16)) for i in range(2)]; OTB = C.bufs("oT", 2)
            OH = C.buf("oall_d")
            for c in range(NCH):
                sl = c % 2
                C.dma("sp", xs[sl][:], xin_t[c], XS[sl], reads=[xin_bufs[c * KT + k] for k in range(KT)], writes=[XS[sl]])
                for k in range(KT):
                    ti = c * KT + k; i = ti % 2
                    pt, PB = ps.next()
                    ptb = pt[:].bitcast(BF16).rearrange("p (a b) -> p a b", a=8)
                    for a in range(8):
                        C.op("pe", lambda E: E.transpose(ptb[:, a, :], o_all[:, ti, a * 128:(a + 1) * 128], idb[:]), reads=[OH, IB], writes=[PB])
                    C.op("act", lambda E: E.copy(oT[i][:], ptb), reads=[PB], writes=[OTB[i]])
                    for dh in range(2):
                        pt, PB = ps.next()
                        for kc in range(8):
                            C.op("pe", lambda E: E.matmul(pt[:, :], oT[i][:, kc, :], wo[:, kc, dh * 512:(dh + 1) * 512], start=(kc == 0), stop=(kc == 7)), reads=[OTB[i], WOB], writes=[PB])
                        tt("dve", xs[sl][:, k, dh * 512:(dh + 1) * 512], xs[sl][:, k, dh * 512:(dh + 1) * 512], pt[:, :], ALU.add, [PB, XS[sl]], [XS[sl]])
                C.dma("sp", xout_t[c], xs[sl][:], XS[sl], reads=[XS[sl]], writes=[xout_bufs[c * KT + k] for k in range(KT)])
            C.barrier()
MIXERS = {}
MIXERS['conv'] = conv_phase
MIXERS['pool'] = pool_phase
MIXERS['s5'] = s5_phase
MIXERS['nsa'] = nsa_phase
PARAM_SHAPES = None

def build_program(shapes, mixers=("conv", "nsa", "s5", "pool")):
    nc = bass.Bass("TRN2", target_bir_lowering=False)
    T = {}
    for name, shp in shapes.items():
        dt = I32 if name == "positions" else F32
        T[name] = nc.dram_tensor(name, list(shp), dt, kind="ExternalInput").ap()
    y = nc.dram_tensor("y", [S, D], F32, kind="ExternalOutput").ap()
    C = Ctx(nc)
    ybufs = C.bufs("yd", NT)
    nobufs = C.bufs("xin", NT)
    cur, curb = T["x"], nobufs
    for i in range(4):
        m = mixers[i]
        fn = MIXERS.get(m)
        if fn is not None:
            fn(C, nc, T, i, cur, curb, y, ybufs)
            cur, curb = y, ybufs
        ffn_phase(C, nc, cur, curb, y, ybufs, T["norm_ffn"][i], T["ffn_w_in"][i], T["ffn_w_out"][i])
        cur, curb = y, ybufs
    C.finish()
    return nc

_NC_CACHE = {}

def kernel(**inputs):
    inputs = {k: np.ascontiguousarray(np.asarray(v)) for k, v in inputs.items()}
    x = inputs["x"]
    B = x.shape[0]
    shapes = {k: (v.shape[1:] if k == "x" else v.shape) for k, v in inputs.items()}
    key = tuple(sorted((k, tuple(s)) for k, s in shapes.items()))
    if key not in _NC_CACHE:
        _NC_CACHE[key] = build_program(shapes)
    nc = _NC_CACHE[key]
    in_maps = []
    for b in range(B):
        m = {k: v for k, v in inputs.items() if k != "x"}
        m["x"] = np.ascontiguousarray(x[b])
        in_maps.append(m)
    res = run_bass_kernel_spmd(nc, in_maps, core_ids=list(range(B)))
    return np.stack([np.asarray(r["y"]) for r in res.results], axis=0).astype(np.float32)
```

```python
import numpy as np
from contextlib import ExitStack
import concourse.bass as bass
import concourse.mybir as mybir
from concourse.bass_utils import run_bass_kernel_spmd

F32 = mybir.dt.float32
BF16 = mybir.dt.bfloat16
I32 = mybir.dt.int32
ALU = mybir.AluOpType
AF = mybir.ActivationFunctionType
AX = mybir.AxisListType

SEM_ROLL = 30000
SAME_ENGINE_SYNC = True


class Buf:
    __slots__ = ("name", "w", "r", "dsem", "dcnt")

    def __init__(self, name):
        self.name = name
        self.w = None
        self.r = {}
        self.dsem = None
        self.dcnt = 0


class Ctx:
    def __init__(self, nc):
        self.nc = nc
        self.engs = {"pe": nc.tensor, "dve": nc.vector, "act": nc.scalar,
                     "pool": nc.gpsimd, "sp": nc.sync}
        self.stack = ExitStack()
        self.sem = {}
        self.cnt = {}
        self.seen = {e: {} for e in self.engs}
        self.nsem = 0
        self.allsems = {}
        self.sem_pool = []
        self.live_owners = []
        for e in self.engs:
            self._newsem(e)

    def _alloc_sem(self, name):
        s = self.stack.enter_context(self.nc.semaphore(name))
        self.nsem += 1
        return s

    def _newsem(self, e):
        self.sem[e] = self._alloc_sem(f"s_{e}_{self.nsem}")
        self.cnt[e] = 0

    def buf(self, name):
        return Buf(name)

    def bufs(self, name, n):
        return [Buf(f"{name}{i}") for i in range(n)]

    def _collect(self, eng, reads, writes, skip_sem=None):
        need = {}

        def req(st):
            if st is None:
                return
            s, v, e = st
            if e == eng and (eng == "pe" or not SAME_ENGINE_SYNC) and e in ("pe", "dve", "act", "pool"):
                return
            if skip_sem is not None and s.name == skip_sem.name:
                return
            k = s.name
            if k not in need or need[k][1] < v:
                need[k] = (s, v)

        for b in reads:
            req(b.w)
        for b in writes:
            req(b.w)
            for st in b.r.values():
                req(st)
        E = self.engs[eng]
        for k, (s, v) in need.items():
            if self.seen[eng].get(k, 0) < v:
                E.wait_ge(s, v)
                self.seen[eng][k] = v

    def op(self, eng, fn, reads=(), writes=()):
        self._collect(eng, reads, writes)
        ins = fn(self.engs[eng])
        if self.cnt[eng] >= SEM_ROLL:
            self._newsem(eng)
        s = self.sem[eng]
        self.cnt[eng] += 1
        ins.then_inc(s, 1)
        st = (s, self.cnt[eng], eng)
        self.allsems[s.name] = (s, self.cnt[eng])
        for b in reads:
            b.r[eng] = st
        for b in writes:
            b.w = st
            b.r = {}
        return ins

    def dma(self, q, out, in_, owner, reads=(), writes=(), par=False, **kw):
        if owner.dsem is None:
            if self.sem_pool:
                owner.dsem, owner.dcnt = self.sem_pool.pop()
            else:
                owner.dsem = self._alloc_sem(f"d_{owner.name}_{self.nsem}")
                owner.dcnt = 0
            self.live_owners.append(owner)
        self._collect(q, reads, writes, skip_sem=owner.dsem if par else None)
        ins = self.engs[q].dma_start(out=out, in_=in_, **kw)
        owner.dcnt += 16
        ins.then_inc(owner.dsem, 16)
        st = (owner.dsem, owner.dcnt, "dma")
        self.allsems[owner.dsem.name] = (owner.dsem, owner.dcnt)
        for b in reads:
            b.r["dma_" + owner.dsem.name] = st
        for b in writes:
            b.w = st
            b.r = {}
        return ins

    def barrier(self, engines=None):
        for e in (engines or self.engs):
            E = self.engs[e]
            for k, (s, v) in self.allsems.items():
                if s.name == self.sem[e].name:
                    continue
                if self.seen[e].get(k, 0) < v:
                    E.wait_ge(s, v)
                    self.seen[e][k] = v
        if engines is None:
            for o in self.live_owners:
                self.sem_pool.append((o.dsem, o.dcnt))
                o.dsem = None
            self.live_owners = []

    def finish(self):
        self.barrier()
        self.stack.close()

_TN = [0]
def SB(nc, name, shape, dt):
    _TN[0] += 1
    return nc.sbuf_tensor(f"{name}_{_TN[0]}", shape, dt)
def PSB(nc, name, shape, dt):
    _TN[0] += 1
    return nc.psum_tensor(f"{name}_{_TN[0]}", shape, dt)
D = 1024; S = 4096; FF = 2816; NT = S // 128
RMS_EPS = 1e-6

class PS:
    def __init__(self, C, nc, es):
        self.t = []; self.b = []
        for i in range(8):
            t = es.enter_context(PSB(nc, f"psb{i}", [128, 512], F32))
            self.t.append(t); self.b.append(C.buf(f"psb{i}"))
        self.i = 0; self.lo = 0
    def reserve(self, n):
        self.lo = n; self.i = n
    def next(self):
        i = self.i; self.i = i + 1 if i + 1 < 8 else self.lo
        return self.t[i], self.b[i]

def make_ident(C, nc, es):
    idf = es.enter_context(SB(nc, "identf", [128, 128], F32))
    idb = es.enter_context(SB(nc, "identb", [128, 128], BF16))
    tmp = es.enter_context(SB(nc, "identi", [128, 128], I32))
    B = C.buf("ident")
    C.op("pool", lambda E: E.iota(tmp[:], pattern=[[1, 128]], base=0, channel_multiplier=-1), writes=[B])
    C.op("dve", lambda E: E.tensor_scalar(idf[:], tmp[:], 0.0, None, op0=ALU.is_equal), reads=[B], writes=[B])
    C.op("dve", lambda E: E.tensor_copy(idb[:], idf[:]), reads=[B], writes=[B])
    return idf, idb, B

def load_bcast_row(C, nc, es, name, row_ap, n):
    t = es.enter_context(SB(nc, name, [128, n], F32))
    B = C.buf(name)
    C.dma("sp", t[:], row_ap.partition_broadcast(128), B, writes=[B])
    return t, B

def norm_tile(C, nc, xs_ap, XB, gbc, GB, h, HB, junk, JB, ss, SSB):
    C.op("dve", lambda E: E.tensor_tensor(junk[:], xs_ap, xs_ap, op=ALU.mult), reads=[XB], writes=[JB])
    C.op("dve", lambda E: E.reduce_sum(ss[:, 0:1], junk[:], axis=AX.X), reads=[JB], writes=[SSB])
    C.op("dve", lambda E: E.tensor_scalar(ss[:, 1:2], ss[:, 0:1], 1.0 / D, RMS_EPS, op0=ALU.mult, op1=ALU.add),
         reads=[SSB], writes=[SSB])
    C.op("act", lambda E: E.activation(out=ss[:, 2:3], in_=ss[:, 1:2], func=AF.Sqrt), reads=[SSB], writes=[SSB])
    C.op("dve", lambda E: E.reciprocal(ss[:, 3:4], ss[:, 2:3]), reads=[SSB], writes=[SSB])
    C.op("dve", lambda E: E.scalar_tensor_tensor(out=h[:], in0=xs_ap, scalar=ss[:, 3:4], in1=gbc[:],
                                                 op0=ALU.mult, op1=ALU.mult),
         reads=[XB, SSB, GB], writes=[HB])

def transpose_tile(C, nc, ps, h, HB, idb, IB, dst_ap, DB, eng="act"):
    pt, PB = ps.next()
    ptb = pt[:].bitcast(BF16).rearrange("p (a b) -> p a b", a=8)
    for dc in range(8):
        C.op("pe", lambda E: E.transpose(ptb[:, dc, :], h[:, dc * 128:(dc + 1) * 128], idb[:]),
             reads=[HB, IB], writes=[PB])
    if eng == "act":
        C.op("act", lambda E: E.copy(dst_ap, ptb), reads=[PB], writes=[DB])
    else:
        C.op("dve", lambda E: E.tensor_copy(dst_ap, ptb), reads=[PB], writes=[DB])
def ffn_phase(C, nc, xin, xin_bufs, xout, xout_bufs, gain_ap, win_ap, wout_ap):
    TC = 256; NCH = S // TC; KT = TC // 128; NF = FF // 128
    with ExitStack() as es:
        ps = PS(C, nc, es)
        idf, idb, IB = make_ident(C, nc, es)
        gbc, GB = load_bcast_row(C, nc, es, "gbc", gain_ap, D)
        win = es.enter_context(SB(nc, "win", [128, 8, 2 * FF], BF16))
        wout = es.enter_context(SB(nc, "wout", [128, NF, D], BF16))
        WIB = C.bufs("winb", 8); WOB = C.bufs("woutb", 2)
        wsrc = win_ap.rearrange("(kc p) f -> p kc f", p=128)
        for kc in range(8):
            C.dma("pool", win[:, kc, :], wsrc[:, kc, :], WIB[kc], writes=[WIB[kc]])
        wosrc = wout_ap.rearrange("(fc p) d -> p fc d", p=128)
        for hh in range(2):
            C.dma("pool", wout[:, hh * 11:(hh + 1) * 11, :], wosrc[:, hh * 11:(hh + 1) * 11, :], WOB[hh], writes=[WOB[hh]])
        xs = [es.enter_context(SB(nc, f"xs{i}", [128, KT, D], F32)) for i in range(2)]
        XS = C.bufs("xs", 2)
        hT = [es.enter_context(SB(nc, f"hT{i}", [128, 8, TC], BF16)) for i in range(2)]
        HT = C.bufs("hT", 2)
        hb = [es.enter_context(SB(nc, f"hb{i}", [128, D], BF16)) for i in range(2)]
        HB = C.bufs("hb", 2)
        junk = es.enter_context(SB(nc, "junk", [128, D], F32)); JB = C.buf("junk")
        ss = [es.enter_context(SB(nc, f"ss{i}", [128, 4], F32)) for i in range(2)]
        SSB = C.bufs("ss", 2)
        actT = es.enter_context(SB(nc, "actT", [128, NF, TC], BF16)); AT = C.bufs("actT", NF)
        sg = [es.enter_context(SB(nc, f"sg{i}", [128, TC], F32)) for i in range(2)]
        SG = C.bufs("sg", 2)
        xin_t = xin.rearrange("(c k p) d -> c p k d", p=128, k=KT)
        xout_t = xout.rearrange("(c k p) d -> c p k d", p=128, k=KT)

        def load(c):
            sl = c % 2
            C.dma("sp", xs[sl][:], xin_t[c], XS[sl], reads=[xin_bufs[c * KT + k] for k in range(KT)], writes=[XS[sl]])

        def prep(c):
            sl = c % 2
            for k in range(KT):
                i = (c * KT + k) % 2
                norm_tile(C, nc, xs[sl][:, k, :], XS[sl], gbc, GB, hb[i], HB[i], junk, JB, ss[i], SSB[i])
                transpose_tile(C, nc, ps, hb[i], HB[i], idb, IB, hT[sl][:, :, k * 128:(k + 1) * 128], HT[sl])

        load(0)
        for c in range(NCH):
            sl = c % 2
            if c + 1 < NCH:
                load(c + 1)
            prep(c)
            for fc in range(NF):
                pt, PB = ps.next()
                for half in range(2):
                    col = half * FF + fc * 128
                    for kc in range(8):
                        C.op("pe", lambda E: E.matmul(pt[:, half * TC:(half + 1) * TC], win[:, kc, col:col + 128], hT[sl][:, kc, :],
                                                      start=(kc == 0), stop=(kc == 7)),
                             reads=[WIB[kc], HT[sl]], writes=[PB])
                j = fc % 2
                C.op("act", lambda E: E.activation(out=sg[j][:], in_=pt[:, 0:TC], func=AF.Silu), reads=[PB], writes=[SG[j]])
                C.op("dve", lambda E: E.tensor_tensor(actT[:, fc, :], sg[j][:], pt[:, TC:2 * TC], op=ALU.mult),
                     reads=[SG[j], PB], writes=[AT[fc]])
            for k in range(KT):
                for dh in range(2):
                    pt, PB = ps.next()
                    for fc in range(NF):
                        C.op("pe", lambda E: E.matmul(pt[:, :], actT[:, fc, k * 128:(k + 1) * 128], wout[:, fc, dh * 512:(dh + 1) * 512],
                                                      start=(fc == 0), stop=(fc == NF - 1)),
                             reads=[AT[fc], WOB[fc // 11]], writes=[PB])
                    C.op("dve", lambda E: E.tensor_tensor(xs[sl][:, k, dh * 512:(dh + 1) * 512], xs[sl][:, k, dh * 512:(dh + 1) * 512], pt[:, :], op=ALU.add),
                         reads=[PB, XS[sl]], writes=[XS[sl]])
            C.dma("sp", xout_t[c], xs[sl][:], XS[sl], reads=[XS[sl]], writes=[xout_bufs[c * KT + k] for k in range(KT)])
        C.barrier()
def load_cols(C, nc, es, ps, idf, IB, rows, name):
    R = sum(n for _, n in rows)
    rt = es.enter_context(SB(nc, name + "_rows", [R, D], F32)); RB = C.buf(name + "_rows")
    r0 = 0
    for ap, n in rows:
        C.dma("sp", rt[r0:r0 + n, :], ap, RB, writes=[RB], par=True)
        r0 += n
    cols = es.enter_context(SB(nc, name + "_cols", [128, 8, R], F32)); CB = C.buf(name + "_cols")
    for dc in range(8):
        pt, PB = ps.next()
        C.op("pe", lambda E: E.transpose(pt[:, 0:R], rt[:, dc * 128:(dc + 1) * 128], idf[0:R, 0:R]), reads=[RB, IB], writes=[PB])
        C.op("dve", lambda E: E.tensor_copy(cols[:, dc, :], pt[:, 0:R]), reads=[PB], writes=[CB])
    return cols, CB

def conv_phase(C, nc, T, li, xin, xin_bufs, xout, xout_bufs):
    j = li // 4
    TC = 256; NCH = S // TC; KT = TC // 128; KW = 31; HALO = KW - 1
    with ExitStack() as es:
        ps = PS(C, nc, es)
        idf, idb, IB = make_ident(C, nc, es)
        gbc, GB = load_bcast_row(C, nc, es, "gbc", T["norm_mix"][li], D)
        win = es.enter_context(SB(nc, "cwin", [128, 8, 2 * D], BF16)); WIB = C.bufs("cwinb", 8)
        wsrc = T["conv_w_in"][j].rearrange("(kc p) f -> p kc f", p=128)
        for kc in range(8):
            C.dma("pool", win[:, kc, :], wsrc[:, kc, :], WIB[kc], writes=[WIB[kc]])
        wout = es.enter_context(SB(nc, "cwout", [128, 8, D], BF16)); WOB = C.buf("cwoutb")
        C.dma("pool", wout[:], T["conv_w_out"][j].rearrange("(kc p) d -> p kc d", p=128), WOB, writes=[WOB])
        cols, CB = load_cols(C, nc, es, ps, idf, IB, [
            (T["conv_b_in"][j].rearrange("(two d) -> two d", two=2), 2),
            (T["conv_b_dw"][j:j + 1, :], 1), (T["conv_ln_g"][j:j + 1, :], 1), (T["conv_ln_b"][j:j + 1, :], 1),
            (T["conv_w_dw"][j], KW)], "cv")
        diag = es.enter_context(SB(nc, "diag", [128, KW, 8, 128], BF16))
        DGB = [[C.buf(f"diag{k}_{dc}") for dc in range(8)] for k in range(KW)]
        for k in range(KW):
            for dc in range(8):
                eng = "dve" if (k + dc) % 2 == 0 else "pool"
                C.op(eng, lambda E: E.tensor_scalar(diag[:, k, dc, :], idf[:], cols[:, dc, 5 + k:6 + k], None, op0=ALU.mult),
                     reads=[IB, CB], writes=[DGB[k][dc]])
        ones = es.enter_context(SB(nc, "ones", [128, 128], F32)); ONB = C.buf("ones")
        C.op("dve", lambda E: E.memset(ones[:], 1.0), writes=[ONB])
        xs = [es.enter_context(SB(nc, f"xs{i}", [128, KT, D], F32)) for i in range(2)]; XS = C.bufs("xs", 2)
        hT = [es.enter_context(SB(nc, f"hT{i}", [128, 8, TC], BF16)) for i in range(2)]; HT = C.bufs("hT", 2)
        hb = [es.enter_context(SB(nc, f"hb{i}", [128, D], BF16)) for i in range(2)]; HB = C.bufs("hb", 2)
        junk = es.enter_context(SB(nc, "junk", [128, D], F32)); JB = C.buf("junk")
        ss = [es.enter_context(SB(nc, f"ss{i}", [128, 4], F32)) for i in range(2)]; SSB = C.bufs("ss", 2)
        ub = es.enter_context(SB(nc, "ubuf", [128, 8, HALO + TC], BF16)); UB = C.bufs("ub", 8)
        C.op("dve", lambda E: E.memset(ub[:], 0.0), writes=UB)
        sg = [es.enter_context(SB(nc, f"sg{i}", [128, TC], F32)) for i in range(2)]; SG = C.bufs("sg", 2)
        v = es.enter_context(SB(nc, "v", [128, 8, TC], F32)); VB = C.bufs("v", 8)
        vsq = es.enter_context(SB(nc, "vsq", [128, 8, TC], F32)); VQ = C.bufs("vsq", 8)
        st = es.enter_context(SB(nc, "st", [128, 4, TC], F32)); STB = C.buf("st")
        z = [es.enter_context(SB(nc, f"z{i}", [128, TC], F32)) for i in range(2)]; ZB = C.bufs("z", 2)
        yT = es.enter_context(SB(nc, "yT", [128, 8, TC], BF16)); YT = C.bufs("yT", 8)
        xin_t = xin.rearrange("(c k p) d -> c p k d", p=128, k=KT)
        xout_t = xout.rearrange("(c k p) d -> c p k d", p=128, k=KT)

        def load(c):
            sl = c % 2
            C.dma("sp", xs[sl][:], xin_t[c], XS[sl], reads=[xin_bufs[c * KT + k] for k in range(KT)], writes=[XS[sl]])

        load(0)
        for c in range(NCH):
            sl = c % 2
            if c + 1 < NCH:
                load(c + 1)
            for k in range(KT):
                i = (c * KT + k) % 2
                norm_tile(C, nc, xs[sl][:, k, :], XS[sl], gbc, GB, hb[i], HB[i], junk, JB, ss[i], SSB[i])
                transpose_tile(C, nc, ps, hb[i], HB[i], idb, IB, hT[sl][:, :, k * 128:(k + 1) * 128], HT[sl])
            for dc in range(8):
                pt, PB = ps.next()
                for half in range(2):
                    col = half * D + dc * 128
                    for kc in range(8):
                        C.op("pe", lambda E: E.matmul(pt[:, half * TC:(half + 1) * TC], win[:, kc, col:col + 128], hT[sl][:, kc, :],
                                                      start=(kc == 0), stop=(kc == 7)), reads=[WIB[kc], HT[sl]], writes=[PB])
                jj = dc % 2
                C.op("act", lambda E: E.activation(out=sg[jj][:], in_=pt[:, TC:2 * TC], func=AF.Sigmoid, bias=cols[:, dc, 1:2]),
                     reads=[PB, CB], writes=[SG[jj]])
                C.op("dve", lambda E: E.scalar_tensor_tensor(out=ub[:, dc, HALO:HALO + TC], in0=pt[:, 0:TC], scalar=cols[:, dc, 0:1],
                                                             in1=sg[jj][:], op0=ALU.add, op1=ALU.mult),
                     reads=[PB, CB, SG[jj]], writes=[UB[dc]])
            for dc in range(8):
                pt, PB = ps.next()
                for k in range(KW):
                    C.op("pe", lambda E: E.matmul(pt[:, 0:TC], diag[:, k, dc, :], ub[:, dc, k:k + TC], start=(k == 0), stop=(k == KW - 1)),
                         reads=[DGB[k][dc], UB[dc]], writes=[PB])
                C.op("act", lambda E: E.activation(out=v[:, dc, :], in_=pt[:, 0:TC], func=AF.Identity, bias=cols[:, dc, 2:3]),
                     reads=[PB, CB], writes=[VB[dc]])
                C.op("pool", lambda E: E.tensor_tensor(vsq[:, dc, :], v[:, dc, :], v[:, dc, :], op=ALU.mult), reads=[VB[dc]], writes=[VQ[dc]])
                C.op("pool", lambda E: E.tensor_copy(ub[:, dc, 0:HALO], ub[:, dc, TC:TC + HALO]), reads=[], writes=[UB[dc]])
            pt, PB = ps.next()
            for dc in range(8):
                C.op("pe", lambda E: E.matmul(pt[:, 0:TC], ones[:], v[:, dc, :], start=(dc == 0), stop=(dc == 7)), reads=[ONB, VB[dc]], writes=[PB])
            for dc in range(8):
                C.op("pe", lambda E: E.matmul(pt[:, TC:2 * TC], ones[:], vsq[:, dc, :], start=(dc == 0), stop=(dc == 7)), reads=[ONB, VQ[dc]], writes=[PB])
            C.op("dve", lambda E: E.tensor_scalar(st[:, 0, :], pt[:, 0:TC], 1.0 / D, None, op0=ALU.mult), reads=[PB], writes=[STB])
            C.op("dve", lambda E: E.tensor_tensor(st[:, 1, :], st[:, 0, :], st[:, 0, :], op=ALU.mult), reads=[STB], writes=[STB])
            C.op("dve", lambda E: E.scalar_tensor_tensor(out=st[:, 2, :], in0=pt[:, TC:2 * TC], scalar=1.0 / D, in1=st[:, 1, :], op0=ALU.mult, op1=ALU.subtract),
                 reads=[PB, STB], writes=[STB])
            C.op("dve", lambda E: E.tensor_scalar(st[:, 2, :], st[:, 2, :], 1e-5, None, op0=ALU.add), reads=[STB], writes=[STB])
            C.op("act", lambda E: E.activation(out=st[:, 3, :], in_=st[:, 2, :], func=AF.Sqrt), reads=[STB], writes=[STB])
            C.op("dve", lambda E: E.reciprocal(st[:, 3, :], st[:, 3, :]), reads=[STB], writes=[STB])
            for dc in range(8):
                jj = dc % 2
                C.op("dve", lambda E: E.tensor_tensor(z[jj][:], v[:, dc, :], st[:, 0, :], op=ALU.subtract), reads=[VB[dc], STB], writes=[ZB[jj]])
                C.op("dve", lambda E: E.tensor_tensor(z[jj][:], z[jj][:], st[:, 3, :], op=ALU.mult), reads=[ZB[jj], STB], writes=[ZB[jj]])
                C.op("act", lambda E: E.activation(out=yT[:, dc, :], in_=z[jj][:], func=AF.Silu, scale=cols[:, dc, 3:4], bias=cols[:, dc, 4:5]),
                     reads=[ZB[jj], CB], writes=[YT[dc]])
            for k in range(KT):
                for dh in range(2):
                    pt, PB = ps.next()
                    for dc in range(8):
                        C.op("pe", lambda E: E.matmul(pt[:, :], yT[:, dc, k * 128:(k + 1) * 128], wout[:, dc, dh * 512:(dh + 1) * 512],
                                                      start=(dc == 0), stop=(dc == 7)), reads=[YT[dc], WOB], writes=[PB])
                    C.op("dve", lambda E: E.tensor_tensor(xs[sl][:, k, dh * 512:(dh + 1) * 512], xs[sl][:, k, dh * 512:(dh + 1) * 512], pt[:, :], op=ALU.add),
                         reads=[PB, XS[sl]], writes=[XS[sl]])
            C.dma("sp", xout_t[c], xs[sl][:], XS[sl], reads=[XS[sl]], writes=[xout_bufs[c * KT + k] for k in range(KT)])
        C.barrier()
def pool_phase(C, nc, T, li, xin, xin_bufs, xout, xout_bufs):
    j = li // 4
    TC = 256; NCH = S // TC; KT = TC // 128; HALO = 16
    WINS = (2, 4, 8, 16)
    with ExitStack() as es:
        ps = PS(C, nc, es)
        idf, idb, IB = make_ident(C, nc, es)
        gbc, GB = load_bcast_row(C, nc, es, "gbc", T["norm_mix"][li], D)
        scb, SCB = load_bcast_row(C, nc, es, "pscale", T["pool_scale"][j], D)
        pw = es.enter_context(SB(nc, "pw", [128, 4, 2, 256], BF16)); PWB = C.buf("pw")
        C.dma("pool", pw[:], T["pool_w"][j].rearrange("g (k p) d -> p g k d", p=128), PWB, writes=[PWB])
        invc = es.enter_context(SB(nc, "invc", [128, 4, 2, TC], F32)); ICB = C.buf("invc")
        iot = es.enter_context(SB(nc, "iot", [128, TC], F32)); IOB = C.buf("iot")
        C.op("pool", lambda E: E.iota(iot[:], pattern=[[1, TC]], base=1, channel_multiplier=0, allow_small_or_imprecise_dtypes=True), writes=[IOB])
        for gi, w in enumerate(WINS):
            C.op("dve", lambda E: E.tensor_scalar(invc[:, gi, 0, :], iot[:], float(w), None, op0=ALU.min), reads=[IOB], writes=[ICB])
            C.op("dve", lambda E: E.reciprocal(invc[:, gi, 0, :], invc[:, gi, 0, :]), reads=[ICB], writes=[ICB])
            C.op("dve", lambda E: E.memset(invc[:, gi, 1, :], 1.0 / w), writes=[ICB])
        xs = [es.enter_context(SB(nc, f"xs{i}", [128, KT, D], F32)) for i in range(2)]; XS = C.bufs("xs", 2)
        hb = [es.enter_context(SB(nc, f"hb{i}", [128, D], BF16)) for i in range(2)]; HB = C.bufs("hb", 2)
        junk = es.enter_context(SB(nc, "junk", [128, D], F32)); JB = C.buf("junk")
        ss = [es.enter_context(SB(nc, f"ss{i}", [128, 4], F32)) for i in range(2)]; SSB = C.bufs("ss", 2)
        hbuf = es.enter_context(SB(nc, "hbuf", [128, 8, HALO + TC], BF16)); HBF = C.buf("hbuf")
        C.op("dve", lambda E: E.memset(hbuf[:], 0.0), writes=[HBF])
        ta = es.enter_context(SB(nc, "ta", [128, 2, HALO + TC], F32)); TA = C.buf("ta")
        tb = es.enter_context(SB(nc, "tb", [128, 2, HALO + TC], F32)); TB = C.buf("tb")
        pT = es.enter_context(SB(nc, "pT", [128, 8, TC], BF16)); PT = C.bufs("pT", 4)
        tmp = [es.enter_context(SB(nc, f"tmp{i}", [128, 512], F32)) for i in range(2)]; TM = C.bufs("tmp", 2)
        xin_t = xin.rearrange("(c k p) d -> c p k d", p=128, k=KT)
        xout_t = xout.rearrange("(c k p) d -> c p k d", p=128, k=KT)
        W = HALO + TC

        def load(c):
            sl = c % 2
            C.dma("sp", xs[sl][:], xin_t[c], XS[sl], reads=[xin_bufs[c * KT + k] for k in range(KT)], writes=[XS[sl]])

        load(0)
        for c in range(NCH):
            sl = c % 2
            if c + 1 < NCH:
                load(c + 1)
            if c > 0:
                C.op("dve", lambda E: E.tensor_copy(hbuf[:, :, 0:HALO], hbuf[:, :, TC:TC + HALO]), reads=[HBF], writes=[HBF])
            for k in range(KT):
                i = (c * KT + k) % 2
                norm_tile(C, nc, xs[sl][:, k, :], XS[sl], gbc, GB, hb[i], HB[i], junk, JB, ss[i], SSB[i])
                transpose_tile(C, nc, ps, hb[i], HB[i], idb, IB, hbuf[:, :, HALO + k * 128:HALO + (k + 1) * 128], HBF)
            for gi, w in enumerate(WINS):
                src = hbuf[:, 2 * gi:2 * gi + 2, :]
                cur, CURB = src, HBF
                sh = 1
                dst_list = [(ta, TA), (tb, TB)]
                di = 0
                while sh < w:
                    dst, DSTB = dst_list[di]; di ^= 1
                    C.op("dve", lambda E: E.tensor_tensor(dst[:, :, sh:W], cur[:, :, sh:W], cur[:, :, 0:W - sh], op=ALU.add),
                         reads=[CURB], writes=[DSTB])
                    cur, CURB = dst, DSTB
                    sh *= 2
                ic = invc[:, gi, 0 if c == 0 else 1, :]
                dst, DSTB = dst_list[di]
                for q in range(2):
                    C.op("dve", lambda E: E.tensor_tensor(dst[:, q, HALO:W], cur[:, q, HALO:W], ic, op=ALU.mult), reads=[CURB, ICB], writes=[DSTB])
                C.op("dve", lambda E: E.tensor_tensor(pT[:, 2 * gi:2 * gi + 2, :], dst[:, :, HALO:W], src[:, :, HALO:W], op=ALU.subtract),
                     reads=[DSTB, HBF], writes=[PT[gi]])
            for k in range(KT):
                for hh in range(2):
                    pt, PB = ps.next()
                    for g2 in range(2):
                        gi = hh * 2 + g2
                        for kc in range(2):
                            C.op("pe", lambda E: E.matmul(pt[:, g2 * 256:(g2 + 1) * 256], pT[:, 2 * gi + kc, k * 128:(k + 1) * 128], pw[:, gi, kc, :],
                                                          start=(kc == 0), stop=(kc == 1)), reads=[PT[gi], PWB], writes=[PB])
                    C.op("dve", lambda E: E.tensor_tensor(tmp[hh][:], pt[:, :], scb[:, hh * 512:(hh + 1) * 512], op=ALU.mult), reads=[PB, SCB], writes=[TM[hh]])
                    C.op("pool", lambda E: E.tensor_tensor(xs[sl][:, k, hh * 512:(hh + 1) * 512], xs[sl][:, k, hh * 512:(hh + 1) * 512], tmp[hh][:], op=ALU.add),
                         reads=[TM[hh], XS[sl]], writes=[XS[sl]])
            C.dma("sp", xout_t[c], xs[sl][:], XS[sl], reads=[XS[sl]], writes=[xout_bufs[c * KT + k] for k in range(KT)])
        C.barrier()
MAGIC = 12582912.0
TWO_PI_S = 6.283185

def _sincos(C, turns, TB, shape, wk, WK, out_sin, out_cos, OB, eng="dve"):
    n, f = wk
    C.op(eng, lambda E: E.tensor_scalar(n, turns, MAGIC, MAGIC, op0=ALU.add, op1=ALU.subtract), reads=[TB], writes=[WK])
    C.op(eng, lambda E: E.tensor_tensor(f, turns, n, op=ALU.subtract), reads=[TB, WK], writes=[WK])
    C.op("act", lambda E: E.activation(out=out_sin, in_=f, func=AF.Sin, scale=TWO_PI_S), reads=[WK], writes=[OB])
    C.op(eng, lambda E: E.tensor_scalar(n, f, 0.25, None, op0=ALU.is_gt), reads=[WK], writes=[WK])
    C.op(eng, lambda E: E.scalar_tensor_tensor(out=f, in0=f, scalar=0.25, in1=n, op0=ALU.add, op1=ALU.subtract), reads=[WK], writes=[WK])
    C.op("act", lambda E: E.activation(out=out_cos, in_=f, func=AF.Sin, scale=TWO_PI_S), reads=[WK], writes=[OB])

def s5_phase(C, nc, T, li, xin, xin_bufs, xout, xout_bufs):
    j = li // 4
    TC = 256; NCH = S // TC; KT = TC // 128; NJ = S // 8
    INV2PI = float(1.0 / (2 * np.pi))
    with ExitStack() as es0:
        ps = PS(C, nc, es0)
        idf, idb, IB = make_ident(C, nc, es0)
        hTf = es0.enter_context(SB(nc, "hTf", [128, 8, S], BF16)); HTA = C.buf("hTfA")
        dcol, DCB = load_cols(C, nc, es0, ps, idf, IB, [(T["s5_d"][j:j + 1, :], 1)], "s5d")
        xin_t = xin.rearrange("(c k p) d -> c p k d", p=128, k=KT)
        xout_t = xout.rearrange("(c k p) d -> c p k d", p=128, k=KT)
        with ExitStack() as es:
            gbc, GB = load_bcast_row(C, nc, es, "gbc", T["norm_mix"][li], D)
            xs = [es.enter_context(SB(nc, f"xs{i}", [128, KT, D], F32)) for i in range(2)]; XS = C.bufs("xs", 2)
            hb = [es.enter_context(SB(nc, f"hb{i}", [128, D], BF16)) for i in range(2)]; HB = C.bufs("hb", 2)
            junk = es.enter_context(SB(nc, "junk", [128, D], F32)); JB = C.buf("junk")
            ss = [es.enter_context(SB(nc, f"ss{i}", [128, 4], F32)) for i in range(2)]; SSB = C.bufs("ss", 2)
            def loadA(c):
                C.dma("sp", xs[c % 2][:], xin_t[c], XS[c % 2], reads=[xin_bufs[c * KT + k] for k in range(KT)], writes=[XS[c % 2]])
            loadA(0)
            for c in range(NCH):
                sl = c % 2
                if c + 1 < NCH:
                    loadA(c + 1)
                for k in range(KT):
                    i = (c * KT + k) % 2
                    norm_tile(C, nc, xs[sl][:, k, :], XS[sl], gbc, GB, hb[i], HB[i], junk, JB, ss[i], SSB[i])
                    t0 = c * TC + k * 128
                    transpose_tile(C, nc, ps, hb[i], HB[i], idb, IB, hTf[:, :, t0:t0 + 128], HTA)
            C.barrier()
        with ExitStack() as es:
            def sb(name, shape, dt=F32):
                return es.enter_context(SB(nc, name, shape, dt))
            es.enter_context(nc.allow_non_contiguous_dma(reason="small parameter loads"))
            mask = sb("mask", [128, 8, 16]); MKB = C.buf("mask")
            mi = sb("maski", [128, 8, 16], I32)
            C.op("pool", lambda E: E.iota(mi[:], pattern=[[-16, 8], [0, 16]], base=0, channel_multiplier=1), writes=[MKB])
            m2 = sb("mask2", [128, 8, 16])
            C.op("dve", lambda E: E.tensor_scalar(mask[:], mi[:], 0.0, None, op0=ALU.is_ge), reads=[MKB], writes=[MKB])
            C.op("dve", lambda E: E.tensor_scalar(m2[:], mi[:], 15.0, None, op0=ALU.is_le), reads=[MKB], writes=[MKB])
            C.op("dve", lambda E: E.tensor_tensor(mask[:], mask[:], m2[:], op=ALU.mult), reads=[MKB], writes=[MKB])
            iot = sb("iot1", [128, NJ]); IOB = C.buf("iot1")
            C.op("pool", lambda E: E.iota(iot[:], pattern=[[1, NJ]], base=1, channel_multiplier=0, allow_small_or_imprecise_dtypes=True), writes=[IOB])
            lst = sb("lst", [128, 8]); LSB = C.buf("lst")
            lrr = sb("lrr", [128, 2, 8, 64]); LRB = C.buf("lrr")
            braw = sb("braw", [64, 2, 64, 16]); BRB = C.buf("braw")
            craw = sb("craw", [128, 2, 8, 64]); CRB = C.buf("craw")
            ls_v = T["s5_log_step"][j].rearrange("(dc g) -> g dc", g=8)
            lam = [T["s5_lam_re"][j], T["s5_lam_im"][j]]
            bb = [T["s5_b_re"][j], T["s5_b_im"][j]]
            cm = [T["s5_c_re"][j], T["s5_c_im"][j]]
            for ri in range(2):
                C.dma("sp", braw[:, ri, :, :], bb[ri].rearrange("g p c -> p g c"), BRB, writes=[BRB], par=True)
                C.dma("sp", craw[:, ri, :, :], cm[ri].rearrange("(dc g) c p -> (g c) dc p", g=8), CRB, writes=[CRB], par=True)
            for g8 in range(8):
                pr = slice(g8 * 16, (g8 + 1) * 16)
                C.dma("sp", lst[pr, :], ls_v[g8].partition_broadcast(16), LSB, writes=[LSB], par=True)
                for ri in range(2):
                    lv = lam[ri].rearrange("(dc g) p -> g dc p", g=8)
                    C.dma("sp", lrr[pr, ri, :, :], lv[g8].partition_broadcast(16), LRB, writes=[LRB], par=True)
            step = sb("step", [128, 8]); STB = C.buf("step")
            C.op("act", lambda E: E.activation(out=step[:], in_=lst[:], func=AF.Exp), reads=[LSB], writes=[STB])
            kst = sb("kst", [128, 8, 9]); KSB = C.buf("kst")
            for k in range(9):
                C.op("dve", lambda E: E.tensor_scalar(kst[:, :, k], step[:], float(k), None, op0=ALU.mult), reads=[STB], writes=[KSB])
            c16 = sb("c16", [128, 1]); C16B = C.buf("c16")
            C.op("dve", lambda E: E.memset(c16[:], 1.0 / 16), writes=[C16B])
            PVm = sb("PVm", [128, 2, 8, 4, 128], BF16); PVB = C.buf("PVm")
            QFm = sb("QFm", [128, 2, 9, 4, 128], BF16); QFB = C.buf("QFm")
            PVT = sb("PVT", [128, 2, 4, 128], BF16); PTB = C.buf("PVT")
            BDm = sb("BDm", [128, 8, 128], BF16); BDB = C.buf("BDm")
            cosT = sb("cosT", [128, 4, NJ]); sinT = sb("sinT", [128, 4, NJ]); TBB = C.buf("tabs")
            Xb = sb("Xb", [128, 2, 4, NJ + 1], BF16); XBB = C.bufs("Xb", 4)
            C.op("dve", lambda E: E.memset(Xb[:], 0.0), writes=XBB)
            w8 = [sb(f"w8_{i}", [128, 9, 64]) for i in range(8)]; W8 = C.bufs("w8", 8)
            w1 = [sb(f"w1_{i}", [128, 64]) for i in range(8)]; W1 = C.bufs("w1", 8)
            bcd = sb("bcd", [128, 2, 64]); BCB = C.buf("bcd")
            yex = [sb(f"yex{i}", [128, 2, 4, 128], BF16) for i in range(2)]; YEX = C.bufs("yex", 2)
            yr = sb("yr", [128, 2, 4, 128]); YRB = C.buf("yr")
            wt = [sb(f"wt_{i}", [128, NJ]) for i in range(8)]; WT = C.bufs("wt", 8)
            r8 = sb("r8", [128, 2, 4]); R8B = C.buf("r8")

            def tt(eng, out, a, b, op, reads, writes):
                C.op(eng, lambda E: E.tensor_tensor(out, a, b, op=op), reads=reads, writes=writes)

            for dc in range(8):
                for ri in range(2):
                    pt, PB = ps.next()
                    C.op("pe", lambda E: E.transpose(pt[:, 0:64], braw[:, ri, dc * 8:(dc + 1) * 8, :].rearrange("p g c -> p (g c)"), idf[0:64, 0:64]),
                         reads=[BRB, IB], writes=[PB])
                    C.op("act", lambda E: E.copy(bcd[:, ri, :], pt[:, 0:64]), reads=[PB], writes=[BCB])
                kb = kst[:, dc, :].unsqueeze(2).to_broadcast([128, 9, 64])
                tt("dve", w8[0][:], kb, lrr[:, 0, dc, :].unsqueeze(1).to_broadcast([128, 9, 64]), ALU.mult, [KSB, LRB], [W8[0]])
                tt("dve", w8[1][:], kb, lrr[:, 1, dc, :].unsqueeze(1).to_broadcast([128, 9, 64]), ALU.mult, [KSB, LRB], [W8[1]])
                C.op("act", lambda E: E.activation(out=w8[0][:], in_=w8[0][:], func=AF.Exp), reads=[W8[0]], writes=[W8[0]])
                C.op("dve", lambda E: E.tensor_scalar(w8[1][:], w8[1][:], INV2PI, None, op0=ALU.mult), reads=[W8[1]], writes=[W8[1]])
                _sincos(C, w8[1][:], W8[1], None, (w8[2][:], w8[3][:]), W8[2], w8[4][:], w8[5][:], W8[4])
                C.op("dve", lambda E: E.tensor_copy(w1[6][:], w8[0][:, 8, :]), reads=[W8[0]], writes=[W1[6]])
                C.op("dve", lambda E: E.tensor_scalar(w1[7][:], w8[1][:, 8, :], MAGIC, MAGIC, op0=ALU.add, op1=ALU.subtract), reads=[W8[1]], writes=[W1[7]])
                tt("dve", w1[7][:], w8[1][:, 8, :], w1[7][:], ALU.subtract, [W8[1], W1[7]], [W1[7]])
                tt("dve", w8[5][:], w8[5][:], w8[0][:], ALU.mult, [W8[4], W8[0]], [W8[4]])
                tt("dve", w8[4][:], w8[4][:], w8[0][:], ALU.mult, [W8[4], W8[0]], [W8[4]])
                Ere, Eim = w8[5], w8[4]
                mk4 = mask[:].unsqueeze(1).to_broadcast([128, 4, 8, 16])
                for q in range(2):
                    src = w1[6 + q][:].rearrange("p (pb r) -> p pb r", pb=4).unsqueeze(2).to_broadcast([128, 4, 8, 16])
                    tt("pool", yr[:, q, :, :].rearrange("p pb (g r) -> p pb g r", g=8), src, mk4, ALU.mult, [W1[6 + q], MKB], [YRB])
                pt, PB = ps.next()
                for q in range(2):
                    for pb in range(4):
                        C.op("pe", lambda E: E.matmul(pt[:, q * 4 + pb:q * 4 + pb + 1], yr[:, q, pb, :], c16[:], start=True, stop=True), reads=[YRB, C16B], writes=[PB])
                C.op("act", lambda E: E.copy(r8[:].rearrange("p a b -> p (a b)"), pt[:, 0:8]), reads=[PB], writes=[R8B])
                lre, lim = lrr[:, 0, dc, :], lrr[:, 1, dc, :]
                C.op("dve", lambda E: E.tensor_scalar(w1[0][:], Ere[:, 1, :], -1.0, None, op0=ALU.add), reads=[W8[4]], writes=[W1[0]])
                ni = Eim[:, 1, :]
                tt("dve", w1[1][:], lre, lre, ALU.mult, [LRB], [W1[1]])
                tt("dve", w1[2][:], lim, lim, ALU.mult, [LRB], [W1[2]])
                tt("dve", w1[1][:], w1[1][:], w1[2][:], ALU.add, [W1[1], W1[2]], [W1[1]])
                C.op("dve", lambda E: E.reciprocal(w1[1][:], w1[1][:]), reads=[W1[1]], writes=[W1[1]])
                tt("dve", w1[2][:], w1[0][:], lre, ALU.mult, [W1[0], LRB], [W1[2]])
                tt("dve", w1[3][:], ni, lim, ALU.mult, [W8[4], LRB], [W1[3]])
                tt("dve", w1[2][:], w1[2][:], w1[3][:], ALU.add, [W1[2], W1[3]], [W1[2]])
                tt("dve", w1[2][:], w1[2][:], w1[1][:], ALU.mult, [W1[2], W1[1]], [W1[2]])
                tt("dve", w1[3][:], ni, lre, ALU.mult, [W8[4], LRB], [W1[3]])
                tt("dve", w1[4][:], w1[0][:], lim, ALU.mult, [W1[0], LRB], [W1[4]])
                tt("dve", w1[3][:], w1[3][:], w1[4][:], ALU.subtract, [W1[3], W1[4]], [W1[3]])
                tt("dve", w1[3][:], w1[3][:], w1[1][:], ALU.mult, [W1[3], W1[1]], [W1[3]])
                bre, bim = bcd[:, 0, :], bcd[:, 1, :]
                tt("dve", w1[4][:], w1[2][:], bre, ALU.mult, [W1[2], BCB], [W1[4]])
                tt("dve", w1[5][:], w1[3][:], bim, ALU.mult, [W1[3], BCB], [W1[5]])
                tt("dve", w1[4][:], w1[4][:], w1[5][:], ALU.subtract, [W1[4], W1[5]], [W1[4]])
                tt("dve", w1[5][:], w1[2][:], bim, ALU.mult, [W1[2], BCB], [W1[5]])
                tt("dve", w1[0][:], w1[3][:], bre, ALU.mult, [W1[3], BCB], [W1[0]])
                tt("dve", w1[5][:], w1[5][:], w1[0][:], ALU.add, [W1[5], W1[0]], [W1[5]])
                Bre_b = w1[4][:].unsqueeze(1).to_broadcast([128, 9, 64]); Bim_b = w1[5][:].unsqueeze(1).to_broadcast([128, 9, 64])
                tt("dve", w8[0][:], Ere[:], Bre_b, ALU.mult, [W8[4], W1[4]], [W8[0]])
                tt("dve", w8[1][:], Eim[:], Bim_b, ALU.mult, [W8[4], W1[5]], [W8[1]])
                tt("dve", w8[0][:], w8[0][:], w8[1][:], ALU.subtract, [W8[0], W8[1]], [W8[0]])
                tt("dve", w8[1][:], Ere[:], Bim_b, ALU.mult, [W8[4], W1[5]], [W8[1]])
                tt("dve", w8[2][:], Eim[:], Bre_b, ALU.mult, [W8[4], W1[4]], [W8[2]])
                tt("dve", w8[1][:], w8[1][:], w8[2][:], ALU.add, [W8[1], W8[2]], [W8[1]])
                for s in range(8):
                    k = 7 - s
                    for ri in range(2):
                        src = w8[ri][:, k, :].rearrange("p (pb r) -> p pb r", pb=4).unsqueeze(2).to_broadcast([128, 4, 8, 16])
                        dst = PVm[:, ri, s, :, :].rearrange("p pb (g r) -> p pb g r", g=8)
                        tt("pool", dst, src, mk4, ALU.mult, [W8[ri], MKB], [PVB])
                cre = craw[:, 0, dc, :].unsqueeze(1).to_broadcast([128, 9, 64]); cim = craw[:, 1, dc, :].unsqueeze(1).to_broadcast([128, 9, 64])
                tt("dve", w8[2][:], cre, Ere[:], ALU.mult, [CRB, W8[4]], [W8[2]])
                tt("dve", w8[3][:], cim, Eim[:], ALU.mult, [CRB, W8[4]], [W8[3]])
                tt("dve", w8[2][:], w8[2][:], w8[3][:], ALU.subtract, [W8[2], W8[3]], [W8[2]])
                tt("dve", w8[3][:], cre, Eim[:], ALU.mult, [CRB, W8[4]], [W8[3]])
                tt("dve", w8[6][:], cim, Ere[:], ALU.mult, [CRB, W8[4]], [W8[6]])
                C.op("dve", lambda E: E.scalar_tensor_tensor(out=w8[3][:], in0=w8[3][:], scalar=-1.0, in1=w8[6][:], op0=ALU.mult, op1=ALU.subtract),
                     reads=[W8[3], W8[6]], writes=[W8[3]])
                for b in range(9):
                    yi = b % 2
                    for ri in range(2):
                        src = w8[2 + ri][:, b, :].rearrange("p (pb r) -> p pb r", pb=4).unsqueeze(2).to_broadcast([128, 4, 8, 16])
                        dst = yex[yi][:, ri, :, :].rearrange("p pb (g r) -> p pb g r", g=8)
                        tt("pool", dst, src, mk4, ALU.mult, [W8[2 + ri], MKB], [YEX[yi]])
                    pt, PB = ps.next()
                    ptb = pt[:].bitcast(BF16).rearrange("p (a b c) -> p a b c", a=2, b=4)
                    for ri in range(2):
                        for pb in range(4):
                            C.op("pe", lambda E: E.transpose(ptb[:, ri, pb, :], yex[yi][:, ri, pb, :], idb[:]), reads=[YEX[yi], IB], writes=[PB])
                    C.op("act", lambda E: E.copy(QFm[:, :, b, :, :], ptb), reads=[PB], writes=[QFB])
                for ri in range(2):
                    pt, PB = ps.next()
                    ptb = pt[:].bitcast(BF16).rearrange("p (a b) -> p a b", a=8)
                    for pb in range(4):
                        C.op("pe", lambda E: E.transpose(ptb[:, pb, :], PVm[:, ri, 7, pb, :], idb[:]), reads=[PVB, IB], writes=[PB])
                    C.op("act", lambda E: E.copy(PVT[:, ri, :, :], ptb[:, 0:4, :]), reads=[PB], writes=[PTB])
                for th in range(2):
                    pt, PB = ps.next()
                    for t4 in range(4):
                        tau = th * 4 + t4
                        n = 0
                        for ri in range(2):
                            for pb in range(4):
                                C.op("pe", lambda E: E.matmul(pt[:, t4 * 128:(t4 + 1) * 128], PVT[:, ri, pb, :], QFm[:, ri, tau, pb, :],
                                                              start=(n == 0), stop=(n == 7)), reads=[PTB, QFB], writes=[PB])
                                n += 1
                    C.op("act", lambda E: E.copy(BDm[:, th * 4:(th + 1) * 4, :], pt[:].rearrange("p (a b) -> p a b", a=4)), reads=[PB], writes=[BDB])
                for pb in range(4):
                    C.op("dve", lambda E: E.tensor_scalar(wt[0][:], iot[:], r8[:, 1, pb:pb + 1], None, op0=ALU.mult), reads=[IOB, R8B], writes=[WT[0]])
                    _sincos(C, wt[0][:], WT[0], None, (wt[1][:], wt[2][:]), WT[1], sinT[:, pb, :], cosT[:, pb, :], TBB)
                for pb in range(4):
                    pr_, PR = ps.next(); pi_, PI = ps.next()
                    for ri, (pt, PB) in enumerate(((pr_, PR), (pi_, PI))):
                        for s in range(8):
                            C.op("pe", lambda E: E.matmul(pt[:, :], PVm[:, ri, s, pb, :], hTf[:, dc, s::8], start=(s == 0), stop=(s == 7)),
                                 reads=[PVB, HTA], writes=[PB])
                    cs, sn = cosT[:, pb, :], sinT[:, pb, :]
                    tt("dve", wt[0][:], pr_[:, :], cs, ALU.mult, [PR, TBB], [WT[0]])
                    tt("dve", wt[1][:], pi_[:, :], sn, ALU.mult, [PI, TBB], [WT[1]])
                    tt("dve", wt[0][:], wt[0][:], wt[1][:], ALU.add, [WT[0], WT[1]], [WT[0]])
                    tt("dve", wt[1][:], pi_[:, :], cs, ALU.mult, [PI, TBB], [WT[1]])
                    tt("dve", wt[2][:], pr_[:, :], sn, ALU.mult, [PR, TBB], [WT[2]])
                    tt("dve", wt[1][:], wt[1][:], wt[2][:], ALU.subtract, [WT[1], WT[2]], [WT[1]])
                    rho = r8[:, 0, pb:pb + 1].to_broadcast([128, NJ])
                    C.op("dve", lambda E: E.tensor_tensor_scan(wt[3][:], rho, wt[0][:], 0.0, op0=ALU.mult, op1=ALU.add), reads=[R8B, WT[0]], writes=[WT[3]])
                    C.op("dve", lambda E: E.tensor_tensor_scan(wt[4][:], rho, wt[1][:], 0.0, op0=ALU.mult, op1=ALU.add), reads=[R8B, WT[1]], writes=[WT[4]])
                    tt("dve", wt[0][:], wt[3][:], cs, ALU.mult, [WT[3], TBB], [WT[0]])
                    tt("dve", wt[1][:], wt[4][:], sn, ALU.mult, [WT[4], TBB], [WT[1]])
                    tt("dve", Xb[:, 0, pb, 1:NJ + 1], wt[0][:], wt[1][:], ALU.subtract, [WT[0], WT[1]], [XBB[pb]])
                    tt("dve", wt[0][:], wt[4][:], cs, ALU.mult, [WT[4], TBB], [WT[0]])
                    tt("dve", wt[1][:], wt[3][:], sn, ALU.mult, [WT[3], TBB], [WT[1]])
                    tt("dve", Xb[:, 1, pb, 1:NJ + 1], wt[0][:], wt[1][:], ALU.add, [WT[0], WT[1]], [XBB[pb]])
                for tp in range(7, -1, -1):
                    pt, PB = ps.next()
                    nmm = (tp + 1) + 8; n = 0
                    for s in range(tp + 1):
                        C.op("pe", lambda E: E.matmul(pt[:, :], BDm[:, tp - s, :], hTf[:, dc, s::8], start=(n == 0), stop=(n == nmm - 1)),
                             reads=[BDB, HTA], writes=[PB]); n += 1
                    for ri in range(2):
                        for pb in range(4):
                            C.op("pe", lambda E: E.matmul(pt[:, :], QFm[:, ri, tp + 1, pb, :], Xb[:, ri, pb, 0:NJ], start=(n == 0), stop=(n == nmm - 1)),
                                 reads=[QFB, XBB[pb]], writes=[PB]); n += 1
                    hv = hTf[:, dc, tp::8]
                    C.op("dve", lambda E: E.scalar_tensor_tensor(out=wt[5][:], in0=hv, scalar=dcol[:, dc, 0:1], in1=pt[:, :], op0=ALU.mult, op1=ALU.add),
                         reads=[HTA, DCB, PB], writes=[WT[5]])
                    tt("pool", wt[6][:], wt[5][:], wt[5][:], ALU.mult, [WT[5]], [WT[6]])
                    C.op("pool", lambda E: E.tensor_scalar(wt[6][:], wt[6][:], 0.044715, 1.0, op0=ALU.mult, op1=ALU.add), reads=[WT[6]], writes=[WT[6]])
                    tt("pool", wt[6][:], wt[6][:], wt[5][:], ALU.mult, [WT[6], WT[5]], [WT[6]])
                    C.op("act", lambda E: E.activation(out=wt[7][:], in_=wt[6][:], func=AF.Sigmoid, scale=1.5957691216), reads=[WT[6]], writes=[WT[7]])
                    tt("dve", hv, wt[5][:], wt[7][:], ALU.mult, [WT[5], WT[7]], [HTA])
            C.barrier()
        with ExitStack() as es:
            wg = es.enter_context(SB(nc, "wglu", [128, 8, 2 * D], BF16)); WGB = C.bufs("wglu", 8)
            wsrc = T["s5_w_glu"][j].rearrange("(kc p) f -> p kc f", p=128)
            for kc in range(8):
                C.dma("pool", wg[:, kc, :], wsrc[:, kc, :], WGB[kc], writes=[WGB[kc]])
            bg, BGB = load_bcast_row(C, nc, es, "bglu", T["s5_b_glu"][j], 2 * D)
            xs = [es.enter_context(SB(nc, f"xs{i}", [128, KT, D], F32)) for i in range(2)]; XS = C.bufs("xs", 2)
            ta = [es.enter_context(SB(nc, f"ta{i}", [128, 512], F32)) for i in range(2)]; TA = C.bufs("ta", 2)
            tg = [es.enter_context(SB(nc, f"tg{i}", [128, 512], F32)) for i in range(2)]; TG = C.bufs("tg", 2)
            HTC = C.buf("hTfC")
            def loadC(c):
                C.dma("sp", xs[c % 2][:], xin_t[c], XS[c % 2], reads=[xin_bufs[c * KT + k] for k in range(KT)], writes=[XS[c % 2]])
            loadC(0)
            for c in range(NCH):
                sl = c % 2
                if c + 1 < NCH:
                    loadC(c + 1)
                for k in range(KT):
                    t0 = c * TC + k * 128
                    for dh in range(2):
                        pa, PA = ps.next(); pg, PG = ps.next()
                        for half, (pt, PB) in enumerate(((pa, PA), (pg, PG))):
                            col = half * D + dh * 512
                            for kc in range(8):
                                C.op("pe", lambda E: E.matmul(pt[:, :], hTf[:, kc, t0:t0 + 128], wg[:, kc, col:col + 512], start=(kc == 0), stop=(kc == 7)),
                                     reads=[HTC, WGB[kc]], writes=[PB])
                        i = dh
                        tt2 = lambda eng, out, a, b, op, r, w: C.op(eng, lambda E: E.tensor_tensor(out, a, b, op=op), reads=r, writes=w)
                        tt2("dve", tg[i][:], pg[:, :], bg[:, D + dh * 512:D + (dh + 1) * 512], ALU.add, [PG, BGB], [TG[i]])
                        C.op("act", lambda E: E.activation(out=tg[i][:], in_=tg[i][:], func=AF.Sigmoid), reads=[TG[i]], writes=[TG[i]])
                        tt2("dve", ta[i][:], pa[:, :], bg[:, dh * 512:(dh + 1) * 512], ALU.add, [PA, BGB], [TA[i]])
                        tt2("pool", ta[i][:], ta[i][:], tg[i][:], ALU.mult, [TA[i], TG[i]], [TA[i]])
                        tt2("pool", xs[sl][:, k, dh * 512:(dh + 1) * 512], xs[sl][:, k, dh * 512:(dh + 1) * 512], ta[i][:], ALU.add, [TA[i], XS[sl]], [XS[sl]])
                C.dma("sp", xout_t[c], xs[sl][:], XS[sl], reads=[XS[sl]], writes=[xout_bufs[c * KT + k] for k in range(KT)])
            C.barrier()
NEG = -30000.0

def nsa_phase(C, nc, T, li, xin, xin_bufs, xout, xout_bufs):
    j = li // 4
    TC = 256; NCH = S // TC; KT = TC // 128
    H, G, R, DH = 16, 4, 4, 64
    NPROJ = 2608; QD = 1024; KVD = 1536
    NCMP = 255; NSEL = 64
    qT_d = nc.dram_tensor(f"nsa_qT_{li}", [H * DH, S], BF16).ap()
    kT_d = nc.dram_tensor(f"nsa_kT_{li}", [12 * DH, S], BF16).ap()
    vT0_d = nc.dram_tensor(f"nsa_vT0_{li}", [4 * DH, S], BF16).ap()
    v_d = nc.dram_tensor(f"nsa_v_{li}", [S, 12 * DH], BF16).ap()
    QTD = C.bufs("qTd", NT); KTD = C.bufs("kTd", NT); VTD = C.bufs("vTd", NT); VD = C.bufs("vd", NT)
    xin_t = xin.rearrange("(c k p) d -> c p k d", p=128, k=KT)
    xout_t = xout.rearrange("(c k p) d -> c p k d", p=128, k=KT)

    def tt(eng, out, a, b, op, reads, writes):
        C.op(eng, lambda E: E.tensor_tensor(out, a, b, op=op), reads=reads, writes=writes)

    with ExitStack() as es0:
        es0.enter_context(nc.allow_non_contiguous_dma(reason="small parameter / strided loads"))
        ps = PS(C, nc, es0)
        idf, idb, IB = make_ident(C, nc, es0)
        gates = es0.enter_context(SB(nc, "gates", [128, NT, 48], F32)); GTB = C.buf("gates")
        with ExitStack() as es:
            def sb(name, shape, dt=F32):
                return es.enter_context(SB(nc, name, shape, dt))
            gbc, GB = load_bcast_row(C, nc, es, "gbc", T["norm_mix"][li], D)
            qg, QGB = load_bcast_row(C, nc, es, "qgain", T["nsa_q_gain"][j], DH)
            kg, KGB = load_bcast_row(C, nc, es, "kgain", T["nsa_k_gain"][j].rearrange("a b -> (a b)"), 3 * DH)
            win = sb("nwin", [128, 8, NPROJ], BF16); WIB = C.bufs("nwin", 8)
            wsrc = T["nsa_w_in"][j].rearrange("(kc p) f -> p kc f", p=128)
            for kc in range(8):
                C.dma("pool", win[:, kc, :], wsrc[:, kc, :], WIB[kc], writes=[WIB[kc]])
            posi = sb("posi", [128, NT], I32); POB = C.buf("posi")
            C.dma("sp", posi[:], T["positions"].rearrange("(k p) -> p k", p=128), POB, writes=[POB])
            posf = sb("posf", [128, NT]);
            C.op("dve", lambda E: E.tensor_copy(posf[:], posi[:]), reads=[POB], writes=[POB])
            trn = sb("trn", [128, NT, 8]); TRB = C.buf("trn")
            for i in range(8):
                invf = float(500000.0 ** (-i / 8.0) / (2 * np.pi))
                C.op("dve", lambda E: E.tensor_scalar(trn[:, :, i], posf[:], invf, None, op0=ALU.mult), reads=[POB], writes=[TRB])
            rc = sb("ropec", [128, NT, 8]); rs = sb("ropes", [128, NT, 8]); RPB = C.buf("rope")
            wk1 = sb("rwk1", [128, NT, 8]); wk2 = sb("rwk2", [128, NT, 8]); RWB = C.buf("rwk")
            _sincos(C, trn[:], TRB, None, (wk1[:], wk2[:]), RWB, rs[:], rc[:], RPB)
            xs = [sb(f"xs{i}", [128, KT, D]) for i in range(2)]; XS = C.bufs("xs", 2)
            hT = [sb(f"hT{i}", [128, 8, TC], BF16) for i in range(2)]; HT = C.bufs("hT", 2)
            hb = [sb(f"hb{i}", [128, D], BF16) for i in range(2)]; HB = C.bufs("hb", 2)
            junk = sb("junk", [128, D]); JB = C.buf("junk")
            ss = [sb(f"ss{i}", [128, 4]) for i in range(2)]; SSB = C.bufs("ss", 2)
            pjL = [sb(f"pj{i}", [128, NPROJ]) for i in range(2)]; PJBL = C.bufs("pj", 2)
            sqL = [sb(f"sq{i}", [128, 28 * DH]) for i in range(2)]; SQBL = C.bufs("sq", 2)
            stL = [sb(f"nst{i}", [128, 4, 28]) for i in range(2)]; STBL = C.bufs("nst", 2)
            qkL = [sb(f"qk{i}", [128, 28, DH]) for i in range(2)]; QKBL = C.bufs("qk", 2)
            rtL = [[sb(f"rt{p}_{i}", [128, 28, 8]) for i in range(4)] for p in range(2)]; RTBL = [C.bufs(f"rt{p}_", 4) for p in range(2)]
            qkbL = [sb(f"qkb{i}", [128, 28 * DH + 4 * DH], BF16) for i in range(2)]; QKBBL = C.bufs("qkb", 2)
            vbL = [sb(f"vb{i}", [128, 12 * DH], BF16) for i in range(2)]; VBBL = C.bufs("vb", 2)
            stg = [sb(f"stg{i}", [128, 16, 128], BF16) for i in range(2)]; STG = C.bufs("stg", 2)
            def loadA(c):
                C.dma("sp", xs[c % 2][:], xin_t[c], XS[c % 2], reads=[xin_bufs[c * KT + k] for k in range(KT)], writes=[XS[c % 2]])
            def prepA(c):
                sl = c % 2
                for k in range(KT):
                    ti = c * KT + k; i = ti % 2
                    norm_tile(C, nc, xs[sl][:, k, :], XS[sl], gbc, GB, hb[i], HB[i], junk, JB, ss[i], SSB[i])
                    transpose_tile(C, nc, ps, hb[i], HB[i], idb, IB, hT[sl][:, :, k * 128:(k + 1) * 128], HT[sl])

            def s1(ti):
                if True:
                    c = ti // KT; k = ti % KT; sl = c % 2
                    pp = ti % 2
                    pj, PJB, sq, SQB, st, STB, qk, QKB, rt, RTB, qkb, QKBB, vb, VBB = (pjL[pp], PJBL[pp], sqL[pp], SQBL[pp], stL[pp], STBL[pp], qkL[pp], QKBL[pp],
                                                                                     rtL[pp], RTBL[pp], qkbL[pp], QKBBL[pp], vbL[pp], VBBL[pp])
                    pv = pj[:, QD:QD + KVD].rearrange("p (b kv g d) -> p b kv g d", b=3, kv=2, g=4)
                    for cb in range(6):
                        c0 = cb * 512; cw = min(512, NPROJ - c0)
                        pt, PB = ps.next()
                        for kc in range(8):
                            C.op("pe", lambda E: E.matmul(pt[:, 0:cw], hT[sl][:, kc, k * 128:(k + 1) * 128], win[:, kc, c0:c0 + cw], start=(kc == 0), stop=(kc == 7)),
                                 reads=[HT[sl], WIB[kc]], writes=[PB])
                        C.op("act", lambda E: E.copy(pj[:, c0:c0 + cw], pt[:, 0:cw]), reads=[PB], writes=[PJB])
                    C.op("act", lambda E: E.activation(out=gates[:, ti, :], in_=pj[:, QD + KVD:NPROJ], func=AF.Sigmoid), reads=[PJB], writes=[GTB])
                    pv = pj[:, QD:QD + KVD].rearrange("p (b kv g d) -> p b kv g d", b=3, kv=2, g=4)
                    C.op("pool", lambda E: E.tensor_copy(vb[:].rearrange("p (b g d) -> p b g d", b=3, g=4), pv[:, :, 1, :, :]), reads=[PJB], writes=[VBB])
                    C.dma("sp", v_d[ti * 128:(ti + 1) * 128, :], vb[:], VBB, reads=[VBB], writes=[VD[ti]])
                    C.op("pool", lambda E: E.tensor_copy(qk[:, 0:16, :], pj[:, 0:QD].rearrange("p (h d) -> p h d", h=16)), reads=[PJB], writes=[QKB])
                    C.op("pool", lambda E: E.tensor_copy(qk[:, 16:28, :].rearrange("p (b g) d -> p b g d", b=3), pv[:, :, 0, :, :]), reads=[PJB], writes=[QKB])
                    tt("pool", sq[:].rearrange("p (h d) -> p h d", h=28), qk[:], qk[:], ALU.mult, [QKB], [SQB])

            def s2(ti):
                if True:
                    c = ti // KT; k = ti % KT; sl = c % 2
                    pp = ti % 2
                    pj, PJB, sq, SQB, st, STB, qk, QKB, rt, RTB, qkb, QKBB, vb, VBB = (pjL[pp], PJBL[pp], sqL[pp], SQBL[pp], stL[pp], STBL[pp], qkL[pp], QKBL[pp],
                                                                                     rtL[pp], RTBL[pp], qkbL[pp], QKBBL[pp], vbL[pp], VBBL[pp])
                    pv = pj[:, QD:QD + KVD].rearrange("p (b kv g d) -> p b kv g d", b=3, kv=2, g=4)
                    C.op("dve", lambda E: E.reduce_sum(st[:, 0, :], sq[:].rearrange("p (h d) -> p h d", h=28), axis=AX.X), reads=[SQB], writes=[STB])
                    C.op("dve", lambda E: E.tensor_scalar(st[:, 1, :], st[:, 0, :], 1.0 / DH, RMS_EPS, op0=ALU.mult, op1=ALU.add), reads=[STB], writes=[STB])
                    C.op("act", lambda E: E.activation(out=st[:, 2, :], in_=st[:, 1, :], func=AF.Sqrt), reads=[STB], writes=[STB])
                    C.op("dve", lambda E: E.reciprocal(st[:, 3, :], st[:, 2, :]), reads=[STB], writes=[STB])
                    C.op("dve", lambda E: E.tensor_scalar(st[:, 3, 0:16], st[:, 3, 0:16], 0.125, None, op0=ALU.mult), reads=[STB], writes=[STB])
                    tt("dve", qk[:], qk[:], st[:, 3, :].unsqueeze(2).to_broadcast([128, 28, DH]), ALU.mult, [QKB, STB], [QKB])
                    tt("dve", qk[:, 0:16, :], qk[:, 0:16, :], qg[:].unsqueeze(1).to_broadcast([128, 16, DH]), ALU.mult, [QKB, QGB], [QKB])
                    tt("dve", qk[:, 16:28, :].rearrange("p (b g) d -> p b g d", b=3), qk[:, 16:28, :].rearrange("p (b g) d -> p b g d", b=3),
                       kg[:].rearrange("p (b d) -> p b d", b=3).unsqueeze(2).to_broadcast([128, 3, 4, DH]), ALU.mult, [QKB, KGB], [QKB])
                    cosb = rc[:, ti, :].unsqueeze(1).to_broadcast([128, 28, 8]); sinb = rs[:, ti, :].unsqueeze(1).to_broadcast([128, 28, 8])
                    x1 = qk[:, :, 0:8]; x2 = qk[:, :, 8:16]
                    tt("dve", rt[0][:], x1, cosb, ALU.mult, [QKB, RPB], [RTB[0]])
                    tt("dve", rt[1][:], x2, sinb, ALU.mult, [QKB, RPB], [RTB[1]])
                    tt("pool", rt[2][:], x2, cosb, ALU.mult, [QKB, RPB], [RTB[2]])
                    tt("pool", rt[3][:], x1, sinb, ALU.mult, [QKB, RPB], [RTB[3]])
                    tt("dve", x1, rt[0][:], rt[1][:], ALU.subtract, [RTB[0], RTB[1]], [QKB])
                    tt("dve", x2, rt[2][:], rt[3][:], ALU.add, [RTB[2], RTB[3]], [QKB])

            def s3(ti):
                if True:
                    c = ti // KT; k = ti % KT; sl = c % 2
                    pp = ti % 2
                    pj, PJB, sq, SQB, st, STB, qk, QKB, rt, RTB, qkb, QKBB, vb, VBB = (pjL[pp], PJBL[pp], sqL[pp], SQBL[pp], stL[pp], STBL[pp], qkL[pp], QKBL[pp],
                                                                                     rtL[pp], RTBL[pp], qkbL[pp], QKBBL[pp], vbL[pp], VBBL[pp])
                    pv = pj[:, QD:QD + KVD].rearrange("p (b kv g d) -> p b kv g d", b=3, kv=2, g=4)
                    C.op("act", lambda E: E.copy(qkb[:, 0:28 * DH], qk[:].rearrange("p h d -> p (h d)")), reads=[QKB], writes=[QKBB])
                    C.op("act", lambda E: E.copy(qkb[:, 28 * DH:32 * DH].rearrange("p (g d) -> p g d", g=4), pv[:, 0, 1, :, :]), reads=[PJB], writes=[QKBB])
                    sg_ = ti % 2
                    for half in range(2):
                        pt, PB = ps.next()
                        ptb = pt[:].bitcast(BF16).rearrange("p (a b) -> p a b", a=8)
                        for a in range(8):
                            blk = half * 8 + a
                            C.op("pe", lambda E: E.transpose(ptb[:, a, :], qkb[:, blk * 128:(blk + 1) * 128], idb[:]), reads=[QKBB, IB], writes=[PB])
                        C.op("act" if half == 0 else "dve",
                             (lambda E: E.copy(stg[sg_][:, half * 8:(half + 1) * 8, :], ptb)) if half == 0 else (lambda E: E.tensor_copy(stg[sg_][:, half * 8:(half + 1) * 8, :], ptb)),
                             reads=[PB], writes=[STG[sg_]])
                    tsl = slice(ti * 128, (ti + 1) * 128)
                    C.dma("sp", qT_d[:, tsl].rearrange("(a p) t -> p a t", p=128), stg[sg_][:, 0:8, :], STG[sg_], reads=[STG[sg_]], writes=[QTD[ti]])
                    C.dma("sp", kT_d[:, tsl].rearrange("(a p) t -> p a t", p=128), stg[sg_][:, 8:14, :], STG[sg_], reads=[STG[sg_]], writes=[KTD[ti]])
                    C.dma("sp", vT0_d[:, tsl].rearrange("(a p) t -> p a t", p=128), stg[sg_][:, 14:16, :], STG[sg_], reads=[STG[sg_]], writes=[VTD[ti]])

            loadA(0)
            if NCH > 1:
                loadA(1)
            prepA(0)
            s1(0)
            for ti in range(NT):
                nxt = ti + 1
                if nxt < NT:
                    if nxt % KT == 0:
                        cn = nxt // KT
                        if cn + 1 < NCH:
                            loadA(cn + 1)
                        prepA(cn)
                    s1(nxt)
                s2(ti)
                s3(ti)
            C.barrier()
        import os
        if os.environ.get('NSA_STOP') == 'A':
            return
        o_all = es0.enter_context(SB(nc, "o_all", [128, NT, H * DH], BF16)); OAB = C.bufs("o_all", NT)
        with ExitStack() as es:
            def sb(name, shape, dt=F32):
                return es.enter_context(SB(nc, name, shape, dt))
            ps.reserve(4)
            ACC = [(ps.t[i], ps.b[i]) for i in range(4)]
            caus = sb("caus", [128, 4, 512], BF16); CAB = C.buf("caus")
            winm = sb("winm", [128, 4, 512], BF16)
            cz = sb("cz", [128, 512]); CZB = C.buf("cz")
            C.op("dve", lambda E: E.memset(cz[:], 0.0), writes=[CZB])
            ctmp = sb("ctmp", [128, 512])
            for d in range(4):
                C.op("pool", lambda E: E.affine_select(ctmp[:], cz[:], pattern=[[1, 512]], compare_op=ALU.is_ge, fill=NEG, base=-128 * d, channel_multiplier=-1),
                     reads=[CZB], writes=[CAB])
                C.op("dve", lambda E: E.tensor_copy(caus[:, d, :], ctmp[:]), reads=[CAB], writes=[CAB])
                C.op("dve", lambda E: E.tensor_scalar(winm[:, d, :], ctmp[:], -1.0, NEG, op0=ALU.mult, op1=ALU.add), reads=[CAB], writes=[CAB])
            mc = sb("mcmp", [128, 2, S], BF16); MCB = C.buf("mcmp")
            EXB = C.buf("expm")
            kTe = sb("kTe", [128, 2, S], BF16); KTEB = C.buf("kTe")
            with ExitStack() as est:
                z16 = est.enter_context(SB(nc, "z16", [128, S], BF16))
                o16 = est.enter_context(SB(nc, "o16", [128, S], BF16))
                e16 = est.enter_context(SB(nc, "e16", [128, S], BF16))
                C.op("dve", lambda E: E.memset(z16[:], 0.0), writes=[MCB])
                for nt_ in range(2):
                    C.op("pool", lambda E: E.affine_select(mc[:, nt_, :], z16[:], pattern=[[1, S]], compare_op=ALU.is_ge, fill=NEG, base=-31 - 16 * 128 * nt_, channel_multiplier=-16),
                         reads=[MCB], writes=[MCB])
                C.op("dve", lambda E: E.memset(o16[:], 1.0), writes=[EXB])
                for hf in range(2):
                    pr_ = slice(hf * 64, (hf + 1) * 64)
                    C.op("pool", lambda E: E.affine_select(e16[pr_, :], o16[pr_, :], pattern=[[1, S]], compare_op=ALU.is_ge, fill=0.0, base=0, channel_multiplier=-64), reads=[EXB], writes=[EXB])
                    C.op("pool", lambda E: E.affine_select(kTe[pr_, 1 - hf, :], e16[pr_, :], pattern=[[-1, S]], compare_op=ALU.is_ge, fill=0.0, base=63, channel_multiplier=64), reads=[EXB], writes=[EXB, KTEB])
                C.barrier()
            ov1 = sb("ov1", [128, 2, NSEL]); ov2 = sb("ov2", [128, 2, NSEL]); OVB = C.buf("ov")
            C.op("dve", lambda E: E.memset(ov1[:], 1.0), writes=[OVB])
            for nt_ in range(2):
                C.op("pool", lambda E: E.affine_select(ov2[:, nt_, :], ov1[:, nt_, :], pattern=[[-4, NSEL]], compare_op=ALU.is_ge, fill=0.0, base=1 + 128 * nt_, channel_multiplier=1), reads=[OVB], writes=[OVB])
                C.op("pool", lambda E: E.affine_select(ov1[:, nt_, :], ov2[:, nt_, :], pattern=[[4, NSEL]], compare_op=ALU.is_ge, fill=0.0, base=3 - 128 * nt_, channel_multiplier=-1), reads=[OVB], writes=[OVB])
            curt = sb("curt", [128, NT]); CUB = C.buf("curt")
            C.op("pool", lambda E: E.iota(curt[0:64, :], pattern=[[2, NT]], base=0, channel_multiplier=0, allow_small_or_imprecise_dtypes=True), writes=[CUB])
            C.op("pool", lambda E: E.iota(curt[64:128, :], pattern=[[2, NT]], base=1, channel_multiplier=0, allow_small_or_imprecise_dtypes=True), writes=[CUB])
            sidx = sb("sidx", [128, NSEL]);
            C.op("pool", lambda E: E.iota(sidx[:], pattern=[[1, NSEL]], base=0, channel_multiplier=0, allow_small_or_imprecise_dtypes=True), writes=[CUB])
            s0m = sb("s0m", [128, NSEL])
            C.op("dve", lambda E: E.tensor_scalar(s0m[:], sidx[:], 0.0, None, op0=ALU.is_equal), reads=[CUB], writes=[CUB])
            w1 = sb("cw1", [64, 2, 32, DH], BF16); W1B = C.buf("cw1")
            w2 = sb("cw2", [64, 2, DH], BF16); W2B = C.buf("cw2")
            for cc_ in range(2):
                C.dma("pool", w1[:, cc_, :, :], T["nsa_cmp_w1"][j][cc_].rearrange("(l d) e -> d l e", d=DH), W1B, writes=[W1B], par=True)
                C.dma("pool", w2[:, cc_, :], T["nsa_cmp_w2"][j][cc_], W2B, writes=[W2B], par=True)
            posT = sb("cposT", [64, 2, 32], BF16); PSTB = C.buf("cposT")
            C.dma("pool", posT[:], T["nsa_cmp_pos"][j].rearrange("c l d -> d c l"), PSTB, writes=[PSTB])
            b1c = sb("cb1", [64, 2]); B1B = C.buf("cb1")
            C.dma("sp", b1c[:], T["nsa_cmp_b1"][j].rearrange("c e -> e c"), B1B, writes=[B1B])
            cbias = sb("cbias", [64, 2]); CBB = C.buf("cbias")
            for cc_ in range(2):
                pt, PB = ps.next()
                for l in range(32):
                    C.op("pe", lambda E: E.matmul(pt[0:64, 0:1], w1[:, cc_, l, :], posT[:, cc_, l:l + 1], start=(l == 0), stop=(l == 31)), reads=[W1B, PSTB], writes=[PB])
                tt("dve", cbias[:, cc_:cc_ + 1], pt[0:64, 0:1], b1c[:, cc_:cc_ + 1], ALU.add, [PB, B1B], [CBB])
            qTg = sb("qTg", [128, 2, S], BF16); QGB_ = C.buf("qTg")
            kTg = sb("kTg", [128, 2, S], BF16); KGB_ = C.buf("kTg")
            vT0g = sb("vT0g", [64, S], BF16); V0B = C.buf("vT0g")
            qsL = [sb(f"qs{i}", [128, 512], BF16) for i in range(2)]; QSBL = C.bufs("qs", 2)
            vaug = sb("vaug", [128, 2, NT, DH + 1], BF16); VAB = C.buf("vaug")
            C.op("dve", lambda E: E.memset(vaug[:, :, :, DH:DH + 1], 1.0), writes=[VAB])
            kcT = sb("kcT", [128, 256], BF16); KCB = C.buf("kcT")
            vcx = sb("vcx", [128, 2, DH + 1 + NSEL], BF16); VCB = C.buf("vcx")
            C.op("dve", lambda E: E.memset(vcx[:], 0.0), writes=[VCB])
            for nt_ in range(2):
                C.op("dve", lambda E: E.tensor_copy(vcx[:, nt_, DH + 1:], ov1[:, nt_, :]), reads=[OVB], writes=[VCB])
                C.op("dve", lambda E: E.memset(vcx[:, nt_, DH:DH + 1], 1.0), writes=[VCB])
            hidT = sb("hidT", [64, 2, 256], BF16); HDB = C.buf("hidT")
            C.op("dve", lambda E: E.memset(hidT[:], 0.0), writes=[HDB])
            gw = [sb(f"gw{i}", [64, 256]) for i in range(3)]; GWB = C.bufs("gw", 3)
            pT = [sb(f"pT{i}", [128, 512], BF16) for i in range(3)]; PTB_ = C.bufs("pT", 3)
            impL = [sb(f"imp{i}", [128, 4, NSEL]) for i in range(2)]; IMBL = C.bufs("imp", 2)
            pTc = [sb(f"pTc{i}", [128, 512], BF16) for i in range(2)]; PTCB = C.bufs("pTc", 2)
            sc = [sb(f"sc{i}", [128, NSEL]) for i in range(4)]; SCB_ = C.bufs("sc", 4)
            m8 = sb("m8", [128, 16]); M8B = C.buf("m8")
            otmp = sb("otmp", [128, 4, DH]); OTB_ = C.buf("otmp")
            selbL = [sb(f"selb{i}", [128, 4, 2 * NSEL], BF16) for i in range(2)]; SLBL = C.bufs("selb", 2)
            selTL = [sb(f"selT{i}", [128, 512], BF16) for i in range(2)]; STB2L = C.bufs("selT", 2)
            rsumL = [sb(f"rsum{i}", [128, 4, 4, 3]) for i in range(2)]; RSBL = C.bufs("rsum", 2)
            ocmL = [sb(f"ocm{i}", [128, 4, 4, DH]) for i in range(2)]; OCBL = C.bufs("ocm", 2)
            for g in range(G):
                for hp in range(2):
                    r0 = (g * 4 + hp * 2) * DH
                    C.dma("sp", qTg[:, hp, :], qT_d[r0:r0 + 128, :], QGB_, reads=QTD, writes=[QGB_], par=True)
                for br in range(3):
                    r0 = (br * 4 + g) * DH
                    if br == 1:
                        C.dma("sp", kTe[0:64, 0, :], kT_d[r0:r0 + 64, :], KTEB, reads=KTD, writes=[KTEB], par=True)
                        C.dma("sp", kTe[64:128, 1, :], kT_d[r0:r0 + 64, :], KTEB, reads=KTD, writes=[KTEB], par=True)
                    else:
                        for cp in range(1 if br == 0 else 2):
                            C.dma("sp", kTg[cp * 64:(cp + 1) * 64, 0 if br == 0 else 1, :], kT_d[r0:r0 + 64, :], KGB_, reads=KTD, writes=[KGB_], par=True)
                C.dma("sp", vT0g[:], vT0_d[g * DH:(g + 1) * DH, :], V0B, reads=VTD, writes=[V0B])
                for bi, br in enumerate((1, 2)):
                    c0 = (br * 4 + g) * DH
                    C.dma("sp", vaug[:, bi, :, 0:DH], v_d[:, c0:c0 + DH].rearrange("(jt p) d -> p jt d", p=128), VAB, reads=VD, writes=[VAB], par=True)
                for cc_ in range(2):
                    src = kTg[0:64, 0, :] if cc_ == 0 else vT0g[:]
                    SRCB = KGB_ if cc_ == 0 else V0B
                    pt, PB = ps.next()
                    for l in range(32):
                        C.op("pe", lambda E: E.matmul(pt[0:64, 0:NCMP], w1[:, cc_, l, :], src[:, l:l + 16 * (NCMP - 1) + 1:16], start=(l == 0), stop=(l == 31)),
                             reads=[W1B, SRCB], writes=[PB])
                    C.op("act", lambda E: E.activation(out=gw[0][:, 0:NCMP], in_=pt[0:64, 0:NCMP], func=AF.Identity, bias=cbias[:, cc_:cc_ + 1]), reads=[PB, CBB], writes=[GWB[0]])
                    tt("dve", gw[1][:, 0:NCMP], gw[0][:, 0:NCMP], gw[0][:, 0:NCMP], ALU.mult, [GWB[0]], [GWB[1]])
                    C.op("dve", lambda E: E.tensor_scalar(gw[1][:, 0:NCMP], gw[1][:, 0:NCMP], 0.044715, 1.0, op0=ALU.mult, op1=ALU.add), reads=[GWB[1]], writes=[GWB[1]])
                    tt("dve", gw[1][:, 0:NCMP], gw[1][:, 0:NCMP], gw[0][:, 0:NCMP], ALU.mult, [GWB[1], GWB[0]], [GWB[1]])
                    C.op("act", lambda E: E.activation(out=gw[2][:, 0:NCMP], in_=gw[1][:, 0:NCMP], func=AF.Sigmoid, scale=1.5957691216), reads=[GWB[1]], writes=[GWB[2]])
                    tt("dve", hidT[:, cc_, 0:NCMP], gw[0][:, 0:NCMP], gw[2][:, 0:NCMP], ALU.mult, [GWB[0], GWB[2]], [HDB])
                pt, PB = ps.next()
                C.op("pe", lambda E: E.matmul(pt[0:64, 0:256], w2[:, 0, :], hidT[:, 0, :], start=True, stop=True), reads=[W2B, HDB], writes=[PB])
                C.op("act", lambda E: E.copy(kcT[0:64, :], pt[0:64, 0:256]), reads=[PB], writes=[KCB])
                pt, PB = ps.next()
                C.op("pe", lambda E: E.matmul(pt[64:128, 0:256], w2[:, 0, :], hidT[:, 0, :], start=True, stop=True, tile_position=(0, 64)), reads=[W2B, HDB], writes=[PB]) if False else None
                C.dma("sp", kcT[64:128, :], kcT[0:64, :], KCB, reads=[KCB], writes=[KCB])
                for nt_ in range(2):
                    pt, PB = ps.next()
                    C.op("pe", lambda E: E.matmul(pt[:, 0:DH], hidT[:, 1, nt_ * 128:(nt_ + 1) * 128], w2[:, 1, :], start=True, stop=True), reads=[HDB, W2B], writes=[PB])
                    C.op("act", lambda E: E.copy(vcx[:, nt_, 0:DH], pt[:, 0:DH]), reads=[PB], writes=[VCB])
                if os.environ.get('NSA_STOP') == 'B':
                    continue
                def cmp_sel(tc):
                    par = tc % 2
                    imp, IMB, rsum, RSB, ocm, OCB, selb, SLB = impL[par], IMBL[par], rsumL[par], RSBL[par], ocmL[par], OCBL[par], selbL[par], SLBL[par]
                    tq = slice(tc * 512, (tc + 1) * 512)
                    nmax = (tc * 512 + 511 - 31) // 16
                    nts = [0] if nmax < 128 else [0, 1]
                    for r in range(R):
                        hp, h2 = r // 2, r % 2
                        prt = slice(h2 * 64, (h2 + 1) * 64)
                        for nt_ in nts:
                            pt, PB = ps.next()
                            C.op("pe", lambda E: E.matmul(pt[:, :], kcT[prt, nt_ * 128:(nt_ + 1) * 128], qTg[prt, hp, tq], start=True, stop=False), reads=[KCB, QGB_], writes=[PB])
                            C.op("pe", lambda E: E.matmul(pt[:, :], idb[:], mc[:, nt_, tq], start=False, stop=True), reads=[IB, MCB], writes=[PB])
                            C.op("act", lambda E: E.activation(out=pTc[nt_][:], in_=pt[:, :], func=AF.Exp), reads=[PB], writes=[PTCB[nt_]])
                        for ts in range(4):
                            pt, PB = ps.next()
                            W_ = DH + 1 + NSEL
                            for ii, nt_ in enumerate(nts):
                                C.op("pe", lambda E: E.matmul(pt[:, 0:W_], pTc[nt_][:, ts * 128:(ts + 1) * 128], vcx[:, nt_, :], start=(ii == 0), stop=(ii == len(nts) - 1)),
                                     reads=[PTCB[nt_], VCB], writes=[PB])
                            ti = tc * 4 + ts
                            C.op("dve", lambda E: E.tensor_scalar(rsum[:, ts, r, 0:1], pt[:, DH:DH + 1], 1e-30, None, op0=ALU.max), reads=[PB], writes=[RSB])
                            C.op("dve", lambda E: E.reciprocal(rsum[:, ts, r, 0:1], rsum[:, ts, r, 0:1]), reads=[RSB], writes=[RSB])
                            if r == 0:
                                C.op("dve", lambda E: E.tensor_scalar(imp[:, ts, :], pt[:, DH + 1:W_], rsum[:, ts, r, 0:1], None, op0=ALU.mult), reads=[PB, RSB], writes=[IMB])
                            else:
                                C.op("dve", lambda E: E.scalar_tensor_tensor(out=imp[:, ts, :], in0=pt[:, DH + 1:W_], scalar=rsum[:, ts, r, 0:1], in1=imp[:, ts, :], op0=ALU.mult, op1=ALU.add),
                                     reads=[PB, RSB, IMB], writes=[IMB])
                            hcol = g * 4 + r
                            tt("dve", rsum[:, ts, r, 0:1], rsum[:, ts, r, 0:1], gates[:, ti, hcol:hcol + 1], ALU.mult, [RSB, GTB], [RSB])
                            C.op("dve", lambda E: E.tensor_scalar(ocm[:, ts, r, :], pt[:, 0:DH], rsum[:, ts, r, 0:1], None, op0=ALU.mult), reads=[PB, RSB], writes=[OCB])
                    for ts in range(4):
                        ti = tc * 4 + ts
                        C.op("dve", lambda E: E.tensor_scalar(sc[0][:], sidx[:], curt[:, ti:ti + 1], None, op0=ALU.subtract), reads=[CUB], writes=[SCB_[0]])
                        C.op("dve", lambda E: E.tensor_scalar(sc[1][:], sc[0][:], -1.0, None, op0=ALU.is_ge), reads=[SCB_[0]], writes=[SCB_[1]])
                        C.op("dve", lambda E: E.tensor_scalar(sc[2][:], sc[0][:], 0.0, None, op0=ALU.is_le), reads=[SCB_[0]], writes=[SCB_[2]])
                        tt("dve", sc[1][:], sc[1][:], sc[2][:], ALU.mult, [SCB_[1], SCB_[2]], [SCB_[1]])
                        tt("dve", sc[1][:], sc[1][:], s0m[:], ALU.max, [SCB_[1], CUB], [SCB_[1]])
                        C.op("dve", lambda E: E.tensor_scalar(sc[2][:], sc[0][:], 0.0, 2e9, op0=ALU.is_gt, op1=ALU.mult), reads=[SCB_[0]], writes=[SCB_[2]])
                        C.op("dve", lambda E: E.scalar_tensor_tensor(out=sc[3][:], in0=sc[1][:], scalar=1e9, in1=imp[:, ts, :], op0=ALU.mult, op1=ALU.add), reads=[SCB_[1], IMB], writes=[SCB_[3]])
                        tt("dve", sc[3][:], sc[3][:], sc[2][:], ALU.subtract, [SCB_[3], SCB_[2]], [SCB_[3]])
                        C.op("dve", lambda E: E.max(out=m8[:, 0:8], in_=sc[3][:]), reads=[SCB_[3]], writes=[M8B])
                        C.op("dve", lambda E: E.match_replace(out=sc[0][:], in_to_replace=m8[:, 0:8], in_values=sc[3][:], imm_value=-4e9), reads=[M8B, SCB_[3]], writes=[SCB_[0]])
                        C.op("dve", lambda E: E.max(out=m8[:, 8:16], in_=sc[0][:]), reads=[SCB_[0]], writes=[M8B])
                        C.op("dve", lambda E: E.tensor_scalar(selb[:, ts, 0:NSEL], sc[3][:], m8[:, 15:16], NEG, op0=ALU.is_lt, op1=ALU.mult), reads=[SCB_[3], M8B], writes=[SLB])
                        C.op("dve", lambda E: E.tensor_copy(selb[:, ts, NSEL:2 * NSEL], selb[:, ts, 0:NSEL]), reads=[SLB], writes=[SLB])

                def sel_pe(tc):
                    par = tc % 2
                    selb, SLB, selT, STB2 = selbL[par], SLBL[par], selTL[par], STB2L[par]
                    pt, PB = ps.next()
                    ptb = pt[:].bitcast(BF16)
                    for ts in range(4):
                        C.op("pe", lambda E: E.transpose(ptb[:, ts * 128:(ts + 1) * 128], selb[:, ts, :], idb[:]), reads=[SLB, IB], writes=[PB])
                    C.op("act", lambda E: E.copy(selT[:], ptb[:, 0:512]), reads=[PB], writes=[STB2])

                def att(tc):
                    par = tc % 2
                    rsum, RSB, ocm, OCB, selT, STB2 = rsumL[par], RSBL[par], ocmL[par], OCBL[par], selTL[par], STB2L[par]
                    tq = slice(tc * 512, (tc + 1) * 512)
                    its = []
                    for r in range(R):
                        for bi, br in enumerate((1, 2)):
                            jts = list(range(0, 4 * tc + 4)) if br == 1 else list(range(max(0, 4 * tc - 4), 4 * tc + 4))
                            for ji, jt in enumerate(jts):
                                its.append(dict(r=r, bi=bi, br=br, jt=jt, ji=ji, n=len(jts)))

                    def emit_qk(it):
                        r, br, jt = it["r"], it["br"], it["jt"]
                        hp, h2 = r // 2, r % 2
                        prt = slice(h2 * 64, (h2 + 1) * 64)
                        pt, PB = ps.next()
                        it["pt"], it["PB"] = pt, PB
                        d = jt - 4 * tc
                        extra = []
                        if d >= 0:
                            extra.append((idb[:], caus[:, d, :], [IB, CAB]))
                        elif br == 2:
                            extra.append((idb[:], winm[:, d + 4, :], [IB, CAB]))
                        if br == 1:
                            qi = (tc * R + r) % 2
                            qs, QSB = qsL[qi], QSBL[qi]
                            if it["ji"] == 0:
                                oth = slice((1 - h2) * 64, (2 - h2) * 64)
                                C.op("pool", lambda E: E.tensor_copy(qs[prt, :], qTg[prt, hp, tq]), reads=[QGB_], writes=[QSB])
                                C.op("pool", lambda E: E.tensor_copy(qs[oth, :], selT[oth, :]), reads=[STB2], writes=[QSB])
                            C.op("pe", lambda E: E.matmul(pt[:, :], kTe[:, h2, jt * 128:(jt + 1) * 128], qs[:], start=True, stop=(len(extra) == 0)),
                                 reads=[KTEB, QSB], writes=[PB])
                        else:
                            C.op("pe", lambda E: E.matmul(pt[:, :], kTg[prt, 1, jt * 128:(jt + 1) * 128], qTg[prt, hp, tq], start=True, stop=(len(extra) == 0)),
                                 reads=[KGB_, QGB_], writes=[PB])
                        for ei, (l_, r_, rb_) in enumerate(extra):
                            C.op("pe", lambda E: E.matmul(pt[:, :], l_, r_, start=False, stop=(ei == len(extra) - 1)), reads=rb_, writes=[PB])

                    def emit_rest(it, idx):
                        r, bi, br, jt, ji, n = it["r"], it["bi"], it["br"], it["jt"], it["ji"], it["n"]
                        pt, PB = it["pt"], it["PB"]
                        hcol = g * 4 + r
                        pi = idx % 3
                        C.op("act", lambda E: E.activation(out=pT[pi][:], in_=pt[:, :], func=AF.Exp), reads=[PB], writes=[PTB_[pi]])
                        at, AB = ACC[(r * 2 + bi) % 4]
                        for ts in range(4):
                            C.op("pe", lambda E: E.matmul(at[:, ts * 128:ts * 128 + DH + 1], pT[pi][:, ts * 128:(ts + 1) * 128], vaug[:, bi, jt, :], start=(ji == 0 and ts == 0), stop=(ji == n - 1)),
                                 reads=[PTB_[pi], VAB], writes=[AB])
                        if ji == n - 1:
                            at3 = at[:].rearrange("p (t c) -> p t c", c=128)
                            cf = rsum[:, :, r, bi + 1:bi + 2]
                            gcol = br * 16 + hcol
                            C.op("dve", lambda E: E.reciprocal(cf, at3[:, :, DH:DH + 1]), reads=[AB], writes=[RSB])
                            tt("dve", cf, cf, gates[:, tc * 4:(tc + 1) * 4, gcol:gcol + 1], ALU.mult, [RSB, GTB], [RSB])
                            tt("dve", otmp[:], at3[:, :, 0:DH], cf.to_broadcast([128, 4, DH]), ALU.mult, [AB, RSB], [OTB_])
                            tt("pool", ocm[:, :, r, :], ocm[:, :, r, :], otmp[:], ALU.add, [OTB_, OCB], [OCB])


                    LOOK = 2
                    for i in range(min(LOOK, len(its))):
                        emit_qk(its[i])
                    for i in range(len(its)):
                        if i + LOOK < len(its):
                            emit_qk(its[i + LOOK])
                        emit_rest(its[i], i)
                    for ts in range(4):
                        ti = tc * 4 + ts
                        C.op("act", lambda E: E.copy(o_all[:, ti, g * 256:(g + 1) * 256], ocm[:, ts, :, :].rearrange("p r d -> p (r d)")), reads=[OCB], writes=[OAB[ti]])

                cmp_sel(0); sel_pe(0)
                for tc in range(8):
                    if tc + 1 < 8:
                        cmp_sel(tc + 1)
                    att(tc)
                    if tc + 1 < 8:
                        sel_pe(tc + 1)
            ps.reserve(0)
            C.barrier()
        with ExitStack() as es:
            wo = es.enter_context(SB(nc, "nwo", [128, 8, D], BF16)); WOB = C.buf("nwo")
            C.dma("pool", wo[:], T["nsa_w_out"][j].rearrange("(kc p) d -> p kc d", p=128), WOB, writes=[WOB])
            xs = [es.enter_context(SB(nc, f"xs{i}", [128, KT, D], F32)) for i in range(2)]; XS = C.bufs("xs", 2)
            oT = [es.enter_context(SB(nc, f"oT{i}", [128, 8, 128], BF16)) for i in range(2)]; OTB = C.bufs("oT", 2)
            OH = C.buf("oall_d")
            for c in range(NCH):
                sl = c % 2
                C.dma("sp", xs[sl][:], xin_t[c], XS[sl], reads=[xin_bufs[c * KT + k] for k in range(KT)], writes=[XS[sl]])
                for k in range(KT):
                    ti = c * KT + k; i = ti % 2
                    pt, PB = ps.next()
                    ptb = pt[:].bitcast(BF16).rearrange("p (a b) -> p a b", a=8)
                    for a in range(8):
                        C.op("pe", lambda E: E.transpose(ptb[:, a, :], o_all[:, ti, a * 128:(a + 1) * 128], idb[:]), reads=[OH, IB], writes=[PB])
                    C.op("act", lambda E: E.copy(oT[i][:], ptb), reads=[PB], writes=[OTB[i]])
                    for dh in range(2):
                        pt, PB = ps.next()
                        for kc in range(8):
                            C.op("pe", lambda E: E.matmul(pt[:, :], oT[i][:, kc, :], wo[:, kc, dh * 512:(dh + 1) * 512], start=(kc == 0), stop=(kc == 7)), reads=[OTB[i], WOB], writes=[PB])
                        tt("dve", xs[sl][:, k, dh * 512:(dh + 1) * 512], xs[sl][:, k, dh * 512:(dh + 1) * 512], pt[:, :], ALU.add, [PB, XS[sl]], [XS[sl]])
                C.dma("sp", xout_t[c], xs[sl][:], XS[sl], reads=[XS[sl]], writes=[xout_bufs[c * KT + k] for k in range(KT)])
            C.barrier()
MIXERS = {}
MIXERS['conv'] = conv_phase
MIXERS['pool'] = pool_phase
MIXERS['s5'] = s5_phase
MIXERS['nsa'] = nsa_phase
PARAM_SHAPES = None

def build_program(shapes, mixers=("conv", "nsa", "s5", "pool")):
    nc = bass.Bass("TRN2", target_bir_lowering=False)
    T = {}
    for name, shp in shapes.items():
        dt = I32 if name == "positions" else F32
        T[name] = nc.dram_tensor(name, list(shp), dt, kind="ExternalInput").ap()
    y = nc.dram_tensor("y", [S, D], F32, kind="ExternalOutput").ap()
    C = Ctx(nc)
    ybufs = C.bufs("yd", NT)
    nobufs = C.bufs("xin", NT)
    cur, curb = T["x"], nobufs
    for i in range(4):
        m = mixers[i]
        fn = MIXERS.get(m)
        if fn is not None:
            fn(C, nc, T, i, cur, curb, y, ybufs)
            cur, curb = y, ybufs
        ffn_phase(C, nc, cur, curb, y, ybufs, T["norm_ffn"][i], T["ffn_w_in"][i], T["ffn_w_out"][i])
        cur, curb = y, ybufs
    C.finish()
    return nc

_NC_CACHE = {}

def kernel(**inputs):
    inputs = {k: np.ascontiguousarray(np.asarray(v)) for k, v in inputs.items()}
    x = inputs["x"]
    B = x.shape[0]
    shapes = {k: (v.shape[1:] if k == "x" else v.shape) for k, v in inputs.items()}
    key = tuple(sorted((k, tuple(s)) for k, s in shapes.items()))
    if key not in _NC_CACHE:
        _NC_CACHE[key] = build_program(shapes)
    nc = _NC_CACHE[key]
    in_maps = []
    for b in range(B):
        m = {k: v for k, v in inputs.items() if k != "x"}
        m["x"] = np.ascontiguousarray(x[b])
        in_maps.append(m)
    res = run_bass_kernel_spmd(nc, in_maps, core_ids=list(range(B)))
    return np.stack([np.asarray(r["y"]) for r in res.results], axis=0).astype(np.float32)
```
